# Optimizing a Trainium2 kernel written in Bass

```python
import math
import jax, jax.numpy as jnp
from jax import lax
import numpy as np

D_MODEL = 1024
BATCH = 8
SEQ = 4096
DEPTH = 2

N_MEM = 256
EPS = 1e-6
N_EVEN = (DEPTH + 1) // 2
N_ODD = DEPTH // 2
DN_HEADS = 4
DN_DK = D_MODEL // 8
DN_DV = D_MODEL // 8
DN_W = DN_HEADS * DN_DK
DN_CONV = 4
DN_CHUNK = 64
SB_HEADS = 4
SB_DH = D_MODEL // 8
SB_W = SB_HEADS * SB_DH
SB_BLOCK = 128
AB_IN = 3 * DN_W + DN_HEADS * DN_DV + 2 * DN_HEADS + 3 * SB_W
GLA_HEADS = 4
GLA_DK = D_MODEL // 8
GLA_DV = D_MODEL // 4
GLA_RANK = 16
GLA_TAU = 16.0
GLA_CHUNK = 64
C_IN = 2 * GLA_HEADS * GLA_DK + 2 * GLA_HEADS * GLA_DV + GLA_RANK
XA_HEADS = 4
XA_DH = D_MODEL // XA_HEADS
D_FF = ((8 * D_MODEL // 3 + 255) // 256) * 256
FFN_CONV = 3

kernel_name = "hybrid_deltanet_stickbreak_gla_convffn"


def rmsnorm(x, g):
    xf = x.astype(jnp.float32)
    y = xf * lax.rsqrt(jnp.mean(xf * xf, axis=-1, keepdims=True) + EPS)
    return (y * g.astype(jnp.float32)).astype(x.dtype)


def l2norm(x):
    return x * lax.rsqrt(jnp.sum(x * x, axis=-1, keepdims=True) + EPS)


def _split(p, sizes):
    idx, acc = [], 0
    for s in sizes[:-1]:
        acc += s
        idx.append(acc)
    return jnp.split(p, idx, axis=-1)


def causal_dwconv(x, w):
    width, ch = w.shape
    return lax.conv_general_dilated(
        x, w.astype(x.dtype)[:, None, :], window_strides=(1,),
        padding=[(width - 1, 0)], dimension_numbers=('NWC', 'WIO', 'NWC'),
        feature_group_count=ch)


def _chunk(t, c):
    b, s, h, d = t.shape
    return t.reshape(b, s // c, c, h, d).transpose(0, 3, 1, 2, 4)


def _chunk_h(t, c):
    b, s, h = t.shape
    return t.reshape(b, s // c, c, h).transpose(0, 3, 1, 2)


def gated_deltanet(q, k, v, a, b, a_log, dt_bias):
    f32 = jnp.float32
    bsz, s, h, dk = q.shape
    dv = v.shape[-1]
    c = DN_CHUNK
    q = l2norm(q.astype(f32)) * dk ** -0.5
    k = l2norm(k.astype(f32))
    v = v.astype(f32)
    beta = jax.nn.sigmoid(b.astype(f32))
    g = -jnp.exp(a_log.astype(f32)) * jax.nn.softplus(a.astype(f32) + dt_bias.astype(f32))
    q, k, v = _chunk(q, c), _chunk(k, c), _chunk(v, c)
    beta, g = _chunk_h(beta, c), _chunk_h(g, c)
    gc = jnp.cumsum(g, axis=-1)
    causal = jnp.tril(jnp.ones((c, c), bool))
    strict = jnp.tril(jnp.ones((c, c), bool), -1)
    diff = gc[..., :, None] - gc[..., None, :]
    decay = jnp.where(causal, jnp.exp(jnp.where(causal, diff, 0.0)), 0.0)
    kb = k * beta[..., None]
    a_mat = jnp.where(strict, jnp.einsum('bhnid,bhnjd->bhnij', kb, k) * decay, 0.0)
    m_mat = a_mat + jnp.eye(c, dtype=f32)
    u = lax.linalg.triangular_solve(m_mat, v * beta[..., None], left_side=True,
                                    lower=True, unit_diagonal=True)
    w = lax.linalg.triangular_solve(m_mat, kb * jnp.exp(gc)[..., None], left_side=True,
                                    lower=True, unit_diagonal=True)
    qk = jnp.einsum('bhnid,bhnjd->bhnij', q, k) * decay
    qg = q * jnp.exp(gc)[..., None]
    glast = gc[..., -1]
    kdec = k * jnp.exp(glast[..., None] - gc)[..., None]

    def step(state, inp):
        u_n, w_n, qk_n, qg_n, kdec_n, gl_n = inp
        delta = u_n - jnp.einsum('bhck,bhkv->bhcv', w_n, state)
        o = (jnp.einsum('bhck,bhkv->bhcv', qg_n, state)
             + jnp.einsum('bhij,bhjv->bhiv', qk_n, delta))
        state = (jnp.exp(gl_n)[..., None, None] * state
                 + jnp.einsum('bhck,bhcv->bhkv', kdec_n, delta))
        return state, o

    s0 = jnp.zeros((bsz, h, dk, dv), f32)
    xs = tuple(jnp.moveaxis(t, 2, 0) for t in (u, w, qk, qg, kdec, glast))
    _, o = lax.scan(step, s0, xs)
    return o.transpose(1, 0, 3, 2, 4).reshape(bsz, s, h, dv)


def stick_breaking(q, k, v):
    s = q.shape[2]
    scale = q.shape[-1] ** -0.5
    outs = []
    for blk in range(s // SB_BLOCK):
        q0 = blk * SB_BLOCK
        end = q0 + SB_BLOCK
        z = jnp.einsum('bhqd,bhkd->bhqk', q[:, :, q0:end], k[:, :, :end]) * scale
        tpos = q0 + jnp.arange(SB_BLOCK)
        spos = jnp.arange(end)
        mask = spos[None, :] < tpos[:, None]
        l = jnp.where(mask, jax.nn.log_sigmoid(-z), 0.0)
        log_a = z + lax.cumsum(l, axis=3, reverse=True)
        att = jnp.exp(jnp.where(mask, log_a, -jnp.inf))
        outs.append(jnp.einsum('bhqk,bhkd->bhqd', att, v[:, :, :end]))
    return jnp.concatenate(outs, axis=2)


def mixer_deltanet_stickbreak(h, w_in, conv_w, a_log, dt_bias, dn_norm_g, sb_g_q, sb_g_k, w_out):
    f32 = jnp.float32
    bsz, s, _ = h.shape
    p = h @ w_in
    qkv_a, z_a, a_a, b_a, q_b, k_b, v_b = _split(
        p, [3 * DN_W, DN_HEADS * DN_DV, DN_HEADS, DN_HEADS, SB_W, SB_W, SB_W])
    qkv_a = jax.nn.silu(causal_dwconv(qkv_a, conv_w))
    q_a, k_a, v_a = _split(qkv_a, [DN_W, DN_W, DN_HEADS * DN_DV])
    o_a = gated_deltanet(q_a.reshape(bsz, s, DN_HEADS, DN_DK),
                         k_a.reshape(bsz, s, DN_HEADS, DN_DK),
                         v_a.reshape(bsz, s, DN_HEADS, DN_DV), a_a, b_a, a_log, dt_bias)
    o_a = rmsnorm(o_a, dn_norm_g) * jax.nn.silu(z_a.reshape(bsz, s, DN_HEADS, DN_DV).astype(f32))
    q_b = rmsnorm(q_b.reshape(bsz, s, SB_HEADS, SB_DH), sb_g_q).astype(f32).transpose(0, 2, 1, 3)
    k_b = rmsnorm(k_b.reshape(bsz, s, SB_HEADS, SB_DH), sb_g_k).astype(f32).transpose(0, 2, 1, 3)
    v_b = v_b.reshape(bsz, s, SB_HEADS, SB_DH).astype(f32).transpose(0, 2, 1, 3)
    o_b = stick_breaking(q_b, k_b, v_b).transpose(0, 2, 1, 3)
    o = jnp.concatenate([o_a.reshape(bsz, s, -1), o_b.reshape(bsz, s, -1)], axis=-1)
    return o.astype(h.dtype) @ w_out


def gla_attention(q, k, v, gk):
    f32 = jnp.float32
    bsz, s, h, dk = q.shape
    dv = v.shape[-1]
    c = GLA_CHUNK
    q = _chunk(q.astype(f32) * dk ** -0.5, c)
    k = _chunk(k.astype(f32), c)
    v = _chunk(v.astype(f32), c)
    bc = jnp.cumsum(_chunk(gk.astype(f32), c), axis=3)
    bmid = bc[..., c // 2:c // 2 + 1, :]
    causal = jnp.tril(jnp.ones((c, c), bool))
    att = jnp.einsum('bhnid,bhnjd->bhnij', q * jnp.exp(bc - bmid), k * jnp.exp(bmid - bc))
    o_intra = jnp.einsum('bhnij,bhnjv->bhniv', jnp.where(causal, att, 0.0), v)
    qg = q * jnp.exp(bc)
    blast = bc[..., -1, :]
    kdec = k * jnp.exp(blast[..., None, :] - bc)

    def step(state, inp):
        qg_n, kdec_n, v_n, bl_n = inp
        o = jnp.einsum('bhck,bhkv->bhcv', qg_n, state)
        state = jnp.exp(bl_n)[..., None] * state + jnp.einsum('bhck,bhcv->bhkv', kdec_n, v_n)
        return state, o

    s0 = jnp.zeros((bsz, h, dk, dv), f32)
    xs = tuple(jnp.moveaxis(t, 2, 0) for t in (qg, kdec, v, blast))
    _, o_inter = lax.scan(step, s0, xs)
    o = o_intra + jnp.moveaxis(o_inter, 0, 2)
    return o.transpose(0, 2, 3, 1, 4).reshape(bsz, s, h, dv)


def mixer_gla(h, w_in, w_gk, b_gk, norm_g, w_out):
    f32 = jnp.float32
    bsz, s, _ = h.shape
    p = h @ w_in
    q, k, v, r, lr = _split(p, [GLA_HEADS * GLA_DK, GLA_HEADS * GLA_DK,
                                GLA_HEADS * GLA_DV, GLA_HEADS * GLA_DV, GLA_RANK])
    gk = jax.nn.log_sigmoid((lr @ w_gk + b_gk).astype(f32)) / GLA_TAU
    o = gla_attention(q.reshape(bsz, s, GLA_HEADS, GLA_DK), k.reshape(bsz, s, GLA_HEADS, GLA_DK),
                      v.reshape(bsz, s, GLA_HEADS, GLA_DV), gk.reshape(bsz, s, GLA_HEADS, GLA_DK))
    o = rmsnorm(o, norm_g) * jax.nn.silu(r.reshape(bsz, s, GLA_HEADS, GLA_DV).astype(f32))
    return o.reshape(bsz, s, -1).astype(h.dtype) @ w_out


def mem_cross_attention(h, m, w_q, w_kv, w_o, g_q, g_k):
    f32 = jnp.float32
    bsz, s, d = h.shape
    q = rmsnorm((h @ w_q).reshape(bsz, s, XA_HEADS, XA_DH), g_q)
    kv = (m @ w_kv).reshape(bsz, m.shape[1], 2, XA_HEADS, XA_DH)
    k = rmsnorm(kv[:, :, 0], g_k)
    v = kv[:, :, 1]
    sc = jnp.einsum('bshd,bmhd->bhsm', q.astype(f32), k.astype(f32)) * XA_DH ** -0.5
    pr = jax.nn.softmax(sc, axis=-1)
    o = jnp.einsum('bhsm,bmhd->bshd', pr, v.astype(f32)).reshape(bsz, s, d)
    return o.astype(h.dtype) @ w_o


def conv_ffn(h, w_up, conv_w, conv_b, w_down):
    u = causal_dwconv(h @ w_up, conv_w) + conv_b.astype(h.dtype)
    g, val = jnp.split(u, 2, axis=-1)
    return (jax.nn.silu(g) * val) @ w_down


def setup_inputs(seed: int = 0) -> dict:
    key = jax.random.key(seed)
    f32 = jnp.float32
    counter = [0]

    def nk():
        counter[0] += 1
        return jax.random.fold_in(key, counter[0])

    def nrm(shape, scale):
        return jax.random.normal(nk(), shape, f32) * scale

    def gain(shape):
        return 1.0 + 0.01 * jax.random.normal(nk(), shape, f32)

    d = D_MODEL
    dt = jnp.exp(jax.random.uniform(nk(), (N_EVEN, DN_HEADS), f32, math.log(1e-3), math.log(1e-1)))
    return {
        "x": nrm((BATCH, SEQ, d), 1.0),
        "mem": nrm((BATCH, N_MEM, d), 1.0),
        "norm_mix": gain((DEPTH, d)),
        "norm_xa": gain((DEPTH, d)),
        "norm_mem": gain((DEPTH, d)),
        "norm_ffn": gain((DEPTH, d)),
        "xa_w_q": nrm((DEPTH, d, d), d ** -0.5),
        "xa_w_kv": nrm((DEPTH, d, 2 * d), d ** -0.5),
        "xa_w_o": nrm((DEPTH, d, d), d ** -0.5),
        "xa_g_q": gain((DEPTH, XA_DH)),
        "xa_g_k": gain((DEPTH, XA_DH)),
        "ffn_w_up": nrm((DEPTH, d, 2 * D_FF), d ** -0.5),
        "ffn_conv_w": nrm((DEPTH, FFN_CONV, 2 * D_FF), FFN_CONV ** -0.5),
        "ffn_conv_b": nrm((DEPTH, 2 * D_FF), 0.01),
        "ffn_w_down": nrm((DEPTH, D_FF, d), D_FF ** -0.5),
        "ab_w_in": nrm((N_EVEN, d, AB_IN), d ** -0.5),
        "ab_conv_w": nrm((N_EVEN, DN_CONV, 3 * DN_W), DN_CONV ** -0.5),
        "dn_a_log": jnp.log(jax.random.uniform(nk(), (N_EVEN, DN_HEADS), f32, 1.0, 16.0)),
        "dn_dt_bias": dt + jnp.log(-jnp.expm1(-dt)),
        "dn_norm_g": gain((N_EVEN, DN_DV)),
        "sb_g_q": gain((N_EVEN, SB_DH)),
        "sb_g_k": gain((N_EVEN, SB_DH)),
        "ab_w_out": nrm((N_EVEN, DN_HEADS * DN_DV + SB_W, d), d ** -0.5),
        "gla_w_in": nrm((N_ODD, d, C_IN), d ** -0.5),
        "gla_w_gk": nrm((N_ODD, GLA_RANK, GLA_HEADS * GLA_DK), GLA_RANK ** -0.5),
        "gla_b_gk": nrm((N_ODD, GLA_HEADS * GLA_DK), 0.01),
        "gla_norm_g": gain((N_ODD, GLA_DV)),
        "gla_w_out": nrm((N_ODD, GLA_HEADS * GLA_DV, d), d ** -0.5),
    }


def reference(x, mem, norm_mix, norm_xa, norm_mem, norm_ffn,
              xa_w_q, xa_w_kv, xa_w_o, xa_g_q, xa_g_k,
              ffn_w_up, ffn_conv_w, ffn_conv_b, ffn_w_down,
              ab_w_in, ab_conv_w, dn_a_log, dn_dt_bias, dn_norm_g, sb_g_q, sb_g_k, ab_w_out,
              gla_w_in, gla_w_gk, gla_b_gk, gla_norm_g, gla_w_out):
    for layer in range(DEPTH):
        i = layer // 2
        h = rmsnorm(x, norm_mix[layer])
        if layer % 2 == 0:
            x = x + mixer_deltanet_stickbreak(h, ab_w_in[i], ab_conv_w[i], dn_a_log[i], dn_dt_bias[i],
                                              dn_norm_g[i], sb_g_q[i], sb_g_k[i], ab_w_out[i])
        else:
            x = x + mixer_gla(h, gla_w_in[i], gla_w_gk[i], gla_b_gk[i], gla_norm_g[i], gla_w_out[i])
        x = x + mem_cross_attention(rmsnorm(x, norm_xa[layer]), rmsnorm(mem, norm_mem[layer]),
                                    xa_w_q[layer], xa_w_kv[layer], xa_w_o[layer],
                                    xa_g_q[layer], xa_g_k[layer])
        x = x + conv_ffn(rmsnorm(x, norm_ffn[layer]), ffn_w_up[layer], ffn_conv_w[layer],
                         ffn_conv_b[layer], ffn_w_down[layer])
    return x
```

```python
import numpy as np
import concourse.bass as bass
import concourse.mybir as mybir

F32 = mybir.dt.float32
BF16 = mybir.dt.bfloat16
AF = mybir.ActivationFunctionType
ALU = mybir.AluOpType
AX = mybir.AxisListType

_DT_SIZE = {F32: 4, BF16: 2}


def _dtsize(dt):
    if dt in _DT_SIZE:
        return _DT_SIZE[dt]
    return mybir.dt.size(dt)


def region(ap):
    t = ap.tensor
    name = t.name
    es = _dtsize(ap.dtype)
    pairs = ap.ap
    off = int(ap.offset)
    sp = type(t).__name__
    if sp.startswith('DRam'):
        lo = off
        hi = off
        for st, cnt in pairs:
            if st >= 0:
                hi += st * (cnt - 1)
            else:
                lo += st * (cnt - 1)
        return (name, 0, 0, lo * es, (hi + 1) * es)
    if sp.startswith('PSum'):
        return (name, 0, 127, 0, 2048)
    pstep, pcnt = pairs[0]
    if pstep == 0:
        p0 = 0
        f0 = off
        pn = 1
        pstep = 1 << 40
    else:
        p0 = off // pstep
        f0 = off % pstep
        pn = pcnt
    lo = f0
    hi = f0
    for st, cnt in pairs[1:]:
        if st >= 0:
            hi += st * (cnt - 1)
        else:
            lo += st * (cnt - 1)
    return (name, p0, p0 + pn - 1, lo * es, (hi + 1) * es)


def _ovl(a, b):
    return not (a[2] < b[1] or b[2] < a[1] or a[4] <= b[3] or b[4] <= a[3])


def _covers(a, b):
    return a[1] <= b[1] and a[2] >= b[2] and a[3] <= b[3] and a[4] >= b[4]


class Sched:
    NSLOT = 8

    def __init__(self, nc):
        self.nc = nc
        self.engs = {'pe': nc.tensor, 'act': nc.scalar, 'dve': nc.vector,
                     'pool': nc.gpsimd, 'sp': nc.sync}
        self.ops = []
        self.hist = {}
        self.readonly = set()
        self.rank_mode = False
        self.dram_names = set()

    def sbuf(self, name, shape, dt):
        g = self.nc.sbuf_tensor(name, list(shape), dt)
        return g.__enter__()

    def psum(self, name, shape, dt):
        g = self.nc.psum_tensor(name, list(shape), dt)
        return g.__enter__()

    def _pages(self, reg):
        sh = 19 if reg[0] in self.dram_names else 12
        return range(reg[3] >> sh, ((reg[4] - 1) >> sh) + 1)

    def op(self, eng, fn, reads, writes, dma=False):
        idx = len(self.ops)
        deps = {}
        rr = [region(a) for a in reads if a.tensor.name not in self.readonly]
        ww = [region(a) for a in writes]
        for a in list(reads) + list(writes):
            if type(a.tensor).__name__.startswith('DRam'):
                self.dram_names.add(a.tensor.name)
        for r in rr:
            h = self.hist.get(r[0])
            if h is None:
                continue
            for pg in self._pages(r):
                e = h.get(pg)
                if e is None:
                    continue
                for (reg, oi) in e['w']:
                    if _ovl(reg, r):
                        deps[oi] = 'raw'
                if r[0].startswith('ps'):
                    for (reg, oi) in e['r']:
                        if oi not in deps and self.ops[oi]['eng'] != eng:
                            deps[oi] = 'rr'
        for w in ww:
            h = self.hist.setdefault(w[0], {})
            for pg in self._pages(w):
                e = h.get(pg)
                if e is None:
                    continue
                for (reg, oi) in e['w']:
                    if _ovl(reg, w) and oi not in deps:
                        deps[oi] = 'waw'
                for (reg, oi) in e['r']:
                    if _ovl(reg, w) and oi not in deps:
                        deps[oi] = 'war'
        for w in ww:
            h = self.hist[w[0]]
            for pg in self._pages(w):
                e = h.setdefault(pg, {'w': [], 'r': []})
                e['w'] = [(reg, oi) for (reg, oi) in e['w'] if not _covers(w, reg)]
                e['r'] = [(reg, oi) for (reg, oi) in e['r'] if not _covers(w, reg)]
                e['w'].append((w, idx))
        for r in rr:
            h = self.hist.setdefault(r[0], {})
            for pg in self._pages(r):
                e = h.setdefault(pg, {'w': [], 'r': []})
                if not dma:
                    keep = []
                    for (reg, oi) in e['r']:
                        if oi == idx or not (self.ops[oi]['eng'] == eng and not self.ops[oi]['dma']
                                             and _covers(r, reg)):
                            keep.append((reg, oi))
                        elif oi not in deps:
                            deps[oi] = 'ord'
                    e['r'] = keep
                e['r'].append((r, idx))
        fdeps = set()
        best = {}
        for oi, kind in deps.items():
            o = self.ops[oi]
            if (not o['dma']) and (not dma) and o['eng'] == eng and kind != 'raw':
                continue
            if o['dma']:
                fdeps.add(oi)
            else:
                if best.get(o['eng'], -1) < oi:
                    best[o['eng']] = oi
        for oi in best.values():
            fdeps.add(oi)
        nbytes = 0
        for w in ww:
            nbytes = max(nbytes, w[4] - w[3])
        for r in rr:
            nbytes = max(nbytes, r[4] - r[3])
        nel = 1
        for a in list(reads) + list(writes):
            k = 1
            for (_st, cnt) in a.ap[1:]:
                k *= cnt
            nel = max(nel, k)
        if eng == 'pe' and len(reads) >= 2:
            k = 1
            for (_st, cnt) in reads[1].ap[1:]:
                k *= cnt
            nel = k * (4 if _dtsize(reads[1].dtype) == 4 else 1)
        self.ops.append({'eng': eng, 'fn': fn, 'deps': fdeps, 'dma': dma, 'alldeps': dict(deps), 'nb': nbytes,
                         'nel': nel, 'cost': None, 'rk': self.rank_mode})
        return idx

    def dma(self, out, in_, q='sp', **kw):
        e = self.engs[q]
        return self.op(q, lambda: e.dma_start(out=out, in_=in_, **kw), [in_], [out], dma=True)

    def mm(self, out, lhsT, rhs, start=True, stop=True, **kw):
        t = self.nc.tensor
        return self.op('pe', lambda: t.matmul(out, lhsT, rhs, start=start, stop=stop, **kw),
                       [lhsT, rhs], [out])

    def transpose(self, out, in_, ident):
        t = self.nc.tensor
        return self.op('pe', lambda: t.transpose(out, in_, ident), [in_, ident], [out])

    def act(self, out, in_, func, bias=None, scale=None, accum_out=None, eng='act'):
        e = self.engs[eng]
        kw = {}
        reads = [in_]
        writes = [out]
        if bias is not None:
            kw['bias'] = bias
            if not isinstance(bias, (int, float)):
                reads.append(bias)
        if scale is not None:
            kw['scale'] = scale
            if not isinstance(scale, (int, float)):
                reads.append(scale)
        if accum_out is not None:
            kw['accum_out'] = accum_out
            writes.append(accum_out)
        return self.op(eng, lambda: e.activation(out, in_, func, **kw), reads, writes)

    def tt(self, out, in0, in1, op, eng='dve'):
        e = self.engs[eng]
        return self.op(eng, lambda: e.tensor_tensor(out, in0, in1, op), [in0, in1], [out])

    def ts(self, out, in0, s1, s2, op0, op1=None, accum_out=None, eng='dve'):
        e = self.engs[eng]
        reads = [in0]
        if not isinstance(s1, (int, float)):
            reads.append(s1)
        if s2 is not None and not isinstance(s2, (int, float)):
            reads.append(s2)
        writes = [out]
        kw = {}
        if op1 is not None:
            kw['op1'] = op1
        if accum_out is not None:
            kw['accum_out'] = accum_out
            writes.append(accum_out)
        return self.op(eng, lambda: e.tensor_scalar(out, in0, s1, s2, op0, **kw), reads, writes)

    def stt(self, out, in0, scalar, in1, op0, op1, eng='dve'):
        e = self.engs[eng]
        reads = [in0, in1]
        if not isinstance(scalar, (int, float)):
            reads.append(scalar)
        return self.op(eng, lambda: e.scalar_tensor_tensor(out, in0, scalar, in1, op0, op1),
                       reads, [out])

    def copy(self, out, in_, eng='dve'):
        e = self.engs[eng]
        if eng == 'act':
            return self.op(eng, lambda: e.copy(out, in_), [in_], [out])
        return self.op(eng, lambda: e.tensor_copy(out, in_), [in_], [out])

    def memset(self, ap, val, eng='dve'):
        e = self.engs[eng]
        return self.op(eng, lambda: e.memset(ap, val), [], [ap])

    def reduce(self, out, in_, op, axis=None, eng='dve'):
        e = self.engs[eng]
        ax = axis if axis is not None else AX.X
        return self.op(eng, lambda: e.tensor_reduce(out, in_, ax, op), [in_], [out])

    def scan(self, out, d0, d1, initial, op0, op1, eng='dve'):
        e = self.engs[eng]
        reads = [d0, d1]
        if not isinstance(initial, (int, float)):
            reads.append(initial)
        return self.op(eng, lambda: e.tensor_tensor_scan(out, d0, d1, initial, op0, op1),
                       reads, [out])

    def recip(self, out, in_):
        e = self.nc.vector
        return self.op('dve', lambda: e.reciprocal(out, in_), [in_], [out])

    def _cost(self, o):
        if o['cost'] is not None:
            return o['cost']
        nb = o['nb']
        nel = o['nel']
        e = o['eng']
        if o['dma']:
            return 0.06
        if e == 'pe':
            return 0.035 + nel / 1950.0
        if e == 'act':
            return 0.2 + nel / 1200.0
        if e == 'dve':
            return 0.07 + nel / 960.0
        if e == 'pool':
            return 0.15 + nel / 600.0
        return 0.05

    def schedule(self, window=None):
        import os
        if window is None:
            window = int(os.environ.get("SCHED_WINDOW", "64"))
        ops = self.ops
        n = len(ops)
        succ = [[] for _ in range(n)]
        nrem = [0] * n
        for i, o in enumerate(ops):
            ds = o['alldeps']
            nrem[i] = len(ds)
            for d in ds:
                succ[d].append(i)
        ready_t = [0.0] * n
        finish = [0.0] * n
        rank = [0.0] * n
        for i in range(n - 1, -1, -1):
            r = 0.0
            for sidx in succ[i]:
                if rank[sidx] > r:
                    r = rank[sidx]
            rank[i] = r + self._cost(ops[i]) + 0.1
        use_rank = os.environ.get("SCHED_RANK", "1") == "1"
        pend = {}
        for i, o in enumerate(ops):
            pend.setdefault(o['eng'], []).append(i)
        pos = {e: 0 for e in pend}
        done = [False] * n
        eng_t = {e: 0.0 for e in pend}
        order = []
        LAT = 0.12
        left = n
        while left:
            best = None
            for e, lst in pend.items():
                p = pos[e]
                while p < len(lst) and done[lst[p]]:
                    p += 1
                pos[e] = p
                cnt = 0
                q = p
                while q < len(lst) and cnt < window:
                    i = lst[q]
                    q += 1
                    if done[i]:
                        continue
                    cnt += 1
                    if nrem[i]:
                        continue
                    st = ready_t[i] if ready_t[i] > eng_t[e] else eng_t[e]
                    key = (round(st, 1), -rank[i], i) if (use_rank and ops[i]['rk']) else (st, 0.0, i)
                    if best is None or key < best[0]:
                        best = (key, e, i)
            assert best is not None, "scheduler deadlock"
            e, i = best[1], best[2]
            st = ready_t[i] if ready_t[i] > eng_t[e] else eng_t[e]
            o = ops[i]
            c = self._cost(o)
            if o['dma']:
                eng_t[e] = st + c
                fin = st + 2.0 + o['nb'] / 200000.0
            else:
                eng_t[e] = st + c
                fin = st + c
            finish[i] = fin
            done[i] = True
            left -= 1
            order.append(i)
            for sidx in succ[i]:
                nrem[sidx] -= 1
                t = fin + LAT
                if t > ready_t[sidx]:
                    ready_t[sidx] = t
        return order

    def finalize(self, final_wait_ops=(), reorder=True):
        nc = self.nc
        ops = self.ops
        order = self.schedule() if reorder else list(range(len(ops)))
        wait_deps = []
        needed = set()
        for idx, o in enumerate(ops):
            wd = []
            for d, kind in o['alldeps'].items():
                po = ops[d]
                if (not po['dma']) and (not o['dma']) and po['eng'] == o['eng'] and kind != 'raw':
                    continue
                wd.append(d)
                needed.add(d)
            wait_deps.append(wd)
        for d in final_wait_ops:
            needed.add(d)
        eng_sem = {}
        eng_cnt = {}
        dma_sems = {}
        dma_cnt = {}
        ticket = {}
        waited = {k: {} for k in self.engs}

        def esem(eng):
            if eng not in eng_sem:
                eng_sem[eng] = nc.alloc_semaphore(name='c_' + eng)
                eng_cnt[eng] = 0
            return eng_sem[eng]

        def do_wait(eng, sem, val):
            w = waited[eng]
            if w.get(sem.num, 0) >= val:
                return
            self.engs[eng].wait_ge(sem, val)
            w[sem.num] = val

        for idx in order:
            o = ops[idx]
            eng = o['eng']
            wmax = {}
            for d in wait_deps[idx]:
                sem, val = ticket[d]
                if wmax.get(sem.num, (None, 0))[1] < val:
                    wmax[sem.num] = (sem, val)
            for k in sorted(wmax):
                do_wait(eng, wmax[k][0], wmax[k][1])
            if o['dma']:
                if eng not in dma_sems:
                    dma_sems[eng] = [nc.alloc_semaphore(name='d_%s_%d' % (eng, i))
                                     for i in range(self.NSLOT)]
                    dma_cnt[eng] = 0
                c = dma_cnt[eng]
                dma_cnt[eng] = c + 1
                sem = dma_sems[eng][c % self.NSLOT]
                rnd = c // self.NSLOT
                if rnd > 0:
                    do_wait(eng, sem, 16 * rnd)
                inst = o['fn']()
                inst.then_inc(sem, 16)
                ticket[idx] = (sem, 16 * (rnd + 1))
            else:
                inst = o['fn']()
                if idx in needed:
                    sem = esem(eng)
                    eng_cnt[eng] += 1
                    inst.then_inc(sem, 1)
                    ticket[idx] = (sem, eng_cnt[eng])
        for d in final_wait_ops:
            sem, val = ticket[d]
            do_wait('sp', sem, val)
        self.stats = {'n_ops': len(ops), 'eng_cnt': dict(eng_cnt), 'dma_cnt': dict(dma_cnt)}
        return self.stats


class Arena:
    def __init__(self, S, nbytes, name='arena'):
        self.t = S.sbuf(name, [128, nbytes // 4], F32)
        self.n = nbytes
        self.top = 0

    def mark(self):
        return self.top

    def release(self, m):
        self.top = m

    def alloc(self, free_shape, dt):
        es = _dtsize(dt)
        n = 1
        for d in free_shape:
            n *= d
        nb = (n * es + 31) // 32 * 32
        off = self.top
        assert off + nb <= self.n, "arena overflow %d + %d > %d" % (off, nb, self.n)
        self.top = off + nb
        v = self.t[:, off // 4:(off + nb) // 4]
        if dt != F32:
            v = v.bitcast(dt)
        v = v[:, 0:n]
        if len(free_shape) == 2:
            v = v.rearrange("p (a b) -> p a b", a=free_shape[0])
        elif len(free_shape) == 3:
            v = v.rearrange("p (a b c) -> p a b c", a=free_shape[0], b=free_shape[1])
        return v


T = 4096
D = 1024
NT = T // 128
DFF = 2816
EPS = 1e-6


class Ctx:
    pass


def setup(nc, phases_debug=None):
    C = Ctx()
    S = Sched(nc)
    C.S = S
    C.nc = nc
    A = Arena(S, 200 * 1024)
    C.A = A
    C.ps = [S.psum("ps%d" % i, [128, 512], F32) for i in range(8)]
    C.psi = 0
    C.norm_i = 0
    C.psi6 = 0
    C.ident = A.alloc([128], F32)
    C.identb = A.alloc([128], BF16)
    C.ones = A.alloc([128], F32)
    C.epsb = A.alloc([1], F32)
    C.tril = A.alloc([128], F32)
    C.triu = A.alloc([128], F32)
    C.triu_s = A.alloc([128], F32)
    g = nc.gpsimd
    S.memset(C.ident, 0.0)
    S.op('pool', lambda: g.affine_select(out=C.ident, in_=C.ident, compare_op=ALU.not_equal, fill=1.0,
                                         base=0, pattern=[[-1, 128]], channel_multiplier=1),
         [C.ident], [C.ident])
    S.copy(C.identb, C.ident)
    S.memset(C.ones, 1.0)
    C.onesb = A.alloc([128], BF16)
    S.copy(C.onesb, C.ones)
    S.memset(C.epsb, EPS)
    S.memset(C.tril, 1.0)
    S.op('pool', lambda: g.affine_select(out=C.tril, in_=C.tril, compare_op=ALU.is_ge, fill=0.0,
                                         base=0, pattern=[[-1, 128]], channel_multiplier=1),
         [C.tril], [C.tril])
    S.memset(C.triu, 1.0)
    S.op('pool', lambda: g.affine_select(out=C.triu, in_=C.triu, compare_op=ALU.is_ge, fill=0.0,
                                         base=0, pattern=[[1, 128]], channel_multiplier=-1),
         [C.triu], [C.triu])
    S.memset(C.triu_s, 1.0)
    S.op('pool', lambda: g.affine_select(out=C.triu_s, in_=C.triu_s, compare_op=ALU.is_gt, fill=0.0,
                                         base=0, pattern=[[1, 128]], channel_multiplier=-1),
         [C.triu_s], [C.triu_s])
    C.base_mark = A.mark()
    C.cvt_i = 0
    return C


def next_ps6(C):
    p = C.ps[C.psi6 % 6]
    C.psi6 += 1
    return p


def next_ps(C):
    p = C.ps[C.psi % 7]
    C.psi += 1
    return p


def load_w_bf16(C, w2d, rows, cols, dst=None, stage=None):
    S, A = C.S, C.A
    kc_n = rows // 128
    if dst is None:
        dst = A.alloc([kc_n, cols], BF16)
    CH = 2048
    for kc in range(kc_n):
        for c0 in range(0, cols, CH):
            c1 = min(cols, c0 + CH)
            st = stage[C.cvt_i % len(stage)]
            S.dma(st[:, 0:c1 - c0], w2d[kc * 128:(kc + 1) * 128, c0:c1])
            eng = ('act', 'dve', 'act', 'dve', 'act', 'dve', 'pool')[C.cvt_i % 7]
            S.copy(dst[:, kc, c0:c1], st[:, 0:c1 - c0], eng=eng)
            C.cvt_i += 1
    return dst


def load_cols(C, row_ap, n, dst):
    S = C.S
    nj = n // 128
    src = row_ap.rearrange("(j p) -> p j", p=128)
    step = 16
    for j0 in range(0, nj, step):
        j1 = min(nj, j0 + step)
        S.dma(dst[:, j0:j1], src[:, j0:j1], allow_slow_non_contiguous=True)


def norm_T(C, xt, gb, hb, hT, col0, sq, ss, rstd):
    S = C.S
    if isinstance(hb, list):
        k = C.norm_i % len(hb)
        C.norm_i += 1
        hb, sq, ss, rstd = hb[k], sq[k], ss[k], rstd[k]
    S.act(sq, xt, AF.Square, accum_out=ss)
    S.act(rstd, ss, AF.Ln, scale=1.0 / D, bias=C.epsb)
    S.act(rstd, rstd, AF.Exp, scale=-0.5)
    S.stt(hb, xt, rstd, gb, ALU.mult, ALU.mult)
    pst = next_ps(C)
    pv = pst[:, :].bitcast(BF16)
    for c in range(8):
        S.transpose(pv[:, c * 128:(c + 1) * 128], hb[:, c * 128:(c + 1) * 128], C.identb)
    S.copy(hT[:, :, col0:col0 + 128], pv.rearrange("p (c t) -> p c t", c=8))


def ffn_phase(C, xsrc, xdst, g_norm, w_up, conv_w, conv_b, w_down):
    S, A = C.S, C.A
    m0 = A.mark()
    wup = A.alloc([22, 8, 256], BF16)
    wdn = A.alloc([22, D], BF16)
    gb = A.alloc([D], F32)
    S.dma(gb, g_norm.partition_broadcast(128))
    cw = A.alloc([3, 44], F32)
    cb = A.alloc([44], F32)
    for i in range(3):
        load_cols(C, conv_w[i, :], 2 * DFF, cw[:, i, :])
    load_cols(C, conv_b, 2 * DFF, cb)
    TB = 256
    NTT = TB // 128
    hTs = [A.alloc([8, TB + 2], BF16) for _ in range(2)]
    xts = [A.alloc([D], F32) for _ in range(2)]
    sq = A.alloc([D], BF16)
    hb = A.alloc([D], BF16)
    ss = A.alloc([1], F32)
    rstd = A.alloc([1], F32)
    Cc = [[A.alloc([TB], F32) for _ in range(2)] for _ in range(2)]
    sg = [A.alloc([TB], F32) for _ in range(2)]
    aj = [A.alloc([TB], BF16) for _ in range(4)]
    ysb = [A.alloc([D], F32) for _ in range(2)]
    acc = [[C.ps[4 + tt * 2 + nh] for nh in range(2)] for tt in range(NTT)]
    rot = {'i': 0}

    def rps():
        p = C.ps[rot['i'] % 4]
        rot['i'] += 1
        return p

    def norm_block(tb):
        hT = hTs[tb % 2]
        if tb == 0:
            S.memset(hT[:, :, 0:2], 0.0)
        else:
            S.copy(hT[:, :, 0:2], hTs[(tb - 1) % 2][:, :, TB:TB + 2], eng='pool')
        for tt in range(NTT):
            t = tb * NTT + tt
            xt = xts[t % 2]
            S.dma(xt, xsrc[t * 128:(t + 1) * 128, :])
            S.act(sq, xt, AF.Square, accum_out=ss)
            S.act(rstd, ss, AF.Ln, scale=1.0 / D, bias=C.epsb)
            S.act(rstd, rstd, AF.Exp, scale=-0.5)
            S.stt(hb, xt, rstd, gb, ALU.mult, ALU.mult)
            pst = rps()
            pv = pst[:, :].bitcast(BF16)
            for c in range(8):
                S.transpose(pv[:, c * 128:(c + 1) * 128], hb[:, c * 128:(c + 1) * 128], C.identb)
            S.copy(hT[:, :, 2 + tt * 128:2 + (tt + 1) * 128], pv.rearrange("p (c t) -> p c t", c=8))

    outs = []
    NB = T // TB
    norm_block(0)
    m1 = A.mark()
    stage = [A.alloc([2048], F32) for _ in range(3)]
    wup_src = w_up.rearrange("(kc p) c -> p kc c", p=128)
    for jp in range(11):
        for half in range(2):
            st = stage[C.cvt_i % 3]
            c0 = half * DFF + jp * 256
            S.dma(st.rearrange("p (a b) -> p a b", a=8), wup_src[:, :, c0:c0 + 256])
            eng = ('act', 'dve', 'act', 'dve', 'act', 'dve', 'pool')[C.cvt_i % 7]
            S.copy(wup[:, half * 11 + jp, :, :], st.rearrange("p (a b) -> p a b", a=8), eng=eng)
            C.cvt_i += 1
        st = stage[C.cvt_i % 3]
        S.dma(st.rearrange("p (a b) -> p a b", a=2),
              w_down[jp * 256:(jp + 1) * 256, :].rearrange("(a p) c -> p a c", p=128))
        eng = ('act', 'dve', 'act', 'dve', 'act', 'dve', 'pool')[C.cvt_i % 7]
        S.copy(wdn[:, 2 * jp:2 * jp + 2, :], st.rearrange("p (a b) -> p a b", a=2), eng=eng)
        C.cvt_i += 1
    A.release(m1)
    for tb in range(NB):
        hT = hTs[tb % 2]
        for j in range(22):
            cc = []
            for half in range(2):
                jj = half * 22 + j
                col0 = half * DFF + j * 128
                ps = rps()
                for kc in range(8):
                    S.mm(ps[:, 0:TB + 2], wup[:, half * 11 + j // 2, kc, (j % 2) * 128:(j % 2 + 1) * 128], hT[:, kc, :],
                         start=(kc == 0), stop=(kc == 7))
                c = Cc[j % 2][half]
                S.act(c, ps[:, 2:TB + 2], AF.Identity, scale=cw[:, 2, jj:jj + 1], bias=cb[:, jj:jj + 1])
                S.stt(c, ps[:, 1:TB + 1], cw[:, 1, jj:jj + 1], c, ALU.mult, ALU.add)
                S.stt(c, ps[:, 0:TB], cw[:, 0, jj:jj + 1], c, ALU.mult, ALU.add)
                cc.append(c)
            sg_ = sg[j % 2]
            a_ = aj[j % 4]
            S.act(sg_, cc[0], AF.Silu)
            S.tt(a_, sg_, cc[1], ALU.mult, eng='pool')

            def down(jd):
                ad = aj[jd % 4]
                for tt in range(NTT):
                    for nh in range(2):
                        S.mm(acc[tt][nh][:, :], ad[:, tt * 128:(tt + 1) * 128], wdn[:, jd, nh * 512:(nh + 1) * 512],
                             start=(jd == 0), stop=(jd == 21))
            if j >= 2:
                down(j - 2)
            if j == 21:
                down(20)
                down(21)
            if j == 10 and tb + 1 < NB:
                norm_block(tb + 1)
        for tt in range(NTT):
            t = tb * NTT + tt
            xt = xts[t % 2]
            S.dma(xt, xsrc[t * 128:(t + 1) * 128, :])
            y = ysb[t % 2]
            for nh in range(2):
                S.tt(y[:, nh * 512:(nh + 1) * 512], acc[tt][nh][:, :], xt[:, nh * 512:(nh + 1) * 512], ALU.add)
            outs.append(S.dma(xdst[t * 128:(t + 1) * 128, :], y, q='pool'))
    A.release(m0)
    return outs


def xa_phase(C, xsrc, xdst, mem, g_xa, g_mem, w_q, w_kv, w_o, g_q, g_k):
    S, A = C.S, C.A
    m0 = A.mark()
    wq = A.alloc([8, D], BF16)
    wkv = A.alloc([8, 2 * D], BF16)
    wo = A.alloc([8, D], BF16)
    m1 = A.mark()
    stage = [A.alloc([2048], F32) for _ in range(4)]
    load_w_bf16(C, w_q, D, D, dst=wq, stage=stage)
    load_w_bf16(C, w_kv, D, 2 * D, dst=wkv, stage=stage)
    load_w_bf16(C, w_o, D, D, dst=wo, stage=stage)
    A.release(m1)
    gb = A.alloc([D], F32)
    gmb = A.alloc([D], F32)
    S.dma(gb, g_xa.partition_broadcast(128))
    S.dma(gmb, g_mem.partition_broadcast(128))
    gq = A.alloc([2], F32)
    gk = A.alloc([2], F32)
    load_cols(C, g_q, 256, gq)
    load_cols(C, g_k, 256, gk)
    onesb = A.alloc([128], BF16)
    S.copy(onesb, C.ones)
    TB = 256
    xts = [A.alloc([D], F32) for _ in range(2)]
    sqj = [A.alloc([D], BF16) for _ in range(2)]
    hb = [A.alloc([D], BF16) for _ in range(2)]
    ss = [A.alloc([1], F32) for _ in range(2)]
    rstd1 = [A.alloc([1], F32) for _ in range(2)]
    memT = A.alloc([8, 256], BF16)
    kT = A.alloc([8, 256], BF16)
    V = A.alloc([2, D], BF16)
    hT = A.alloc([8, TB], BF16)
    qraw = [A.alloc([TB], F32) for _ in range(2)]
    sq = [A.alloc([TB], F32) for _ in range(2)]
    rstd = A.alloc([TB], F32)
    qn = [A.alloc([TB], BF16) for _ in range(2)]
    ex = [A.alloc([TB], BF16) for _ in range(2)]
    rinv = A.alloc([TB], F32)
    oT = A.alloc([8, TB], BF16)
    ysb = A.alloc([D], F32)
    for mt in range(2):
        xt = xts[mt]
        S.dma(xt, mem[mt * 128:(mt + 1) * 128, :])
        norm_T(C, xt, gmb, hb, memT, mt * 128, sqj, ss, rstd1)
    for h in range(4):
        for c in range(2):
            ch = 2 * h + c
            ps = next_ps(C)
            for kc in range(8):
                S.mm(ps[:, 0:256], wkv[:, kc, ch * 128:(ch + 1) * 128], memT[:, kc, :], start=(kc == 0), stop=(kc == 7))
            S.copy(qraw[c][:, 0:256], ps[:, 0:256], eng='act')
            S.act(sq[c][:, 0:256], ps[:, 0:256], AF.Square)
        ps = next_ps(C)
        for c in range(2):
            S.mm(ps[:, 0:256], C.ones, sq[c][:, 0:256], start=(c == 0), stop=(c == 1))
        S.act(rstd[:, 0:256], ps[:, 0:256], AF.Ln, scale=1.0 / 256, bias=C.epsb)
        S.act(rstd[:, 0:256], rstd[:, 0:256], AF.Exp, scale=-0.5)
        for c in range(2):
            S.stt(kT[:, 2 * h + c, :], qraw[c][:, 0:256], gk[:, c:c + 1], rstd[:, 0:256], ALU.mult, ALU.mult)
    for mt in range(2):
        for nh in range(2):
            ps = next_ps(C)
            for kc in range(8):
                S.mm(ps[:, :], memT[:, kc, mt * 128:(mt + 1) * 128], wkv[:, kc, D + nh * 512:D + (nh + 1) * 512],
                     start=(kc == 0), stop=(kc == 7))
            S.copy(V[:, mt, nh * 512:(nh + 1) * 512], ps[:, :], eng='act')
    outs = []
    NTT = TB // 128
    NB = T // TB
    hTs = [hT, A.alloc([8, TB], BF16)]
    oTs = [oT, A.alloc([8, TB], BF16)]
    xtb = [[A.alloc([D], F32) for _ in range(NTT)] for _ in range(3)]
    ysbs = [ysb, A.alloc([D], F32)]

    class HB:
        pass
    HBs = []
    for h in range(4):
        b_ = HB()
        b_.qraw = [A.alloc([TB], F32) for _ in range(2)]
        b_.sq = [A.alloc([TB], BF16) for _ in range(2)]
        b_.rstd = A.alloc([TB], F32)
        b_.qn = [A.alloc([TB], BF16) for _ in range(2)]
        b_.ex = [A.alloc([TB], BF16) for _ in range(2)]
        b_.rinv = A.alloc([TB], F32)
        HBs.append(b_)

    def front(tb):
        for tt in range(NTT):
            t = tb * NTT + tt
            xt = xtb[tb % 3][tt]
            S.dma(xt, xsrc[t * 128:(t + 1) * 128, :])
            norm_T(C, xt, gb, hb, hTs[tb % 2], tt * 128, sqj, ss, rstd1)
            yield

    def head(tb, h):
        B = HBs[h]
        hT_ = hTs[tb % 2]
        oT_ = oTs[tb % 2]
        for c in range(2):
            ch = 2 * h + c
            ps = next_ps(C)
            for kc in range(8):
                S.mm(ps[:, 0:TB], wq[:, kc, ch * 128:(ch + 1) * 128], hT_[:, kc, :], start=(kc == 0), stop=(kc == 7))
            S.copy(B.qraw[c], ps[:, 0:TB], eng='act')
            S.act(B.sq[c], ps[:, 0:TB], AF.Square)
            yield
        ps = next_ps(C)
        for c in range(2):
            S.mm(ps[:, 0:TB], C.onesb, B.sq[c], start=(c == 0), stop=(c == 1))
        S.act(B.rstd, ps[:, 0:TB], AF.Ln, scale=1.0 / 256, bias=C.epsb)
        yield
        S.act(B.rstd, B.rstd, AF.Exp, scale=-0.5)
        for c in range(2):
            S.stt(B.qn[c], B.qraw[c], gq[:, c:c + 1], B.rstd, ALU.mult, ALU.mult)
        yield
        for mc in range(2):
            ps = next_ps(C)
            for c in range(2):
                S.mm(ps[:, 0:TB], kT[:, 2 * h + c, mc * 128:(mc + 1) * 128], B.qn[c], start=(c == 0), stop=(c == 1))
            S.act(B.ex[mc], ps[:, 0:TB], AF.Exp, scale=1.0 / 16)
        yield
        ps = next_ps(C)
        for mc in range(2):
            S.mm(ps[:, 0:TB], onesb, B.ex[mc], start=(mc == 0), stop=(mc == 1))
        S.act(B.rinv, ps[:, 0:TB], AF.Ln)
        S.act(B.rinv, B.rinv, AF.Exp, scale=-1.0)
        yield
        for c in range(2):
            ps = next_ps(C)
            for mc in range(2):
                S.mm(ps[:, 0:TB], V[:, mc, h * 256 + c * 128:h * 256 + (c + 1) * 128], B.ex[mc],
                     start=(mc == 0), stop=(mc == 1))
            S.tt(oT_[:, 2 * h + c, :], ps[:, 0:TB], B.rinv, ALU.mult)
        yield

    def back(tb):
        oT_ = oTs[tb % 2]
        for tt in range(NTT):
            t = tb * NTT + tt
            xt = xtb[tb % 3][tt]
            y = ysbs[t % 2]
            for nh in range(2):
                ps = next_ps(C)
                for kc in range(8):
                    S.mm(ps[:, :], oT_[:, kc, tt * 128:(tt + 1) * 128], wo[:, kc, nh * 512:(nh + 1) * 512],
                         start=(kc == 0), stop=(kc == 7))
                S.tt(y[:, nh * 512:(nh + 1) * 512], ps[:, :], xt[:, nh * 512:(nh + 1) * 512], ALU.add)
            outs.append(S.dma(xdst[t * 128:(t + 1) * 128, :], y, q='pool'))
            yield

    def rr(gens):
        live = list(gens)
        while live:
            nxt = []
            for g_ in live:
                try:
                    next(g_)
                    nxt.append(g_)
                except StopIteration:
                    pass
            live = nxt

    rr([front(0)])
    for tb in range(NB):
        gens = [head(tb, h) for h in range(4)]
        if tb + 1 < NB:
            gens.append(front(tb + 1))
        if tb >= 1:
            gens.append(back(tb - 1))
        rr(gens)
    rr([back(NB - 1)])
    A.release(m0)
    return outs


def gla_phase(C, xsrc, xdst, g_norm, w_in, w_gk, b_gk, norm_g, w_out):
    S, A = C.S, C.A
    S.rank_mode = True
    m0 = A.mark()
    NIN = 3088
    win = A.alloc([8, NIN], BF16)
    wout = A.alloc([8, D], BF16)
    m1 = A.mark()
    stage = [A.alloc([2048], F32) for _ in range(4)]
    load_w_bf16(C, w_in, D, NIN, dst=win, stage=stage)
    load_w_bf16(C, w_out, D, D, dst=wout, stage=stage)
    A.release(m1)
    gb = A.alloc([D], F32)
    S.dma(gb, g_norm.partition_broadcast(128))
    bgk = A.alloc([512], F32)
    S.dma(bgk, b_gk.partition_broadcast(128))
    ngb = A.alloc([256], F32)
    S.dma(ngb, norm_g.partition_broadcast(128))
    wgk32 = A.alloc([512], F32)
    S.dma(wgk32[0:16, :], w_gk)
    wgk = A.alloc([512], BF16)
    S.copy(wgk[0:16, :], wgk32[0:16, :])
    xts = [A.alloc([D], F32) for _ in range(4)]
    sqj = [A.alloc([D], BF16) for _ in range(2)]
    hb = [A.alloc([D], BF16) for _ in range(2)]
    ss = [A.alloc([1], F32) for _ in range(2)]
    rstd1 = [A.alloc([1], F32) for _ in range(2)]
    hTs = [A.alloc([8, 128], BF16) for _ in range(2)]
    lrT = A.alloc([128], BF16)
    zb = A.alloc([512], F32)
    Lgs = [A.alloc([512], F32) for _ in range(2)]
    vtms = [A.alloc([D], BF16) for _ in range(3)]
    rss = [A.alloc([D], F32) for _ in range(4)]

    class HB:
        pass
    HBs = []
    for h in range(4):
        b_ = HB()
        b_.BC = A.alloc([128], F32)
        b_.nbm = A.alloc([1], F32)
        b_.eb = A.alloc([1], F32)
        b_.E = [A.alloc([128], F32) for _ in range(4)]
        for nm in ("qe", "ke", "kdT"):
            setattr(b_, nm, A.alloc([128], BF16))
        b_.hand = []
        for par in range(2):
            hd = HB()
            for nm in ("qg", "kd", "att"):
                setattr(hd, nm, A.alloc([128], BF16))
            hd.eb = A.alloc([1], F32)
            b_.hand.append(hd)
        HBs.append(b_)
    St = [A.alloc([256], F32) for _ in range(4)]
    Sb = [A.alloc([256], BF16) for _ in range(4)]
    o_alls = [A.alloc([D], F32) for _ in range(2)]
    oss = A.alloc([4], F32)
    orstd = A.alloc([4], F32)
    ob = A.alloc([D], BF16)
    oT = A.alloc([8, 128], BF16)
    ysbs = [A.alloc([D], F32) for _ in range(2)]
    for h in range(4):
        S.memset(St[h], 0.0)
        S.memset(Sb[h], 0.0)
    outs = []
    sc = 128.0 ** -0.5

    def front(t):
        xt = xts[t % 4]
        hT, Lg, vtm, rs = hTs[t % 2], Lgs[t % 2], vtms[t % 3], rss[t % 4]
        S.dma(xt, xsrc[t * 128:(t + 1) * 128, :])
        norm_T(C, xt, gb, hb, hT, 0, sqj, ss, rstd1)
        yield
        ps = next_ps(C)
        for kc in range(8):
            S.mm(ps[0:16, 0:128], win[:, kc, 3072:3088], hT[:, kc, :], start=(kc == 0), stop=(kc == 7))
        S.copy(lrT[0:16, :], ps[0:16, 0:128], eng='act')
        yield
        ps = next_ps(C)
        S.mm(ps[:, :], lrT[0:16, :], wgk[0:16, :])
        S.tt(zb, ps[:, :], bgk, ALU.add)
        S.act(zb, zb, AF.Exp, scale=-1.0)
        S.act(Lg, zb, AF.Ln, bias=C.ones[:, 0:1])
        yield
        for nh in range(2):
            ps = next_ps(C)
            for kc in range(8):
                S.mm(ps[:, :], hT[:, kc, :], win[:, kc, 1024 + nh * 512:1024 + (nh + 1) * 512], start=(kc == 0), stop=(kc == 7))
            S.copy(vtm[:, nh * 512:(nh + 1) * 512], ps[:, :], eng='act')
            yield
        for nh in range(2):
            ps = next_ps(C)
            for kc in range(8):
                S.mm(ps[:, :], hT[:, kc, :], win[:, kc, 2048 + nh * 512:2048 + (nh + 1) * 512], start=(kc == 0), stop=(kc == 7))
            S.act(rs[:, nh * 512:(nh + 1) * 512], ps[:, :], AF.Silu)
            yield

    def headA(t, h):
        B = HBs[h]
        Hd = B.hand[t % 2]
        hT, Lg = hTs[t % 2], Lgs[t % 2]
        E = B.E
        ps = next_ps(C)
        S.mm(ps[:, 0:128], Lg[:, h * 128:(h + 1) * 128], C.triu)
        S.act(B.BC, ps[:, 0:128], AF.Copy, scale=-1.0 / 16)
        yield
        S.act(B.nbm, B.BC[:, 64:65], AF.Copy, scale=-1.0)
        S.act(E[0], B.BC, AF.Exp, bias=B.nbm)
        S.act(E[1], B.BC, AF.Exp, scale=-1.0, bias=B.BC[:, 64:65])
        yield
        S.act(E[2], B.BC, AF.Exp)
        S.act(E[3], B.BC, AF.Exp, scale=-1.0, bias=B.BC[:, 127:128])
        S.act(Hd.eb, B.BC[:, 127:128], AF.Exp)
        yield
        psq = next_ps(C)
        for kc in range(8):
            S.mm(psq[:, 0:128], win[:, kc, h * 128:(h + 1) * 128], hT[:, kc, :], start=(kc == 0), stop=(kc == 7))
        S.stt(B.qe, psq[:, 0:128], sc, E[0], ALU.mult, ALU.mult)
        S.stt(Hd.qg, psq[:, 0:128], sc, E[2], ALU.mult, ALU.mult)
        yield
        psk = next_ps(C)
        for kc in range(8):
            S.mm(psk[:, 0:128], win[:, kc, 512 + h * 128:512 + (h + 1) * 128], hT[:, kc, :], start=(kc == 0), stop=(kc == 7))
        S.tt(B.ke, psk[:, 0:128], E[1], ALU.mult)
        S.tt(B.kdT, psk[:, 0:128], E[3], ALU.mult)
        yield
        pst = next_ps(C)
        ptv = pst[:, :].bitcast(BF16)
        S.transpose(ptv[:, 0:128], B.kdT, C.identb)
        S.copy(Hd.kd, ptv[:, 0:128], eng='act')
        psa = next_ps(C)
        S.mm(psa[:, 0:128], B.ke, B.qe)
        S.tt(Hd.att, psa[:, 0:128], C.triu, ALU.mult)
        yield

    def headB(t, h):
        B = HBs[h]
        Hd = B.hand[t % 2]
        vtm, o_all = vtms[t % 3], o_alls[t % 2]
        pso = next_ps(C)
        S.mm(pso[:, 0:256], Hd.att, vtm[:, h * 256:(h + 1) * 256], start=True, stop=False)
        S.mm(pso[:, 0:256], Hd.qg, Sb[h], start=False, stop=True)
        S.copy(o_all[:, h * 256:(h + 1) * 256], pso[:, 0:256], eng='act')
        pss = next_ps(C)
        S.mm(pss[:, 0:256], Hd.kd, vtm[:, h * 256:(h + 1) * 256])
        S.stt(St[h], St[h], Hd.eb, pss[:, 0:256], ALU.mult, ALU.add)
        S.copy(Sb[h], St[h], eng='pool')
        yield

    def back(t):
        xt = xts[t % 4]
        o_all, rs, ysb = o_alls[t % 2], rss[t % 4], ysbs[t % 2]
        for h in range(4):
            S.act(ob[:, h * 256:(h + 1) * 256], o_all[:, h * 256:(h + 1) * 256], AF.Square, accum_out=oss[:, h:h + 1])
        S.act(orstd, oss, AF.Ln, scale=1.0 / 256, bias=C.epsb)
        S.act(orstd, orstd, AF.Exp, scale=-0.5)
        yield
        for h in range(4):
            S.stt(o_all[:, h * 256:(h + 1) * 256], o_all[:, h * 256:(h + 1) * 256], orstd[:, h:h + 1], ngb, ALU.mult, ALU.mult)
        S.tt(ob, o_all, rs, ALU.mult)
        yield
        pst = next_ps(C)
        pv = pst[:, :].bitcast(BF16)
        for c in range(8):
            S.transpose(pv[:, c * 128:(c + 1) * 128], ob[:, c * 128:(c + 1) * 128], C.identb)
        S.copy(oT, pv.rearrange("p (c t) -> p c t", c=8))
        yield
        for nh in range(2):
            ps = next_ps(C)
            for kc in range(8):
                S.mm(ps[:, :], oT[:, kc, :], wout[:, kc, nh * 512:(nh + 1) * 512], start=(kc == 0), stop=(kc == 7))
            S.tt(ysb[:, nh * 512:(nh + 1) * 512], ps[:, :], xt[:, nh * 512:(nh + 1) * 512], ALU.add)
        outs.append(S.dma(xdst[t * 128:(t + 1) * 128, :], ysb, q='pool'))
        yield

    def rr(gens):
        live = list(gens)
        while live:
            nxt = []
            for g_ in live:
                try:
                    next(g_)
                    nxt.append(g_)
                except StopIteration:
                    pass
            live = nxt

    rr([front(0)])
    rr([headA(0, h) for h in range(4)] + [front(1)])
    for t in range(NT):
        gens = []
        if t >= 1:
            gens.append(back(t - 1))
        gens += [headB(t, h) for h in range(4)]
        if t + 1 < NT:
            gens += [headA(t + 1, h) for h in range(4)]
        if t + 2 < NT:
            gens.append(front(t + 2))
        rr(gens)
    rr([back(NT - 1)])
    S.rank_mode = False
    A.release(m0)
    return outs


def dn_phase(C, xsrc, xdst, g_norm, w_in, conv_w, a_log, dt_bias, norm_g, w_out):
    S, A = C.S, C.A
    S.rank_mode = True
    m0 = A.mark()
    NA = 2056
    win = A.alloc([8, NA], BF16)
    wout = A.alloc([4, D], BF16)
    m1 = A.mark()
    stage = [A.alloc([2048], F32) for _ in range(4)]
    load_w_bf16(C, w_in[:, 0:NA], D, NA, dst=win, stage=stage)
    load_w_bf16(C, w_out[0:512, :], 512, D, dst=wout, stage=stage)
    A.release(m1)
    gb = A.alloc([D], F32)
    S.dma(gb, g_norm.partition_broadcast(128))
    cw = A.alloc([4, 12], F32)
    for i in range(4):
        load_cols(C, conv_w[i, :], 1536, cw[:, i, :])
    ngb = A.alloc([128], F32)
    S.dma(ngb, norm_g.partition_broadcast(128))
    dtb = A.alloc([4], F32)
    S.dma(dtb, dt_bias.partition_broadcast(128))
    nea = A.alloc([4], F32)
    S.dma(nea, a_log.partition_broadcast(128))
    S.act(nea, nea, AF.Exp)
    S.ts(nea, nea, -1.0, None, ALU.mult)
    sqj = [A.alloc([D], BF16) for _ in range(2)]
    hb = [A.alloc([D], BF16) for _ in range(2)]
    ss = [A.alloc([1], F32) for _ in range(2)]
    rstd1 = [A.alloc([1], F32) for _ in range(2)]
    hT = A.alloc([8, 128], BF16)
    U = A.alloc([12, 131], F32)
    S.memset(U, 0.0)
    cvs = [A.alloc([12, 128], F32) for _ in range(2)]
    sqf = A.alloc([128], F32)
    rinv = A.alloc([128], F32)
    qns = [[A.alloc([128], F32) for _ in range(4)] for _ in range(3)]
    kns = [[A.alloc([128], F32) for _ in range(4)] for _ in range(2)]
    ab = A.alloc([8], F32)
    betas = [A.alloc([4], F32) for _ in range(2)]
    ggs = [A.alloc([4], F32) for _ in range(2)]
    zss = [A.alloc([512], F32) for _ in range(4)]
    sq8 = A.alloc([8, 128], F32)
    rinv8 = A.alloc([8, 128], F32)

    class HB:
        pass
    HBs = []
    for h in range(4):
        b_ = HB()
        for nm in ("ktm", "vtm", "tmp", "arg", "DT", "AT0", "Am", "ATm", "P0", "P1", "PT0", "PT1", "TT", "vb", "kbg",
                   "delta", "o1"):
            setattr(b_, nm, A.alloc([128], F32))
        for nm in ("gcc", "beg"):
            setattr(b_, nm, A.alloc([1], F32))
        b_.hand = []
        for par in range(2):
            hd = HB()
            for nm in ("usb", "wT", "qkm", "kdec"):
                setattr(hd, nm, A.alloc([128], F32))
            for nm in ("egc", "egl"):
                setattr(hd, nm, A.alloc([1], F32))
            b_.hand.append(hd)
        HBs.append(b_)
    St = [A.alloc([128], F32) for _ in range(4)]
    o_as = [A.alloc([512], F32) for _ in range(2)]
    oss = A.alloc([4], F32)
    orstd = A.alloc([4], F32)
    ob = A.alloc([512], BF16)
    oT = A.alloc([4, 128], BF16)
    ysbs = [A.alloc([D], F32) for _ in range(2)]
    xts = [A.alloc([D], F32) for _ in range(4)]
    for h in range(4):
        S.memset(St[h], 0.0)
    outs = []
    sc = 128.0 ** -0.5
    def front(t):
        xt = xts[t % 4]
        cv, qn, kn, beta, gg, zs = cvs[t % 2], qns[t % 3], kns[t % 2], betas[t % 2], ggs[t % 2], zss[t % 4]
        S.dma(xt, xsrc[t * 128:(t + 1) * 128, :])
        norm_T(C, xt, gb, hb, hT, 0, sqj, ss, rstd1)
        for c in range(12):
            ps = next_ps(C)
            for kc in range(8):
                S.mm(ps[:, 0:128], win[:, kc, c * 128:(c + 1) * 128], hT[:, kc, :], start=(kc == 0), stop=(kc == 7))
            S.copy(U[:, c, 3:131], ps[:, 0:128], eng='act')
            S.act(cv[:, c, :], ps[:, 0:128], AF.Copy, scale=cw[:, 3, c:c + 1])
            for i in range(3):
                S.stt(cv[:, c, :], U[:, c, i:i + 128], cw[:, i, c:c + 1], cv[:, c, :], ALU.mult, ALU.add)
            S.copy(U[:, c, 0:3], U[:, c, 128:131], eng='pool')
            yield
        S.act(cv, cv, AF.Silu)
        ps = next_ps(C)
        for kc in range(8):
            S.mm(ps[:, 0:8], hT[:, kc, :], win[:, kc, 2048:2056], start=(kc == 0), stop=(kc == 7))
        S.copy(ab, ps[:, 0:8], eng='act')
        ps = next_ps(C)
        for kc in range(8):
            S.mm(ps[:, :], hT[:, kc, :], win[:, kc, 1536:2048], start=(kc == 0), stop=(kc == 7))
        S.act(zs, ps[:, :], AF.Silu)
        yield
        S.act(beta, ab[:, 4:8], AF.Exp, scale=-1.0)
        S.ts(beta, beta, 1.0, None, ALU.add)
        S.recip(beta, beta)
        S.tt(gg, ab[:, 0:4], dtb, ALU.add)
        S.act(gg, gg, AF.Exp)
        S.act(gg, gg, AF.Ln, bias=C.ones[:, 0:1])
        S.tt(gg, gg, nea, ALU.mult)
        yield
        S.act(sq8, cv[:, 0:8, :], AF.Square)
        for half in range(2):
            ps = next_ps(C)
            S.mm(ps[:, :], C.ones, sq8[:, half * 4:(half + 1) * 4, :])
            S.act(rinv8[:, half * 4:(half + 1) * 4, :], ps[:, :].rearrange("p (a b) -> p a b", a=4), AF.Ln, bias=C.epsb)
        S.act(rinv8, rinv8, AF.Exp, scale=-0.5)
        yield
        for h in range(4):
            S.stt(qn[h], cv[:, h, :], sc, rinv8[:, h, :], ALU.mult, ALU.mult)
        for h in range(4):
            S.tt(kn[h], cv[:, 4 + h, :], rinv8[:, 4 + h, :], ALU.mult)

        yield

    def chainA(t, h):
        B = HBs[h]
        Hd = B.hand[t % 2]
        cv, qn, kn, beta, gg = cvs[t % 2], qns[t % 3], kns[t % 2], betas[t % 2], ggs[t % 2]
        ps = next_ps(C)
        S.transpose(ps[:, 0:128], kn[h], C.ident)
        S.copy(B.ktm, ps[:, 0:128], eng='act')
        ps = next_ps(C)
        S.transpose(ps[:, 0:128], cv[:, 8 + h, :], C.ident)
        S.copy(B.vtm, ps[:, 0:128], eng='act')
        yield
        S.ts(B.tmp, C.triu, gg[:, h:h + 1], None, ALU.mult)
        psg = next_ps(C)
        S.mm(psg[:, 0:128], C.ones, B.tmp)
        psc = next_ps(C)
        S.mm(psc[:, 0:1], C.triu, gg[:, h:h + 1])
        S.copy(B.gcc, psc[:, 0:1], eng='act')
        S.act(Hd.egl, psg[:, 127:128], AF.Exp)
        S.ts(B.arg, psg[:, 0:128], B.gcc, 0.0, ALU.subtract, ALU.min)
        yield
        S.act(B.DT, B.arg, AF.Exp)
        S.act(Hd.egc, B.gcc, AF.Exp)
        S.tt(B.beg, beta[:, h:h + 1], Hd.egc, ALU.mult)
        psk = next_ps(C)
        S.mm(psk[:, 0:128], kn[h], kn[h])
        S.tt(B.AT0, psk[:, 0:128], B.DT, ALU.mult)
        S.tt(B.AT0, B.AT0, C.triu_s, ALU.mult, eng='pool')
        yield
        ps = next_ps(C)
        S.transpose(ps[:, 0:128], B.AT0, C.ident)
        S.act(B.Am, ps[:, 0:128], AF.Copy, scale=beta[:, h:h + 1])
        yield
        ps = next_ps(C)
        S.transpose(ps[:, 0:128], B.Am, C.ident)
        S.copy(B.ATm, ps[:, 0:128], eng='act')
        S.tt(B.TT, C.ident, B.ATm, ALU.subtract, eng='pool')
        yield
        pk, pkt = B.ATm, B.Am
        Ps = (B.P0, B.P1)
        PTs = (B.PT0, B.PT1)
        for lvl in range(1, 7):
            npk, npkt = Ps[lvl % 2], PTs[lvl % 2]
            if lvl < 6:
                ps1 = next_ps(C)
                S.mm(ps1[:, 0:128], pkt, pk)
                S.copy(npk, ps1[:, 0:128], eng='act')
            ps2 = next_ps(C)
            S.mm(ps2[:, 0:128], pk, pkt)
            S.copy(npkt, ps2[:, 0:128], eng='act')
            pk, pkt = npk, npkt
            yield
            ps3 = next_ps(C)
            S.mm(ps3[:, 0:128], pkt, B.TT)
            S.tt(B.TT, ps3[:, 0:128], B.TT, ALU.add)
            yield
        S.ts(B.vb, B.vtm, beta[:, h:h + 1], None, ALU.mult)
        S.ts(B.kbg, B.ktm, B.beg, None, ALU.mult)
        psu = next_ps(C)
        S.mm(psu[:, 0:128], B.TT, B.vb)
        S.copy(Hd.usb, psu[:, 0:128], eng='act')
        psw = next_ps(C)
        S.mm(psw[:, 0:128], B.kbg, B.TT)
        S.copy(Hd.wT, psw[:, 0:128], eng='act')
        ps = next_ps(C)
        S.mm(ps[:, 0:128], kn[h], qn[h])
        S.tt(Hd.qkm, ps[:, 0:128], B.DT, ALU.mult)
        S.tt(Hd.qkm, Hd.qkm, C.triu, ALU.mult, eng='pool')
        S.ts(Hd.kdec, B.ktm, B.DT[:, 127:128], None, ALU.mult)
        yield

    def chainB(t, h):
        B = HBs[h]
        Hd = B.hand[t % 2]
        qn, o_a = qns[t % 3], o_as[t % 2]
        ps = next_ps(C)
        S.mm(ps[:, 0:128], Hd.wT, St[h])
        S.tt(B.delta, Hd.usb, ps[:, 0:128], ALU.subtract)
        psq = next_ps(C)
        S.mm(psq[:, 0:128], qn[h], St[h])
        S.act(B.o1, psq[:, 0:128], AF.Copy, scale=Hd.egc)
        yield
        ps = next_ps(C)
        S.mm(ps[:, 0:128], Hd.qkm, B.delta)
        S.tt(o_a[:, h * 128:(h + 1) * 128], ps[:, 0:128], B.o1, ALU.add)
        ps = next_ps(C)
        S.mm(ps[:, 0:128], Hd.kdec, B.delta)
        S.stt(St[h], St[h], Hd.egl, ps[:, 0:128], ALU.mult, ALU.add)
        yield


    def back(t):
        xt = xts[t % 4]
        o_a, zs, ysb = o_as[t % 2], zss[t % 4], ysbs[t % 2]
        for h in range(4):
            S.act(ob[:, h * 128:(h + 1) * 128], o_a[:, h * 128:(h + 1) * 128], AF.Square, accum_out=oss[:, h:h + 1])
        S.act(orstd, oss, AF.Ln, scale=1.0 / 128, bias=C.epsb)
        S.act(orstd, orstd, AF.Exp, scale=-0.5)
        for h in range(4):
            S.stt(o_a[:, h * 128:(h + 1) * 128], o_a[:, h * 128:(h + 1) * 128], orstd[:, h:h + 1], ngb, ALU.mult, ALU.mult)
        S.tt(ob, o_a, zs, ALU.mult)
        yield
        pst = next_ps(C)
        pv = pst[:, :].bitcast(BF16)
        for c in range(4):
            S.transpose(pv[:, c * 128:(c + 1) * 128], ob[:, c * 128:(c + 1) * 128], C.identb)
        S.copy(oT, pv[:, 0:512].rearrange("p (c t) -> p c t", c=4))
        yield
        for nh in range(2):
            ps = next_ps(C)
            for c in range(4):
                S.mm(ps[:, :], oT[:, c, :], wout[:, c, nh * 512:(nh + 1) * 512], start=(c == 0), stop=(c == 3))
            S.tt(ysb[:, nh * 512:(nh + 1) * 512], ps[:, :], xt[:, nh * 512:(nh + 1) * 512], ALU.add)
        outs.append(S.dma(xdst[t * 128:(t + 1) * 128, :], ysb, q='pool'))
        yield

    def run_round_robin(gens):
        live = list(gens)
        while live:
            nxt = []
            for g_ in live:
                try:
                    next(g_)
                    nxt.append(g_)
                except StopIteration:
                    pass
            live = nxt

    run_round_robin([front(0)])
    run_round_robin([chainA(0, h) for h in range(4)] + [front(1)])
    for t in range(NT):
        gens = []
        if t >= 1:
            gens.append(back(t - 1))
        gens += [chainB(t, h) for h in range(4)]
        if t + 1 < NT:
            gens += [chainA(t + 1, h) for h in range(4)]
        if t + 2 < NT:
            gens.append(front(t + 2))
        run_round_robin(gens)
    run_round_robin([back(NT - 1)])
    S.rank_mode = False
    A.release(m0)
    return outs


def sb_phase(C, xorig, xsrc, xdst, g_norm, w_in, g_q, g_k, w_out):
    S, A, nc = C.S, C.A, C.nc
    S.rank_mode = True
    m0 = A.mark()
    if not hasattr(C, 'sbq'):
        C.sbq = nc.dram_tensor("sb_q", [4, 128, T], BF16, kind="Internal").ap()
        C.sbk = nc.dram_tensor("sb_k", [4, 128, T], BF16, kind="Internal").ap()
        C.sbv = nc.dram_tensor("sb_v", [4, 128, NT, 128], BF16, kind="Internal").ap()
    wout = A.alloc([4, D], BF16)
    m1 = A.mark()
    gb = A.alloc([D], F32)
    S.dma(gb, g_norm.partition_broadcast(128))
    gq = A.alloc([1], F32)
    gk = A.alloc([1], F32)
    load_cols(C, g_q, 128, gq)
    load_cols(C, g_k, 128, gk)
    win = A.alloc([8, 1536], BF16)
    stage = [A.alloc([2048], F32) for _ in range(4)]
    load_w_bf16(C, w_in[:, 2056:3592], D, 1536, dst=win, stage=stage)
    load_w_bf16(C, w_out[512:1024, :], 512, D, dst=wout, stage=stage)
    TB = 512
    xts = [A.alloc([D], F32) for _ in range(2)]
    sqj = [A.alloc([D], BF16) for _ in range(2)]
    hb = [A.alloc([D], BF16) for _ in range(2)]
    ss = [A.alloc([1], F32) for _ in range(2)]
    rstd1 = [A.alloc([1], F32) for _ in range(2)]
    hT = A.alloc([8, TB], BF16)
    hTs = [hT, A.alloc([8, TB], BF16)]
    sq = [A.alloc([TB], BF16) for _ in range(8)]
    rstd = [A.alloc([TB], F32) for _ in range(8)]
    qk_o = [A.alloc([TB], BF16) for _ in range(8)]
    qg_b = [A.alloc([TB], F32) for _ in range(8)]
    v_o = [A.alloc([512], BF16) for _ in range(2)]
    sc = 128.0 ** -0.5
    NTT = TB // 128

    def p1_front(tb):
        for tt in range(NTT):
            t = tb * NTT + tt
            xt = xts[t % 2]
            S.dma(xt, xorig[t * 128:(t + 1) * 128, :])
            norm_T(C, xt, gb, hb, hTs[tb % 2], tt * 128, sqj, ss, rstd1)
            yield

    def p1_qk(tb, idx):
        h, isk = idx // 2, idx % 2
        c0, dst, gcol, scl = ((h * 128, C.sbq, gq, sc), (512 + h * 128, C.sbk, gk, 1.0))[isk]
        hT_ = hTs[tb % 2]
        sq_, rs_, o_ = sq[idx], rstd[idx], qk_o[idx]
        ps = next_ps(C)
        for kc in range(8):
            S.mm(ps[:, :], win[:, kc, c0:c0 + 128], hT_[:, kc, :], start=(kc == 0), stop=(kc == 7))
        S.act(sq_, ps[:, :], AF.Square)
        S.ts(qg_b[idx], ps[:, :], gcol, scl, ALU.mult, ALU.mult)
        yield
        ps2 = next_ps(C)
        S.mm(ps2[:, :], C.onesb, sq_)
        S.act(rs_, ps2[:, :], AF.Ln, scale=1.0 / 128, bias=C.epsb)
        yield
        S.act(rs_, rs_, AF.Exp, scale=-0.5)
        S.tt(o_, qg_b[idx], rs_, ALU.mult, eng='pool')
        S.dma(dst[h, :, tb * TB:(tb + 1) * TB], o_, q='sp')
        yield

    def p1_v(tb):
        hT_ = hTs[tb % 2]
        for tt in range(NTT):
            t = tb * NTT + tt
            ps = next_ps(C)
            for kc in range(8):
                S.mm(ps[:, :], hT_[:, kc, tt * 128:(tt + 1) * 128], win[:, kc, 1024:1536], start=(kc == 0), stop=(kc == 7))
            vo = v_o[t % 2]
            S.copy(vo, ps[:, :], eng='act')
            S.dma(C.sbv[:, :, t, :].rearrange("h p d -> p h d"), vo.rearrange("p (h d) -> p h d", h=4), q='sp')
            yield

    def rr1(gens):
        live = list(gens)
        while live:
            nxt = []
            for g_ in live:
                try:
                    next(g_)
                    nxt.append(g_)
                except StopIteration:
                    pass
            live = nxt

    NB1 = T // TB
    rr1([p1_front(0)])
    for tb in range(NB1):
        gens = [p1_qk(tb, idx) for idx in range(8)] + [p1_v(tb)]
        if tb + 1 < NB1:
            gens.append(p1_front(tb + 1))
        rr1(gens)
    A.release(m1)
    o_all = A.alloc([NT, 512], BF16)
    qT = [A.alloc([T], BF16) for _ in range(1)]
    kT = [A.alloc([T], BF16) for _ in range(1)]
    V = [A.alloc([NT, 128], BF16) for _ in range(1)]
    sp = [A.alloc([T], F32) for _ in range(2)]
    w = [A.alloc([T], F32) for _ in range(2)]
    Pc = A.alloc([T], F32)
    att = [A.alloc([T], BF16) for _ in range(2)]
    attT = [A.alloc([512], BF16) for _ in range(3)]
    ntot = [A.alloc([1], F32) for _ in range(2)]
    mask_s = A.alloc([128], F32)
    S.tt(mask_s, C.tril, C.ident, ALU.subtract)
    mask_b = A.alloc([128], BF16)
    S.copy(mask_b, mask_s)
    V2 = [V[0], A.alloc([NT, 128], BF16)]
    its = [(h, i) for h in range(4) for i in range(NT)]
    state = {'ai': 0}

    pss = {}

    def stage_a1(n):
        h, i = its[n]
        b = n % 2
        L = 128 * (i + 1)
        if i == 0:
            S.dma(qT[0], C.sbq[h])
            S.dma(kT[0], C.sbk[h])
            S.dma(V2[h % 2], C.sbv[h])
        sp_, w_ = sp[b], w[b]
        pss[n] = []
        for ci, k0 in enumerate(range(0, L, 512)):
            k1 = min(L, k0 + 512)
            if ci >= 5:
                sub_chunk(n, ci - 5)
            ps = next_ps6(C)
            pss[n].append((ps, k0, k1, False))
            S.mm(ps[:, 0:k1 - k0], qT[0][:, i * 128:(i + 1) * 128], kT[0][:, k0:k1])
            S.act(sp_[:, k0:k1], ps[:, 0:k1 - k0], AF.Exp)
            S.act(sp_[:, k0:k1], sp_[:, k0:k1], AF.Ln, bias=C.ones[:, 0:1])

    def sub_chunk(n, ci):
        b = n % 2
        ps, k0, k1, done = pss[n][ci]
        if done:
            return
        S.tt(w[b][:, k0:k1], ps[:, 0:k1 - k0], sp[b][:, k0:k1], ALU.subtract)
        pss[n][ci] = (ps, k0, k1, True)

    def stage_a2(n):
        h, i = its[n]
        b = n % 2
        L = 128 * (i + 1)
        sp_ = sp[b]
        for ci in range(len(pss[n])):
            sub_chunk(n, ci)
        del pss[n]
        S.tt(sp_[:, L - 128:L], sp_[:, L - 128:L], mask_s, ALU.mult, eng='pool')

    def stage_b1(n):
        h, i = its[n]
        b = n % 2
        L = 128 * (i + 1)
        sp_, w_, nt_ = sp[b], w[b], ntot[b]
        S.scan(Pc[:, 0:L], sp_[:, 0:L], sp_[:, 0:L], 0.0, ALU.add, ALU.max)
        S.ts(nt_, Pc[:, L - 1:L], -1.0, None, ALU.mult)
        S.tt(Pc[:, 0:L], Pc[:, 0:L], w_[:, 0:L], ALU.add, eng=('dve' if n % 3 == 0 else 'pool'))

    def stage_b2(n):
        h, i = its[n]
        b = n % 2
        L = 128 * (i + 1)
        att_, nt_ = att[b], ntot[b]
        S.act(att_[:, 0:L], Pc[:, 0:L], AF.Exp, bias=nt_)
        S.tt(att_[:, L - 128:L], att_[:, L - 128:L], mask_b, ALU.mult, eng='pool')

    def stage_c(n):
        h, i = its[n]
        b = n % 2
        att_ = att[b]
        vh = V2[h % 2]
        pso = C.ps[6 + b]
        nkt = i + 1
        groups = [(g0, min(nkt, g0 + 4)) for g0 in range(0, nkt, 4)]
        ats = {}

        def tr(gi):
            g0, g1 = groups[gi]
            pst = next_ps6(C)
            ptv = pst[:, :].bitcast(BF16)
            for kt in range(g0, g1):
                S.transpose(ptv[:, (kt - g0) * 128:(kt - g0 + 1) * 128], att_[:, kt * 128:(kt + 1) * 128], C.identb)
            at = attT[state['ai'] % 3]
            state['ai'] += 1
            S.copy(at[:, 0:(g1 - g0) * 128], ptv[:, 0:(g1 - g0) * 128], eng=('act' if state['ai'] % 3 == 0 else 'dve'))
            ats[gi] = at

        def av(gi):
            g0, g1 = groups[gi]
            at = ats[gi]
            for kt in range(g0, g1):
                S.mm(pso[:, 0:128], at[:, (kt - g0) * 128:(kt - g0 + 1) * 128], vh[:, kt, :],
                     start=(kt == 0), stop=(kt == nkt - 1))

        tr(0)
        for gi in range(len(groups)):
            if gi + 1 < len(groups):
                tr(gi + 1)
            av(gi)
        S.copy(o_all[:, i, h * 128:(h + 1) * 128], pso[:, 0:128], eng='act')

    NI = len(its)
    for n in range(NI + 2):
        if n < NI:
            stage_a1(n)
        if 0 <= n - 1 < NI:
            stage_b1(n - 1)
        if n < NI:
            stage_a2(n)
        if 0 <= n - 2 < NI:
            stage_c(n - 2)
        if 0 <= n - 1 < NI:
            stage_b2(n - 1)
    oT = [A.alloc([4, 128], BF16) for _ in range(2)]
    xts = [A.alloc([D], F32) for _ in range(2)]
    ysb = [A.alloc([D], F32) for _ in range(2)]
    outs = []
    for i in range(NT):
        xt = xts[i % 2]
        S.dma(xt, xsrc[i * 128:(i + 1) * 128, :])
        pst = next_ps(C)
        pv = pst[:, :].bitcast(BF16)
        for c in range(4):
            S.transpose(pv[:, c * 128:(c + 1) * 128], o_all[:, i, c * 128:(c + 1) * 128], C.identb)
        S.copy(oT[i % 2], pv[:, 0:512].rearrange("p (c t) -> p c t", c=4))
        y = ysb[i % 2]
        for nh in range(2):
            ps = next_ps(C)
            for c in range(4):
                S.mm(ps[:, :], oT[i % 2][:, c, :], wout[:, c, nh * 512:(nh + 1) * 512], start=(c == 0), stop=(c == 3))
            S.tt(y[:, nh * 512:(nh + 1) * 512], ps[:, :], xt[:, nh * 512:(nh + 1) * 512], ALU.add)
        outs.append(S.dma(xdst[i * 128:(i + 1) * 128, :], y, q='pool'))
    S.rank_mode = False
    A.release(m0)
    return outs


from concourse.bass_utils import run_bass_kernel_spmd

_NC_CACHE = {}

_WNAMES = ["norm_mix", "norm_xa", "norm_mem", "norm_ffn", "xa_w_q", "xa_w_kv", "xa_w_o", "xa_g_q", "xa_g_k",
           "ffn_w_up", "ffn_conv_w", "ffn_conv_b", "ffn_w_down", "ab_w_in", "ab_conv_w", "dn_a_log", "dn_dt_bias",
           "dn_norm_g", "sb_g_q", "sb_g_k", "ab_w_out", "gla_w_in", "gla_w_gk", "gla_b_gk", "gla_norm_g", "gla_w_out"]


def build_program(shapes):
    nc = bass.Bass("TRN2", target_bir_lowering=False)
    ap = {}
    for name, shp in shapes.items():
        ap[name] = nc.dram_tensor(name, list(shp), F32, kind="ExternalInput").ap()
    y = nc.dram_tensor("y", [T, D], F32, kind="ExternalOutput").ap()
    C = setup(nc)
    C.S.readonly |= set(shapes.keys())
    x = ap["x"]
    outs = []
    dn_phase(C, x, y, ap["norm_mix"][0:1, :], ap["ab_w_in"][0], ap["ab_conv_w"][0], ap["dn_a_log"][0:1, :],
             ap["dn_dt_bias"][0:1, :], ap["dn_norm_g"][0:1, :], ap["ab_w_out"][0])
    sb_phase(C, x, y, y, ap["norm_mix"][0:1, :], ap["ab_w_in"][0], ap["sb_g_q"][0], ap["sb_g_k"][0], ap["ab_w_out"][0])
    for layer in range(2):
        if layer == 1:
            gla_phase(C, y, y, ap["norm_mix"][1:2, :], ap["gla_w_in"][0], ap["gla_w_gk"][0], ap["gla_b_gk"][0:1, :],
                      ap["gla_norm_g"][0:1, :], ap["gla_w_out"][0])
        xa_phase(C, y, y, ap["mem"], ap["norm_xa"][layer:layer + 1, :], ap["norm_mem"][layer:layer + 1, :],
                 ap["xa_w_q"][layer], ap["xa_w_kv"][layer], ap["xa_w_o"][layer], ap["xa_g_q"][layer], ap["xa_g_k"][layer])
        outs = ffn_phase(C, y, y, ap["norm_ffn"][layer:layer + 1, :], ap["ffn_w_up"][layer], ap["ffn_conv_w"][layer],
                         ap["ffn_conv_b"][layer], ap["ffn_w_down"][layer])
    C.S.finalize(outs)
    return nc


def kernel(**inputs):
    x = np.ascontiguousarray(inputs["x"], dtype=np.float32)
    mem = np.ascontiguousarray(inputs["mem"], dtype=np.float32)
    B = x.shape[0]
    w = {k: np.ascontiguousarray(inputs[k], dtype=np.float32) for k in _WNAMES}
    shapes = {"x": x.shape[1:], "mem": mem.shape[1:]}
    for k in _WNAMES:
        shapes[k] = w[k].shape
    key = tuple(sorted((k, tuple(v)) for k, v in shapes.items()))
    nc = build_program(shapes)
    in_maps = []
    for b in range(B):
        m = {"x": x[b], "mem": mem[b]}
        m.update(w)
        in_maps.append(m)
    res = run_bass_kernel_spmd(nc, in_maps, core_ids=list(range(B)))
    return np.stack([np.asarray(r["y"], dtype=np.float32) for r in res.results], axis=0)
```

```python
import numpy as np
import concourse.bass as bass
import concourse.mybir as mybir

F32 = mybir.dt.float32
BF16 = mybir.dt.bfloat16
AF = mybir.ActivationFunctionType
ALU = mybir.AluOpType
AX = mybir.AxisListType

_DT_SIZE = {F32: 4, BF16: 2}


def _dtsize(dt):
    if dt in _DT_SIZE:
        return _DT_SIZE[dt]
    return mybir.dt.size(dt)


def region(ap):
    t = ap.tensor
    name = t.name
    es = _dtsize(ap.dtype)
    pairs = ap.ap
    off = int(ap.offset)
    sp = type(t).__name__
    if sp.startswith('DRam'):
        lo = off
        hi = off
        for st, cnt in pairs:
            if st >= 0:
                hi += st * (cnt - 1)
            else:
                lo += st * (cnt - 1)
        return (name, 0, 0, lo * es, (hi + 1) * es)
    if sp.startswith('PSum'):
        return (name, 0, 127, 0, 2048)
    pstep, pcnt = pairs[0]
    if pstep == 0:
        p0 = 0
        f0 = off
        pn = 1
        pstep = 1 << 40
    else:
        p0 = off // pstep
        f0 = off % pstep
        pn = pcnt
    lo = f0
    hi = f0
    for st, cnt in pairs[1:]:
        if st >= 0:
            hi += st * (cnt - 1)
        else:
            lo += st * (cnt - 1)
    return (name, p0, p0 + pn - 1, lo * es, (hi + 1) * es)


def _ovl(a, b):
    return not (a[2] < b[1] or b[2] < a[1] or a[4] <= b[3] or b[4] <= a[3])


def _covers(a, b):
    return a[1] <= b[1] and a[2] >= b[2] and a[3] <= b[3] and a[4] >= b[4]


class Sched:
    NSLOT = 8

    def __init__(self, nc):
        self.nc = nc
        self.engs = {'pe': nc.tensor, 'act': nc.scalar, 'dve': nc.vector,
                     'pool': nc.gpsimd, 'sp': nc.sync}
        self.ops = []
        self.hist = {}
        self.readonly = set()
        self.rank_mode = False
        self.dram_names = set()

    def sbuf(self, name, shape, dt):
        g = self.nc.sbuf_tensor(name, list(shape), dt)
        return g.__enter__()

    def psum(self, name, shape, dt):
        g = self.nc.psum_tensor(name, list(shape), dt)
        return g.__enter__()

    def _pages(self, reg):
        sh = 19 if reg[0] in self.dram_names else 12
        return range(reg[3] >> sh, ((reg[4] - 1) >> sh) + 1)

    def op(self, eng, fn, reads, writes, dma=False):
        idx = len(self.ops)
        deps = {}
        rr = [region(a) for a in reads if a.tensor.name not in self.readonly]
        ww = [region(a) for a in writes]
        for a in list(reads) + list(writes):
            if type(a.tensor).__name__.startswith('DRam'):
                self.dram_names.add(a.tensor.name)
        for r in rr:
            h = self.hist.get(r[0])
            if h is None:
                continue
            for pg in self._pages(r):
                e = h.get(pg)
                if e is None:
                    continue
                for (reg, oi) in e['w']:
                    if _ovl(reg, r):
                        deps[oi] = 'raw'
                if r[0].startswith('ps'):
                    for (reg, oi) in e['r']:
                        if oi not in deps and self.ops[oi]['eng'] != eng:
                            deps[oi] = 'rr'
        for w in ww:
            h = self.hist.setdefault(w[0], {})
            for pg in self._pages(w):
                e = h.get(pg)
                if e is None:
                    continue
                for (reg, oi) in e['w']:
                    if _ovl(reg, w) and oi not in deps:
                        deps[oi] = 'waw'
                for (reg, oi) in e['r']:
                    if _ovl(reg, w) and oi not in deps:
                        deps[oi] = 'war'
        for w in ww:
            h = self.hist[w[0]]
            for pg in self._pages(w):
                e = h.setdefault(pg, {'w': [], 'r': []})
                e['w'] = [(reg, oi) for (reg, oi) in e['w'] if not _covers(w, reg)]
                e['r'] = [(reg, oi) for (reg, oi) in e['r'] if not _covers(w, reg)]
                e['w'].append((w, idx))
        for r in rr:
            h = self.hist.setdefault(r[0], {})
            for pg in self._pages(r):
                e = h.setdefault(pg, {'w': [], 'r': []})
                if not dma:
                    keep = []
                    for (reg, oi) in e['r']:
                        if oi == idx or not (self.ops[oi]['eng'] == eng and not self.ops[oi]['dma']
                                             and _covers(r, reg)):
                            keep.append((reg, oi))
                        elif oi not in deps:
                            deps[oi] = 'ord'
                    e['r'] = keep
                e['r'].append((r, idx))
        fdeps = set()
        best = {}
        for oi, kind in deps.items():
            o = self.ops[oi]
            if (not o['dma']) and (not dma) and o['eng'] == eng and kind != 'raw':
                continue
            if o['dma']:
                fdeps.add(oi)
            else:
                if best.get(o['eng'], -1) < oi:
                    best[o['eng']] = oi
        for oi in best.values():
            fdeps.add(oi)
        nbytes = 0
        for w in ww:
            nbytes = max(nbytes, w[4] - w[3])
        for r in rr:
            nbytes = max(nbytes, r[4] - r[3])
        nel = 1
        for a in list(reads) + list(writes):
            k = 1
            for (_st, cnt) in a.ap[1:]:
                k *= cnt
            nel = max(nel, k)
        if eng == 'pe' and len(reads) >= 2:
            k = 1
            for (_st, cnt) in reads[1].ap[1:]:
                k *= cnt
            nel = k * (4 if _dtsize(reads[1].dtype) == 4 else 1)
        self.ops.append({'eng': eng, 'fn': fn, 'deps': fdeps, 'dma': dma, 'alldeps': dict(deps), 'nb': nbytes,
                         'nel': nel, 'cost': None, 'rk': self.rank_mode})
        return idx

    def dma(self, out, in_, q='sp', **kw):
        e = self.engs[q]
        return self.op(q, lambda: e.dma_start(out=out, in_=in_, **kw), [in_], [out], dma=True)

    def mm(self, out, lhsT, rhs, start=True, stop=True, **kw):
        t = self.nc.tensor
        return self.op('pe', lambda: t.matmul(out, lhsT, rhs, start=start, stop=stop, **kw),
                       [lhsT, rhs], [out])

    def transpose(self, out, in_, ident):
        t = self.nc.tensor
        return self.op('pe', lambda: t.transpose(out, in_, ident), [in_, ident], [out])

    def act(self, out, in_, func, bias=None, scale=None, accum_out=None, eng='act'):
        e = self.engs[eng]
        kw = {}
        reads = [in_]
        writes = [out]
        if bias is not None:
            kw['bias'] = bias
            if not isinstance(bias, (int, float)):
                reads.append(bias)
        if scale is not None:
            kw['scale'] = scale
            if not isinstance(scale, (int, float)):
                reads.append(scale)
        if accum_out is not None:
            kw['accum_out'] = accum_out
            writes.append(accum_out)
        return self.op(eng, lambda: e.activation(out, in_, func, **kw), reads, writes)

    def tt(self, out, in0, in1, op, eng='dve'):
        e = self.engs[eng]
        return self.op(eng, lambda: e.tensor_tensor(out, in0, in1, op), [in0, in1], [out])

    def ts(self, out, in0, s1, s2, op0, op1=None, accum_out=None, eng='dve'):
        e = self.engs[eng]
        reads = [in0]
        if not isinstance(s1, (int, float)):
            reads.append(s1)
        if s2 is not None and not isinstance(s2, (int, float)):
            reads.append(s2)
        writes = [out]
        kw = {}
        if op1 is not None:
            kw['op1'] = op1
        if accum_out is not None:
            kw['accum_out'] = accum_out
            writes.append(accum_out)
        return self.op(eng, lambda: e.tensor_scalar(out, in0, s1, s2, op0, **kw), reads, writes)

    def stt(self, out, in0, scalar, in1, op0, op1, eng='dve'):
        e = self.engs[eng]
        reads = [in0, in1]
        if not isinstance(scalar, (int, float)):
            reads.append(scalar)
        return self.op(eng, lambda: e.scalar_tensor_tensor(out, in0, scalar, in1, op0, op1),
                       reads, [out])

    def copy(self, out, in_, eng='dve'):
        e = self.engs[eng]
        if eng == 'act':
            return self.op(eng, lambda: e.copy(out, in_), [in_], [out])
        return self.op(eng, lambda: e.tensor_copy(out, in_), [in_], [out])

    def memset(self, ap, val, eng='dve'):
        e = self.engs[eng]
        return self.op(eng, lambda: e.memset(ap, val), [], [ap])

    def reduce(self, out, in_, op, axis=None, eng='dve'):
        e = self.engs[eng]
        ax = axis if axis is not None else AX.X
        return self.op(eng, lambda: e.tensor_reduce(out, in_, ax, op), [in_], [out])

    def scan(self, out, d0, d1, initial, op0, op1, eng='dve'):
        e = self.engs[eng]
        reads = [d0, d1]
        if not isinstance(initial, (int, float)):
            reads.append(initial)
        return self.op(eng, lambda: e.tensor_tensor_scan(out, d0, d1, initial, op0, op1),
                       reads, [out])

    def recip(self, out, in_):
        e = self.nc.vector
        return self.op('dve', lambda: e.reciprocal(out, in_), [in_], [out])

    def _cost(self, o):
        if o['cost'] is not None:
            return o['cost']
        nb = o['nb']
        nel = o['nel']
        e = o['eng']
        if o['dma']:
            return 0.06
        if e == 'pe':
            return 0.035 + nel / 2400.0
        if e == 'act':
            return 0.2 + nel / 1200.0
        if e == 'dve':
            return 0.07 + nel / 960.0
        if e == 'pool':
            return 0.15 + nel / 600.0
        return 0.05

    def schedule(self, window=None):
        import os
        if window is None:
            window = int(os.environ.get("SCHED_WINDOW", "64"))
        ops = self.ops
        n = len(ops)
        succ = [[] for _ in range(n)]
        nrem = [0] * n
        for i, o in enumerate(ops):
            ds = o['alldeps']
            nrem[i] = len(ds)
            for d in ds:
                succ[d].append(i)
        ready_t = [0.0] * n
        finish = [0.0] * n
        rank = [0.0] * n
        for i in range(n - 1, -1, -1):
            r = 0.0
            for sidx in succ[i]:
                if rank[sidx] > r:
                    r = rank[sidx]
            rank[i] = r + self._cost(ops[i]) + 0.1
        use_rank = os.environ.get("SCHED_RANK", "1") == "1"
        pend = {}
        for i, o in enumerate(ops):
            pend.setdefault(o['eng'], []).append(i)
        pos = {e: 0 for e in pend}
        done = [False] * n
        eng_t = {e: 0.0 for e in pend}
        order = []
        LAT = 0.12
        left = n
        while left:
            best = None
            for e, lst in pend.items():
                p = pos[e]
                while p < len(lst) and done[lst[p]]:
                    p += 1
                pos[e] = p
                cnt = 0
                q = p
                while q < len(lst) and cnt < window:
                    i = lst[q]
                    q += 1
                    if done[i]:
                        continue
                    cnt += 1
                    if nrem[i]:
                        continue
                    st = ready_t[i] if ready_t[i] > eng_t[e] else eng_t[e]
                    key = (round(st, 1), -rank[i], i) if (use_rank and ops[i]['rk']) else (st, 0.0, i)
                    if best is None or key < best[0]:
                        best = (key, e, i)
            assert best is not None, "scheduler deadlock"
            e, i = best[1], best[2]
            st = ready_t[i] if ready_t[i] > eng_t[e] else eng_t[e]
            o = ops[i]
            c = self._cost(o)
            if o['dma']:
                eng_t[e] = st + c
                fin = st + 2.0 + o['nb'] / 200000.0
            else:
                eng_t[e] = st + c
                fin = st + c
            finish[i] = fin
            done[i] = True
            left -= 1
            order.append(i)
            for sidx in succ[i]:
                nrem[sidx] -= 1
                t = fin + LAT
                if t > ready_t[sidx]:
                    ready_t[sidx] = t
        return order

    def finalize(self, final_wait_ops=(), reorder=True):
        nc = self.nc
        ops = self.ops
        order = self.schedule() if reorder else list(range(len(ops)))
        wait_deps = []
        needed = set()
        for idx, o in enumerate(ops):
            wd = []
            for d, kind in o['alldeps'].items():
                po = ops[d]
                if (not po['dma']) and (not o['dma']) and po['eng'] == o['eng'] and kind != 'raw':
                    continue
                wd.append(d)
                needed.add(d)
            wait_deps.append(wd)
        for d in final_wait_ops:
            needed.add(d)
        eng_sem = {}
        eng_cnt = {}
        dma_sems = {}
        dma_cnt = {}
        ticket = {}
        waited = {k: {} for k in self.engs}

        def esem(eng):
            if eng not in eng_sem:
                eng_sem[eng] = nc.alloc_semaphore(name='c_' + eng)
                eng_cnt[eng] = 0
            return eng_sem[eng]

        snap = {}
        nwait = [0, 0]

        def do_wait(eng, sem, val):
            w = waited[eng]
            if w.get(sem.num, 0) >= val:
                nwait[1] += 1
                return
            self.engs[eng].wait_ge(sem, val)
            nwait[0] += 1
            w[sem.num] = val
            k = snap.get((sem.num, val))
            if k:
                for sn, v in k.items():
                    if w.get(sn, 0) < v:
                        w[sn] = v

        for idx in order:
            o = ops[idx]
            eng = o['eng']
            wmax = {}
            for d in wait_deps[idx]:
                sem, val = ticket[d]
                if wmax.get(sem.num, (None, 0))[1] < val:
                    wmax[sem.num] = (sem, val)
            for k in sorted(wmax):
                do_wait(eng, wmax[k][0], wmax[k][1])
            if o['dma']:
                if eng not in dma_sems:
                    dma_sems[eng] = [nc.alloc_semaphore(name='d_%s_%d' % (eng, i))
                                     for i in range(self.NSLOT)]
                    dma_cnt[eng] = 0
                c = dma_cnt[eng]
                dma_cnt[eng] = c + 1
                sem = dma_sems[eng][c % self.NSLOT]
                rnd = c // self.NSLOT
                if rnd > 0:
                    do_wait(eng, sem, 16 * rnd)
                inst = o['fn']()
                inst.then_inc(sem, 16)
                ticket[idx] = (sem, 16 * (rnd + 1))
                snap[(sem.num, 16 * (rnd + 1))] = dict(waited[eng])
            else:
                inst = o['fn']()
                if idx in needed:
                    sem = esem(eng)
                    eng_cnt[eng] += 1
                    inst.then_inc(sem, 1)
                    ticket[idx] = (sem, eng_cnt[eng])
                    snap[(sem.num, eng_cnt[eng])] = dict(waited[eng])
        for d in final_wait_ops:
            sem, val = ticket[d]
            do_wait('sp', sem, val)
        self.stats = {'n_ops': len(ops), 'eng_cnt': dict(eng_cnt), 'dma_cnt': dict(dma_cnt), 'waits': nwait[0],
                      'waits_skipped': nwait[1]}
        return self.stats


class Arena:
    def __init__(self, S, nbytes, name='arena'):
        self.t = S.sbuf(name, [128, nbytes // 4], F32)
        self.n = nbytes
        self.top = 0

    def mark(self):
        return self.top

    def release(self, m):
        self.top = m

    def alloc(self, free_shape, dt):
        es = _dtsize(dt)
        n = 1
        for d in free_shape:
            n *= d
        nb = (n * es + 31) // 32 * 32
        off = self.top
        assert off + nb <= self.n, "arena overflow %d + %d > %d" % (off, nb, self.n)
        self.top = off + nb
        v = self.t[:, off // 4:(off + nb) // 4]
        if dt != F32:
            v = v.bitcast(dt)
        v = v[:, 0:n]
        if len(free_shape) == 2:
            v = v.rearrange("p (a b) -> p a b", a=free_shape[0])
        elif len(free_shape) == 3:
            v = v.rearrange("p (a b c) -> p a b c", a=free_shape[0], b=free_shape[1])
        return v


T = 4096
D = 1024
NT = T // 128
DFF = 2816
EPS = 1e-6


class Ctx:
    pass


def setup(nc, phases_debug=None):
    C = Ctx()
    S = Sched(nc)
    C.S = S
    C.nc = nc
    A = Arena(S, 200 * 1024)
    C.A = A
    C.ps = [S.psum("ps%d" % i, [128, 512], F32) for i in range(8)]
    C.psi = 0
    C.norm_i = 0
    C.psi6 = 0
    C.ident = A.alloc([128], F32)
    C.identb = A.alloc([128], BF16)
    C.ones = A.alloc([128], F32)
    C.epsb = A.alloc([1], F32)
    C.tril = A.alloc([128], F32)
    C.triu = A.alloc([128], F32)
    C.triu_s = A.alloc([128], F32)
    g = nc.gpsimd
    S.memset(C.ident, 0.0)
    S.op('pool', lambda: g.affine_select(out=C.ident, in_=C.ident, compare_op=ALU.not_equal, fill=1.0,
                                         base=0, pattern=[[-1, 128]], channel_multiplier=1),
         [C.ident], [C.ident])
    S.copy(C.identb, C.ident)
    S.memset(C.ones, 1.0)
    C.onesb = A.alloc([128], BF16)
    S.copy(C.onesb, C.ones)
    S.memset(C.epsb, EPS)
    S.memset(C.tril, 1.0)
    S.op('pool', lambda: g.affine_select(out=C.tril, in_=C.tril, compare_op=ALU.is_ge, fill=0.0,
                                         base=0, pattern=[[-1, 128]], channel_multiplier=1),
         [C.tril], [C.tril])
    S.memset(C.triu, 1.0)
    S.op('pool', lambda: g.affine_select(out=C.triu, in_=C.triu, compare_op=ALU.is_ge, fill=0.0,
                                         base=0, pattern=[[1, 128]], channel_multiplier=-1),
         [C.triu], [C.triu])
    S.memset(C.triu_s, 1.0)
    S.op('pool', lambda: g.affine_select(out=C.triu_s, in_=C.triu_s, compare_op=ALU.is_gt, fill=0.0,
                                         base=0, pattern=[[1, 128]], channel_multiplier=-1),
         [C.triu_s], [C.triu_s])
    C.base_mark = A.mark()
    C.cvt_i = 0
    return C


def next_ps6(C):
    p = C.ps[C.psi6 % 6]
    C.psi6 += 1
    return p


def next_ps(C):
    p = C.ps[C.psi % 7]
    C.psi += 1
    return p


def load_w_bf16(C, w2d, rows, cols, dst=None, stage=None):
    S, A = C.S, C.A
    kc_n = rows // 128
    if dst is None:
        dst = A.alloc([kc_n, cols], BF16)
    CH = 2048
    for kc in range(kc_n):
        for c0 in range(0, cols, CH):
            c1 = min(cols, c0 + CH)
            st = stage[C.cvt_i % len(stage)]
            S.dma(st[:, 0:c1 - c0], w2d[kc * 128:(kc + 1) * 128, c0:c1])
            eng = ('act', 'dve', 'act', 'dve', 'act', 'dve', 'pool')[C.cvt_i % 7]
            S.copy(dst[:, kc, c0:c1], st[:, 0:c1 - c0], eng=eng)
            C.cvt_i += 1
    return dst


def load_cols(C, row_ap, n, dst):
    S = C.S
    nj = n // 128
    src = row_ap.rearrange("(j p) -> p j", p=128)
    step = 16
    for j0 in range(0, nj, step):
        j1 = min(nj, j0 + step)
        S.dma(dst[:, j0:j1], src[:, j0:j1], allow_slow_non_contiguous=True)


def norm_T(C, xt, gb, hb, hT, col0, sq, ss, rstd):
    S = C.S
    if isinstance(hb, list):
        k = C.norm_i % len(hb)
        C.norm_i += 1
        hb, sq, ss, rstd = hb[k], sq[k], ss[k], rstd[k]
    S.act(sq, xt, AF.Square, accum_out=ss)
    S.act(rstd, ss, AF.Ln, scale=1.0 / D, bias=C.epsb)
    S.act(rstd, rstd, AF.Exp, scale=-0.5)
    S.stt(hb, xt, rstd, gb, ALU.mult, ALU.mult)
    pst = next_ps(C)
    pv = pst[:, :].bitcast(BF16)
    for c in range(8):
        S.transpose(pv[:, c * 128:(c + 1) * 128], hb[:, c * 128:(c + 1) * 128], C.identb)
    S.copy(hT[:, :, col0:col0 + 128], pv.rearrange("p (c t) -> p c t", c=8))


def ffn_phase(C, xsrc, xdst, g_norm, w_up, conv_w, conv_b, w_down):
    S, A = C.S, C.A
    m0 = A.mark()
    wup = A.alloc([22, 8, 256], BF16)
    wdn = A.alloc([22, D], BF16)
    gb = A.alloc([D], F32)
    S.dma(gb, g_norm.partition_broadcast(128))
    cw = A.alloc([3, 44], F32)
    cb = A.alloc([44], F32)
    for i in range(3):
        load_cols(C, conv_w[i, :], 2 * DFF, cw[:, i, :])
    load_cols(C, conv_b, 2 * DFF, cb)
    TB = 256
    NTT = TB // 128
    hTs = [A.alloc([8, TB + 2], BF16) for _ in range(2)]
    xts = [A.alloc([D], F32) for _ in range(2)]
    sq = A.alloc([D], BF16)
    hb = A.alloc([D], BF16)
    ss = A.alloc([1], F32)
    rstd = A.alloc([1], F32)
    Cc = [[A.alloc([TB], F32) for _ in range(2)] for _ in range(2)]
    sg = [A.alloc([TB], F32) for _ in range(2)]
    aj = [A.alloc([TB], BF16) for _ in range(4)]
    ysb = [A.alloc([D], F32) for _ in range(2)]
    acc = [[C.ps[4 + tt * 2 + nh] for nh in range(2)] for tt in range(NTT)]
    rot = {'i': 0}

    def rps():
        p = C.ps[rot['i'] % 4]
        rot['i'] += 1
        return p

    def norm_block(tb):
        hT = hTs[tb % 2]
        if tb == 0:
            S.memset(hT[:, :, 0:2], 0.0)
        else:
            S.copy(hT[:, :, 0:2], hTs[(tb - 1) % 2][:, :, TB:TB + 2], eng='pool')
        for tt in range(NTT):
            t = tb * NTT + tt
            xt = xts[t % 2]
            S.dma(xt, xsrc[t * 128:(t + 1) * 128, :])
            S.act(sq, xt, AF.Square, accum_out=ss)
            S.act(rstd, ss, AF.Ln, scale=1.0 / D, bias=C.epsb)
            S.act(rstd, rstd, AF.Exp, scale=-0.5)
            S.stt(hb, xt, rstd, gb, ALU.mult, ALU.mult)
            pst = rps()
            pv = pst[:, :].bitcast(BF16)
            for c in range(8):
                S.transpose(pv[:, c * 128:(c + 1) * 128], hb[:, c * 128:(c + 1) * 128], C.identb)
            S.copy(hT[:, :, 2 + tt * 128:2 + (tt + 1) * 128], pv.rearrange("p (c t) -> p c t", c=8))

    outs = []
    NB = T // TB
    norm_block(0)
    m1 = A.mark()
    stage = [A.alloc([2048], F32) for _ in range(3)]
    wup_src = w_up.rearrange("(kc p) c -> p kc c", p=128)
    for jp in range(11):
        for half in range(2):
            st = stage[C.cvt_i % 3]
            c0 = half * DFF + jp * 256
            S.dma(st.rearrange("p (a b) -> p a b", a=8), wup_src[:, :, c0:c0 + 256])
            eng = ('act', 'dve', 'act', 'dve', 'act', 'dve', 'pool')[C.cvt_i % 7]
            S.copy(wup[:, half * 11 + jp, :, :], st.rearrange("p (a b) -> p a b", a=8), eng=eng)
            C.cvt_i += 1
        st = stage[C.cvt_i % 3]
        S.dma(st.rearrange("p (a b) -> p a b", a=2),
              w_down[jp * 256:(jp + 1) * 256, :].rearrange("(a p) c -> p a c", p=128))
        eng = ('act', 'dve', 'act', 'dve', 'act', 'dve', 'pool')[C.cvt_i % 7]
        S.copy(wdn[:, 2 * jp:2 * jp + 2, :], st.rearrange("p (a b) -> p a b", a=2), eng=eng)
        C.cvt_i += 1
    A.release(m1)
    for tb in range(NB):
        hT = hTs[tb % 2]
        for j in range(22):
            cc = []
            for half in range(2):
                jj = half * 22 + j
                col0 = half * DFF + j * 128
                ps = rps()
                for kc in range(8):
                    S.mm(ps[:, 0:TB + 2], wup[:, half * 11 + j // 2, kc, (j % 2) * 128:(j % 2 + 1) * 128], hT[:, kc, :],
                         start=(kc == 0), stop=(kc == 7))
                c = Cc[j % 2][half]
                S.act(c, ps[:, 2:TB + 2], AF.Identity, scale=cw[:, 2, jj:jj + 1], bias=cb[:, jj:jj + 1])
                S.stt(c, ps[:, 1:TB + 1], cw[:, 1, jj:jj + 1], c, ALU.mult, ALU.add)
                S.stt(c, ps[:, 0:TB], cw[:, 0, jj:jj + 1], c, ALU.mult, ALU.add)
                cc.append(c)
            sg_ = sg[j % 2]
            a_ = aj[j % 4]
            S.act(sg_, cc[0], AF.Silu)
            S.tt(a_, sg_, cc[1], ALU.mult, eng='pool')

            def down(jd):
                ad = aj[jd % 4]
                for tt in range(NTT):
                    for nh in range(2):
                        S.mm(acc[tt][nh][:, :], ad[:, tt * 128:(tt + 1) * 128], wdn[:, jd, nh * 512:(nh + 1) * 512],
                             start=(jd == 0), stop=(jd == 21))
            if j >= 2:
                down(j - 2)
            if j == 21:
                down(20)
                down(21)
            if j == 10 and tb + 1 < NB:
                norm_block(tb + 1)
        for tt in range(NTT):
            t = tb * NTT + tt
            xt = xts[t % 2]
            S.dma(xt, xsrc[t * 128:(t + 1) * 128, :])
            y = ysb[t % 2]
            for nh in range(2):
                S.tt(y[:, nh * 512:(nh + 1) * 512], acc[tt][nh][:, :], xt[:, nh * 512:(nh + 1) * 512], ALU.add)
            outs.append(S.dma(xdst[t * 128:(t + 1) * 128, :], y, q='pool'))
    A.release(m0)
    return outs


def xa_phase(C, xsrc, xdst, mem, g_xa, g_mem, w_q, w_kv, w_o, g_q, g_k):
    S, A = C.S, C.A
    m0 = A.mark()
    wq = A.alloc([8, D], BF16)
    wkv = A.alloc([8, 2 * D], BF16)
    wo = A.alloc([8, D], BF16)
    m1 = A.mark()
    stage = [A.alloc([2048], F32) for _ in range(4)]
    load_w_bf16(C, w_q, D, D, dst=wq, stage=stage)
    load_w_bf16(C, w_kv, D, 2 * D, dst=wkv, stage=stage)
    load_w_bf16(C, w_o, D, D, dst=wo, stage=stage)
    A.release(m1)
    gb = A.alloc([D], F32)
    gmb = A.alloc([D], F32)
    S.dma(gb, g_xa.partition_broadcast(128))
    S.dma(gmb, g_mem.partition_broadcast(128))
    gq = A.alloc([2], F32)
    gk = A.alloc([2], F32)
    load_cols(C, g_q, 256, gq)
    load_cols(C, g_k, 256, gk)
    onesb = A.alloc([128], BF16)
    S.copy(onesb, C.ones)
    TB = 256
    xts = [A.alloc([D], F32) for _ in range(2)]
    sqj = [A.alloc([D], BF16) for _ in range(2)]
    hb = [A.alloc([D], BF16) for _ in range(2)]
    ss = [A.alloc([1], F32) for _ in range(2)]
    rstd1 = [A.alloc([1], F32) for _ in range(2)]
    memT = A.alloc([8, 256], BF16)
    kT = A.alloc([8, 256], BF16)
    V = A.alloc([2, D], BF16)
    hT = A.alloc([8, TB], BF16)
    qraw = [A.alloc([TB], F32) for _ in range(2)]
    sq = [A.alloc([TB], F32) for _ in range(2)]
    rstd = A.alloc([TB], F32)
    qn = [A.alloc([TB], BF16) for _ in range(2)]
    ex = [A.alloc([TB], BF16) for _ in range(2)]
    rinv = A.alloc([TB], F32)
    oT = A.alloc([8, TB], BF16)
    ysb = A.alloc([D], F32)
    for mt in range(2):
        xt = xts[mt]
        S.dma(xt, mem[mt * 128:(mt + 1) * 128, :])
        norm_T(C, xt, gmb, hb, memT, mt * 128, sqj, ss, rstd1)
    for h in range(4):
        for c in range(2):
            ch = 2 * h + c
            ps = next_ps(C)
            for kc in range(8):
                S.mm(ps[:, 0:256], wkv[:, kc, ch * 128:(ch + 1) * 128], memT[:, kc, :], start=(kc == 0), stop=(kc == 7))
            S.copy(qraw[c][:, 0:256], ps[:, 0:256], eng='act')
            S.act(sq[c][:, 0:256], ps[:, 0:256], AF.Square)
        ps = next_ps(C)
        for c in range(2):
            S.mm(ps[:, 0:256], C.ones, sq[c][:, 0:256], start=(c == 0), stop=(c == 1))
        S.act(rstd[:, 0:256], ps[:, 0:256], AF.Ln, scale=1.0 / 256, bias=C.epsb)
        S.act(rstd[:, 0:256], rstd[:, 0:256], AF.Exp, scale=-0.5)
        for c in range(2):
            S.stt(kT[:, 2 * h + c, :], qraw[c][:, 0:256], gk[:, c:c + 1], rstd[:, 0:256], ALU.mult, ALU.mult)
    for mt in range(2):
        for nh in range(2):
            ps = next_ps(C)
            for kc in range(8):
                S.mm(ps[:, :], memT[:, kc, mt * 128:(mt + 1) * 128], wkv[:, kc, D + nh * 512:D + (nh + 1) * 512],
                     start=(kc == 0), stop=(kc == 7))
            S.copy(V[:, mt, nh * 512:(nh + 1) * 512], ps[:, :], eng='act')
    outs = []
    NTT = TB // 128
    NB = T // TB
    hTs = [hT, A.alloc([8, TB], BF16)]
    oTs = [oT, A.alloc([8, TB], BF16)]
    xtb = [[A.alloc([D], F32) for _ in range(NTT)] for _ in range(3)]
    ysbs = [ysb, A.alloc([D], F32)]

    class HB:
        pass
    HBs = []
    for h in range(4):
        b_ = HB()
        b_.qraw = [A.alloc([TB], F32) for _ in range(2)]
        b_.sq = [A.alloc([TB], BF16) for _ in range(2)]
        b_.rstd = A.alloc([TB], F32)
        b_.qn = [A.alloc([TB], BF16) for _ in range(2)]
        b_.ex = [A.alloc([TB], BF16) for _ in range(2)]
        b_.rinv = A.alloc([TB], F32)
        HBs.append(b_)

    def front(tb):
        for tt in range(NTT):
            t = tb * NTT + tt
            xt = xtb[tb % 3][tt]
            S.dma(xt, xsrc[t * 128:(t + 1) * 128, :])
            norm_T(C, xt, gb, hb, hTs[tb % 2], tt * 128, sqj, ss, rstd1)
            yield

    def head(tb, h):
        B = HBs[h]
        hT_ = hTs[tb % 2]
        oT_ = oTs[tb % 2]
        for c in range(2):
            ch = 2 * h + c
            ps = next_ps(C)
            for kc in range(8):
                S.mm(ps[:, 0:TB], wq[:, kc, ch * 128:(ch + 1) * 128], hT_[:, kc, :], start=(kc == 0), stop=(kc == 7))
            S.copy(B.qraw[c], ps[:, 0:TB], eng='act')
            S.act(B.sq[c], ps[:, 0:TB], AF.Square)
            yield
        ps = next_ps(C)
        for c in range(2):
            S.mm(ps[:, 0:TB], C.onesb, B.sq[c], start=(c == 0), stop=(c == 1))
        S.act(B.rstd, ps[:, 0:TB], AF.Ln, scale=1.0 / 256, bias=C.epsb)
        yield
        S.act(B.rstd, B.rstd, AF.Exp, scale=-0.5)
        for c in range(2):
            S.stt(B.qn[c], B.qraw[c], gq[:, c:c + 1], B.rstd, ALU.mult, ALU.mult)
        yield
        for mc in range(2):
            ps = next_ps(C)
            for c in range(2):
                S.mm(ps[:, 0:TB], kT[:, 2 * h + c, mc * 128:(mc + 1) * 128], B.qn[c], start=(c == 0), stop=(c == 1))
            S.act(B.ex[mc], ps[:, 0:TB], AF.Exp, scale=1.0 / 16)
        yield
        ps = next_ps(C)
        for mc in range(2):
            S.mm(ps[:, 0:TB], onesb, B.ex[mc], start=(mc == 0), stop=(mc == 1))
        S.act(B.rinv, ps[:, 0:TB], AF.Ln)
        S.act(B.rinv, B.rinv, AF.Exp, scale=-1.0)
        yield
        for c in range(2):
            ps = next_ps(C)
            for mc in range(2):
                S.mm(ps[:, 0:TB], V[:, mc, h * 256 + c * 128:h * 256 + (c + 1) * 128], B.ex[mc],
                     start=(mc == 0), stop=(mc == 1))
            S.tt(oT_[:, 2 * h + c, :], ps[:, 0:TB], B.rinv, ALU.mult)
        yield

    def back(tb):
        oT_ = oTs[tb % 2]
        for tt in range(NTT):
            t = tb * NTT + tt
            xt = xtb[tb % 3][tt]
            y = ysbs[t % 2]
            for nh in range(2):
                ps = next_ps(C)
                for kc in range(8):
                    S.mm(ps[:, :], oT_[:, kc, tt * 128:(tt + 1) * 128], wo[:, kc, nh * 512:(nh + 1) * 512],
                         start=(kc == 0), stop=(kc == 7))
                S.tt(y[:, nh * 512:(nh + 1) * 512], ps[:, :], xt[:, nh * 512:(nh + 1) * 512], ALU.add)
            outs.append(S.dma(xdst[t * 128:(t + 1) * 128, :], y, q='pool'))
            yield

    def rr(gens):
        live = list(gens)
        while live:
            nxt = []
            for g_ in live:
                try:
                    next(g_)
                    nxt.append(g_)
                except StopIteration:
                    pass
            live = nxt

    rr([front(0)])
    for tb in range(NB):
        gens = [head(tb, h) for h in range(4)]
        if tb + 1 < NB:
            gens.append(front(tb + 1))
        if tb >= 1:
            gens.append(back(tb - 1))
        rr(gens)
    rr([back(NB - 1)])
    A.release(m0)
    return outs


def gla_phase(C, xsrc, xdst, g_norm, w_in, w_gk, b_gk, norm_g, w_out):
    S, A = C.S, C.A
    S.rank_mode = True
    m0 = A.mark()
    NIN = 3088
    win = A.alloc([8, NIN], BF16)
    wout = A.alloc([8, D], BF16)
    m1 = A.mark()
    stage = [A.alloc([2048], F32) for _ in range(4)]
    load_w_bf16(C, w_in, D, NIN, dst=win, stage=stage)
    load_w_bf16(C, w_out, D, D, dst=wout, stage=stage)
    A.release(m1)
    gb = A.alloc([D], F32)
    S.dma(gb, g_norm.partition_broadcast(128))
    bgk = A.alloc([512], F32)
    S.dma(bgk, b_gk.partition_broadcast(128))
    ngb = A.alloc([256], F32)
    S.dma(ngb, norm_g.partition_broadcast(128))
    wgk32 = A.alloc([512], F32)
    S.dma(wgk32[0:16, :], w_gk)
    wgk = A.alloc([512], BF16)
    S.copy(wgk[0:16, :], wgk32[0:16, :])
    xts = [A.alloc([D], F32) for _ in range(4)]
    sqj = [A.alloc([D], BF16) for _ in range(2)]
    hb = [A.alloc([D], BF16) for _ in range(2)]
    ss = [A.alloc([1], F32) for _ in range(2)]
    rstd1 = [A.alloc([1], F32) for _ in range(2)]
    hTs = [A.alloc([8, 128], BF16) for _ in range(2)]
    lrT = A.alloc([128], BF16)
    zb = A.alloc([512], F32)
    Lgs = [A.alloc([512], F32) for _ in range(2)]
    vtms = [A.alloc([D], BF16) for _ in range(3)]
    rss = [A.alloc([D], F32) for _ in range(4)]

    class HB:
        pass
    HBs = []
    for h in range(4):
        b_ = HB()
        b_.BC = A.alloc([128], F32)
        b_.nbm = A.alloc([1], F32)
        b_.eb = A.alloc([1], F32)
        b_.E = [A.alloc([128], F32) for _ in range(4)]
        for nm in ("qe", "ke", "kdT"):
            setattr(b_, nm, A.alloc([128], BF16))
        b_.hand = []
        for par in range(2):
            hd = HB()
            for nm in ("qg", "kd", "att"):
                setattr(hd, nm, A.alloc([128], BF16))
            hd.eb = A.alloc([1], F32)
            b_.hand.append(hd)
        HBs.append(b_)
    St = [A.alloc([256], F32) for _ in range(4)]
    Sb = [A.alloc([256], BF16) for _ in range(4)]
    o_alls = [A.alloc([D], F32) for _ in range(2)]
    oss = A.alloc([4], F32)
    orstd = A.alloc([4], F32)
    ob = A.alloc([D], BF16)
    oT = A.alloc([8, 128], BF16)
    ysbs = [A.alloc([D], F32) for _ in range(2)]
    for h in range(4):
        S.memset(St[h], 0.0)
        S.memset(Sb[h], 0.0)
    outs = []
    sc = 128.0 ** -0.5

    def front(t):
        xt = xts[t % 4]
        hT, Lg, vtm, rs = hTs[t % 2], Lgs[t % 2], vtms[t % 3], rss[t % 4]
        S.dma(xt, xsrc[t * 128:(t + 1) * 128, :])
        norm_T(C, xt, gb, hb, hT, 0, sqj, ss, rstd1)
        yield
        ps = next_ps(C)
        for kc in range(8):
            S.mm(ps[0:16, 0:128], win[:, kc, 3072:3088], hT[:, kc, :], start=(kc == 0), stop=(kc == 7))
        S.copy(lrT[0:16, :], ps[0:16, 0:128], eng='act')
        yield
        ps = next_ps(C)
        S.mm(ps[:, :], lrT[0:16, :], wgk[0:16, :])
        S.tt(zb, ps[:, :], bgk, ALU.add)
        S.act(zb, zb, AF.Exp, scale=-1.0)
        S.act(Lg, zb, AF.Ln, bias=C.ones[:, 0:1])
        yield
        for nh in range(2):
            ps = next_ps(C)
            for kc in range(8):
                S.mm(ps[:, :], hT[:, kc, :], win[:, kc, 1024 + nh * 512:1024 + (nh + 1) * 512], start=(kc == 0), stop=(kc == 7))
            S.copy(vtm[:, nh * 512:(nh + 1) * 512], ps[:, :], eng='act')
            yield
        for nh in range(2):
            ps = next_ps(C)
            for kc in range(8):
                S.mm(ps[:, :], hT[:, kc, :], win[:, kc, 2048 + nh * 512:2048 + (nh + 1) * 512], start=(kc == 0), stop=(kc == 7))
            S.act(rs[:, nh * 512:(nh + 1) * 512], ps[:, :], AF.Silu)
            yield

    def headA(t, h):
        B = HBs[h]
        Hd = B.hand[t % 2]
        hT, Lg = hTs[t % 2], Lgs[t % 2]
        E = B.E
        ps = next_ps(C)
        S.mm(ps[:, 0:128], Lg[:, h * 128:(h + 1) * 128], C.triu)
        S.act(B.BC, ps[:, 0:128], AF.Copy, scale=-1.0 / 16)
        yield
        S.act(B.nbm, B.BC[:, 64:65], AF.Copy, scale=-1.0)
        S.act(E[0], B.BC, AF.Exp, bias=B.nbm)
        S.act(E[1], B.BC, AF.Exp, scale=-1.0, bias=B.BC[:, 64:65])
        yield
        S.act(E[2], B.BC, AF.Exp)
        S.act(E[3], B.BC, AF.Exp, scale=-1.0, bias=B.BC[:, 127:128])
        S.act(Hd.eb, B.BC[:, 127:128], AF.Exp)
        yield
        psq = next_ps(C)
        for kc in range(8):
            S.mm(psq[:, 0:128], win[:, kc, h * 128:(h + 1) * 128], hT[:, kc, :], start=(kc == 0), stop=(kc == 7))
        S.stt(B.qe, psq[:, 0:128], sc, E[0], ALU.mult, ALU.mult)
        S.stt(Hd.qg, psq[:, 0:128], sc, E[2], ALU.mult, ALU.mult)
        yield
        psk = next_ps(C)
        for kc in range(8):
            S.mm(psk[:, 0:128], win[:, kc, 512 + h * 128:512 + (h + 1) * 128], hT[:, kc, :], start=(kc == 0), stop=(kc == 7))
        S.tt(B.ke, psk[:, 0:128], E[1], ALU.mult)
        S.tt(B.kdT, psk[:, 0:128], E[3], ALU.mult)
        yield
        pst = next_ps(C)
        ptv = pst[:, :].bitcast(BF16)
        S.transpose(ptv[:, 0:128], B.kdT, C.identb)
        S.copy(Hd.kd, ptv[:, 0:128], eng='act')
        psa = next_ps(C)
        S.mm(psa[:, 0:128], B.ke, B.qe)
        S.tt(Hd.att, psa[:, 0:128], C.triu, ALU.mult)
        yield

    def headB(t, h):
        B = HBs[h]
        Hd = B.hand[t % 2]
        vtm, o_all = vtms[t % 3], o_alls[t % 2]
        pso = next_ps(C)
        S.mm(pso[:, 0:256], Hd.att, vtm[:, h * 256:(h + 1) * 256], start=True, stop=False)
        S.mm(pso[:, 0:256], Hd.qg, Sb[h], start=False, stop=True)
        S.copy(o_all[:, h * 256:(h + 1) * 256], pso[:, 0:256], eng='act')
        pss = next_ps(C)
        S.mm(pss[:, 0:256], Hd.kd, vtm[:, h * 256:(h + 1) * 256])
        S.stt(St[h], St[h], Hd.eb, pss[:, 0:256], ALU.mult, ALU.add)
        S.copy(Sb[h], St[h], eng='pool')
        yield

    def back(t):
        xt = xts[t % 4]
        o_all, rs, ysb = o_alls[t % 2], rss[t % 4], ysbs[t % 2]
        for h in range(4):
            S.act(ob[:, h * 256:(h + 1) * 256], o_all[:, h * 256:(h + 1) * 256], AF.Square, accum_out=oss[:, h:h + 1])
        S.act(orstd, oss, AF.Ln, scale=1.0 / 256, bias=C.epsb)
        S.act(orstd, orstd, AF.Exp, scale=-0.5)
        yield
        for h in range(4):
            S.stt(o_all[:, h * 256:(h + 1) * 256], o_all[:, h * 256:(h + 1) * 256], orstd[:, h:h + 1], ngb, ALU.mult, ALU.mult)
        S.tt(ob, o_all, rs, ALU.mult)
        yield
        pst = next_ps(C)
        pv = pst[:, :].bitcast(BF16)
        for c in range(8):
            S.transpose(pv[:, c * 128:(c + 1) * 128], ob[:, c * 128:(c + 1) * 128], C.identb)
        S.copy(oT, pv.rearrange("p (c t) -> p c t", c=8))
        yield
        for nh in range(2):
            ps = next_ps(C)
            for kc in range(8):
                S.mm(ps[:, :], oT[:, kc, :], wout[:, kc, nh * 512:(nh + 1) * 512], start=(kc == 0), stop=(kc == 7))
            S.tt(ysb[:, nh * 512:(nh + 1) * 512], ps[:, :], xt[:, nh * 512:(nh + 1) * 512], ALU.add)
        outs.append(S.dma(xdst[t * 128:(t + 1) * 128, :], ysb, q='pool'))
        yield

    def rr(gens):
        live = list(gens)
        while live:
            nxt = []
            for g_ in live:
                try:
                    next(g_)
                    nxt.append(g_)
                except StopIteration:
                    pass
            live = nxt

    rr([front(0)])
    rr([headA(0, h) for h in range(4)] + [front(1)])
    for t in range(NT):
        gens = []
        if t >= 1:
            gens.append(back(t - 1))
        gens += [headB(t, h) for h in range(4)]
        if t + 1 < NT:
            gens += [headA(t + 1, h) for h in range(4)]
        if t + 2 < NT:
            gens.append(front(t + 2))
        rr(gens)
    rr([back(NT - 1)])
    S.rank_mode = False
    A.release(m0)
    return outs


def dn_phase(C, xsrc, xdst, g_norm, w_in, conv_w, a_log, dt_bias, norm_g, w_out):
    S, A = C.S, C.A
    S.rank_mode = True
    m0 = A.mark()
    NA = 2056
    win = A.alloc([8, NA], BF16)
    wout = A.alloc([4, D], BF16)
    m1 = A.mark()
    stage = [A.alloc([2048], F32) for _ in range(4)]
    load_w_bf16(C, w_in[:, 0:NA], D, NA, dst=win, stage=stage)
    load_w_bf16(C, w_out[0:512, :], 512, D, dst=wout, stage=stage)
    A.release(m1)
    gb = A.alloc([D], F32)
    S.dma(gb, g_norm.partition_broadcast(128))
    cw = A.alloc([4, 12], F32)
    for i in range(4):
        load_cols(C, conv_w[i, :], 1536, cw[:, i, :])
    ngb = A.alloc([128], F32)
    S.dma(ngb, norm_g.partition_broadcast(128))
    dtb = A.alloc([4], F32)
    S.dma(dtb, dt_bias.partition_broadcast(128))
    nea = A.alloc([4], F32)
    S.dma(nea, a_log.partition_broadcast(128))
    S.act(nea, nea, AF.Exp)
    S.ts(nea, nea, -1.0, None, ALU.mult)
    sqj = [A.alloc([D], BF16) for _ in range(2)]
    hb = [A.alloc([D], BF16) for _ in range(2)]
    ss = [A.alloc([1], F32) for _ in range(2)]
    rstd1 = [A.alloc([1], F32) for _ in range(2)]
    hT = A.alloc([8, 128], BF16)
    U = A.alloc([12, 131], F32)
    S.memset(U, 0.0)
    cvs = [A.alloc([12, 128], F32) for _ in range(2)]
    sqf = A.alloc([128], F32)
    rinv = A.alloc([128], F32)
    qns = [[A.alloc([128], F32) for _ in range(4)] for _ in range(3)]
    kns = [[A.alloc([128], F32) for _ in range(4)] for _ in range(2)]
    ab = A.alloc([8], F32)
    betas = [A.alloc([4], F32) for _ in range(2)]
    ggs = [A.alloc([4], F32) for _ in range(2)]
    zss = [A.alloc([512], F32) for _ in range(4)]
    sq8 = A.alloc([8, 128], F32)
    rinv8 = A.alloc([8, 128], F32)

    class HB:
        pass
    HBs = []
    for h in range(4):
        b_ = HB()
        for nm in ("ktm", "vtm", "tmp", "arg", "DT", "AT0", "Am", "ATm", "P0", "P1", "PT0", "PT1", "TT", "vb", "kbg",
                   "delta", "o1"):
            setattr(b_, nm, A.alloc([128], F32))
        for nm in ("gcc", "beg"):
            setattr(b_, nm, A.alloc([1], F32))
        b_.hand = []
        for par in range(2):
            hd = HB()
            for nm in ("usb", "wT", "qkm", "kdec"):
                setattr(hd, nm, A.alloc([128], F32))
            for nm in ("egc", "egl"):
                setattr(hd, nm, A.alloc([1], F32))
            b_.hand.append(hd)
        HBs.append(b_)
    St = [A.alloc([128], F32) for _ in range(4)]
    o_as = [A.alloc([512], F32) for _ in range(2)]
    oss = A.alloc([4], F32)
    orstd = A.alloc([4], F32)
    ob = A.alloc([512], BF16)
    oT = A.alloc([4, 128], BF16)
    ysbs = [A.alloc([D], F32) for _ in range(2)]
    xts = [A.alloc([D], F32) for _ in range(4)]
    for h in range(4):
        S.memset(St[h], 0.0)
    outs = []
    sc = 128.0 ** -0.5
    def front(t):
        xt = xts[t % 4]
        cv, qn, kn, beta, gg, zs = cvs[t % 2], qns[t % 3], kns[t % 2], betas[t % 2], ggs[t % 2], zss[t % 4]
        S.dma(xt, xsrc[t * 128:(t + 1) * 128, :])
        norm_T(C, xt, gb, hb, hT, 0, sqj, ss, rstd1)
        for c in range(12):
            ps = next_ps(C)
            for kc in range(8):
                S.mm(ps[:, 0:128], win[:, kc, c * 128:(c + 1) * 128], hT[:, kc, :], start=(kc == 0), stop=(kc == 7))
            S.copy(U[:, c, 3:131], ps[:, 0:128], eng='act')
            S.act(cv[:, c, :], ps[:, 0:128], AF.Copy, scale=cw[:, 3, c:c + 1])
            for i in range(3):
                S.stt(cv[:, c, :], U[:, c, i:i + 128], cw[:, i, c:c + 1], cv[:, c, :], ALU.mult, ALU.add)
            S.copy(U[:, c, 0:3], U[:, c, 128:131], eng='pool')
            yield
        S.act(cv, cv, AF.Silu)
        ps = next_ps(C)
        for kc in range(8):
            S.mm(ps[:, 0:8], hT[:, kc, :], win[:, kc, 2048:2056], start=(kc == 0), stop=(kc == 7))
        S.copy(ab, ps[:, 0:8], eng='act')
        ps = next_ps(C)
        for kc in range(8):
            S.mm(ps[:, :], hT[:, kc, :], win[:, kc, 1536:2048], start=(kc == 0), stop=(kc == 7))
        S.act(zs, ps[:, :], AF.Silu)
        yield
        S.act(beta, ab[:, 4:8], AF.Exp, scale=-1.0)
        S.ts(beta, beta, 1.0, None, ALU.add)
        S.recip(beta, beta)
        S.tt(gg, ab[:, 0:4], dtb, ALU.add)
        S.act(gg, gg, AF.Exp)
        S.act(gg, gg, AF.Ln, bias=C.ones[:, 0:1])
        S.tt(gg, gg, nea, ALU.mult)
        yield
        S.act(sq8, cv[:, 0:8, :], AF.Square)
        for half in range(2):
            ps = next_ps(C)
            S.mm(ps[:, :], C.ones, sq8[:, half * 4:(half + 1) * 4, :])
            S.act(rinv8[:, half * 4:(half + 1) * 4, :], ps[:, :].rearrange("p (a b) -> p a b", a=4), AF.Ln, bias=C.epsb)
        S.act(rinv8, rinv8, AF.Exp, scale=-0.5)
        yield
        for h in range(4):
            S.stt(qn[h], cv[:, h, :], sc, rinv8[:, h, :], ALU.mult, ALU.mult)
        for h in range(4):
            S.tt(kn[h], cv[:, 4 + h, :], rinv8[:, 4 + h, :], ALU.mult)

        yield

    def chainA(t, h):
        B = HBs[h]
        Hd = B.hand[t % 2]
        cv, qn, kn, beta, gg = cvs[t % 2], qns[t % 3], kns[t % 2], betas[t % 2], ggs[t % 2]
        ps = next_ps(C)
        S.transpose(ps[:, 0:128], kn[h], C.ident)
        S.copy(B.ktm, ps[:, 0:128], eng='act')
        ps = next_ps(C)
        S.transpose(ps[:, 0:128], cv[:, 8 + h, :], C.ident)
        S.copy(B.vtm, ps[:, 0:128], eng='act')
        yield
        S.ts(B.tmp, C.triu, gg[:, h:h + 1], None, ALU.mult)
        psg = next_ps(C)
        S.mm(psg[:, 0:128], C.ones, B.tmp)
        psc = next_ps(C)
        S.mm(psc[:, 0:1], C.triu, gg[:, h:h + 1])
        S.copy(B.gcc, psc[:, 0:1], eng='act')
        S.act(Hd.egl, psg[:, 127:128], AF.Exp)
        S.ts(B.arg, psg[:, 0:128], B.gcc, 0.0, ALU.subtract, ALU.min)
        yield
        S.act(B.DT, B.arg, AF.Exp)
        S.act(Hd.egc, B.gcc, AF.Exp)
        S.tt(B.beg, beta[:, h:h + 1], Hd.egc, ALU.mult)
        psk = next_ps(C)
        S.mm(psk[:, 0:128], kn[h], kn[h])
        S.tt(B.AT0, psk[:, 0:128], B.DT, ALU.mult)
        S.tt(B.AT0, B.AT0, C.triu_s, ALU.mult, eng='pool')
        yield
        ps = next_ps(C)
        S.transpose(ps[:, 0:128], B.AT0, C.ident)
        S.act(B.Am, ps[:, 0:128], AF.Copy, scale=beta[:, h:h + 1])
        yield
        ps = next_ps(C)
        S.transpose(ps[:, 0:128], B.Am, C.ident)
        S.copy(B.ATm, ps[:, 0:128], eng='act')
        S.tt(B.TT, C.ident, B.ATm, ALU.subtract, eng='pool')
        yield
        pk, pkt = B.ATm, B.Am
        Ps = (B.P0, B.P1)
        PTs = (B.PT0, B.PT1)
        for lvl in range(1, 7):
            npk, npkt = Ps[lvl % 2], PTs[lvl % 2]
            if lvl < 6:
                ps1 = next_ps(C)
                S.mm(ps1[:, 0:128], pkt, pk)
                S.copy(npk, ps1[:, 0:128], eng='act')
            ps2 = next_ps(C)
            S.mm(ps2[:, 0:128], pk, pkt)
            S.copy(npkt, ps2[:, 0:128], eng='act')
            pk, pkt = npk, npkt
            yield
            ps3 = next_ps(C)
            S.mm(ps3[:, 0:128], pkt, B.TT)
            S.tt(B.TT, ps3[:, 0:128], B.TT, ALU.add)
            yield
        S.ts(B.vb, B.vtm, beta[:, h:h + 1], None, ALU.mult)
        S.ts(B.kbg, B.ktm, B.beg, None, ALU.mult)
        psu = next_ps(C)
        S.mm(psu[:, 0:128], B.TT, B.vb)
        S.copy(Hd.usb, psu[:, 0:128], eng='act')
        psw = next_ps(C)
        S.mm(psw[:, 0:128], B.kbg, B.TT)
        S.copy(Hd.wT, psw[:, 0:128], eng='act')
        ps = next_ps(C)
        S.mm(ps[:, 0:128], kn[h], qn[h])
        S.tt(Hd.qkm, ps[:, 0:128], B.DT, ALU.mult)
        S.tt(Hd.qkm, Hd.qkm, C.triu, ALU.mult, eng='pool')
        S.ts(Hd.kdec, B.ktm, B.DT[:, 127:128], None, ALU.mult)
        yield

    def chainB(t, h):
        B = HBs[h]
        Hd = B.hand[t % 2]
        qn, o_a = qns[t % 3], o_as[t % 2]
        ps = next_ps(C)
        S.mm(ps[:, 0:128], Hd.wT, St[h])
        S.tt(B.delta, Hd.usb, ps[:, 0:128], ALU.subtract)
        psq = next_ps(C)
        S.mm(psq[:, 0:128], qn[h], St[h])
        S.act(B.o1, psq[:, 0:128], AF.Copy, scale=Hd.egc)
        yield
        ps = next_ps(C)
        S.mm(ps[:, 0:128], Hd.qkm, B.delta)
        S.tt(o_a[:, h * 128:(h + 1) * 128], ps[:, 0:128], B.o1, ALU.add)
        ps = next_ps(C)
        S.mm(ps[:, 0:128], Hd.kdec, B.delta)
        S.stt(St[h], St[h], Hd.egl, ps[:, 0:128], ALU.mult, ALU.add)
        yield


    def back(t):
        xt = xts[t % 4]
        o_a, zs, ysb = o_as[t % 2], zss[t % 4], ysbs[t % 2]
        for h in range(4):
            S.act(ob[:, h * 128:(h + 1) * 128], o_a[:, h * 128:(h + 1) * 128], AF.Square, accum_out=oss[:, h:h + 1])
        S.act(orstd, oss, AF.Ln, scale=1.0 / 128, bias=C.epsb)
        S.act(orstd, orstd, AF.Exp, scale=-0.5)
        for h in range(4):
            S.stt(o_a[:, h * 128:(h + 1) * 128], o_a[:, h * 128:(h + 1) * 128], orstd[:, h:h + 1], ngb, ALU.mult, ALU.mult)
        S.tt(ob, o_a, zs, ALU.mult)
        yield
        pst = next_ps(C)
        pv = pst[:, :].bitcast(BF16)
        for c in range(4):
            S.transpose(pv[:, c * 128:(c + 1) * 128], ob[:, c * 128:(c + 1) * 128], C.identb)
        S.copy(oT, pv[:, 0:512].rearrange("p (c t) -> p c t", c=4))
        yield
        for nh in range(2):
            ps = next_ps(C)
            for c in range(4):
                S.mm(ps[:, :], oT[:, c, :], wout[:, c, nh * 512:(nh + 1) * 512], start=(c == 0), stop=(c == 3))
            S.tt(ysb[:, nh * 512:(nh + 1) * 512], ps[:, :], xt[:, nh * 512:(nh + 1) * 512], ALU.add)
        outs.append(S.dma(xdst[t * 128:(t + 1) * 128, :], ysb, q='pool'))
        yield

    def run_round_robin(gens):
        live = list(gens)
        while live:
            nxt = []
            for g_ in live:
                try:
                    next(g_)
                    nxt.append(g_)
                except StopIteration:
                    pass
            live = nxt

    run_round_robin([front(0)])
    run_round_robin([chainA(0, h) for h in range(4)] + [front(1)])
    for t in range(NT):
        gens = []
        if t >= 1:
            gens.append(back(t - 1))
        gens += [chainB(t, h) for h in range(4)]
        if t + 1 < NT:
            gens += [chainA(t + 1, h) for h in range(4)]
        if t + 2 < NT:
            gens.append(front(t + 2))
        run_round_robin(gens)
    run_round_robin([back(NT - 1)])
    S.rank_mode = False
    A.release(m0)
    return outs


def sb_phase(C, xorig, xsrc, xdst, g_norm, w_in, g_q, g_k, w_out):
    S, A, nc = C.S, C.A, C.nc
    S.rank_mode = True
    m0 = A.mark()
    if not hasattr(C, 'sbq'):
        C.sbq = nc.dram_tensor("sb_q", [4, 128, T], BF16, kind="Internal").ap()
        C.sbk = nc.dram_tensor("sb_k", [4, 128, T], BF16, kind="Internal").ap()
        C.sbv = nc.dram_tensor("sb_v", [4, 128, NT, 128], BF16, kind="Internal").ap()
    wout = A.alloc([4, D], BF16)
    m1 = A.mark()
    gb = A.alloc([D], F32)
    S.dma(gb, g_norm.partition_broadcast(128))
    gq = A.alloc([1], F32)
    gk = A.alloc([1], F32)
    load_cols(C, g_q, 128, gq)
    load_cols(C, g_k, 128, gk)
    win = A.alloc([8, 1536], BF16)
    stage = [A.alloc([2048], F32) for _ in range(4)]
    load_w_bf16(C, w_in[:, 2056:3592], D, 1536, dst=win, stage=stage)
    load_w_bf16(C, w_out[512:1024, :], 512, D, dst=wout, stage=stage)
    TB = 512
    xts = [A.alloc([D], F32) for _ in range(2)]
    sqj = [A.alloc([D], BF16) for _ in range(2)]
    hb = [A.alloc([D], BF16) for _ in range(2)]
    ss = [A.alloc([1], F32) for _ in range(2)]
    rstd1 = [A.alloc([1], F32) for _ in range(2)]
    hT = A.alloc([8, TB], BF16)
    hTs = [hT, A.alloc([8, TB], BF16)]
    sq = [A.alloc([TB], BF16) for _ in range(8)]
    rstd = [A.alloc([TB], F32) for _ in range(8)]
    qk_o = [A.alloc([TB], BF16) for _ in range(8)]
    qg_b = [A.alloc([TB], F32) for _ in range(8)]
    v_o = [A.alloc([512], BF16) for _ in range(2)]
    sc = 128.0 ** -0.5
    NTT = TB // 128

    def p1_front(tb):
        for tt in range(NTT):
            t = tb * NTT + tt
            xt = xts[t % 2]
            S.dma(xt, xorig[t * 128:(t + 1) * 128, :])
            norm_T(C, xt, gb, hb, hTs[tb % 2], tt * 128, sqj, ss, rstd1)
            yield

    def p1_qk(tb, idx):
        h, isk = idx // 2, idx % 2
        c0, dst, gcol, scl = ((h * 128, C.sbq, gq, sc), (512 + h * 128, C.sbk, gk, 1.0))[isk]
        hT_ = hTs[tb % 2]
        sq_, rs_, o_ = sq[idx], rstd[idx], qk_o[idx]
        ps = next_ps(C)
        for kc in range(8):
            S.mm(ps[:, :], win[:, kc, c0:c0 + 128], hT_[:, kc, :], start=(kc == 0), stop=(kc == 7))
        S.act(sq_, ps[:, :], AF.Square)
        S.ts(qg_b[idx], ps[:, :], gcol, scl, ALU.mult, ALU.mult)
        yield
        ps2 = next_ps(C)
        S.mm(ps2[:, :], C.onesb, sq_)
        S.act(rs_, ps2[:, :], AF.Ln, scale=1.0 / 128, bias=C.epsb)
        yield
        S.act(rs_, rs_, AF.Exp, scale=-0.5)
        S.tt(o_, qg_b[idx], rs_, ALU.mult, eng='pool')
        S.dma(dst[h, :, tb * TB:(tb + 1) * TB], o_, q='sp')
        yield

    def p1_v(tb):
        hT_ = hTs[tb % 2]
        for tt in range(NTT):
            t = tb * NTT + tt
            ps = next_ps(C)
            for kc in range(8):
                S.mm(ps[:, :], hT_[:, kc, tt * 128:(tt + 1) * 128], win[:, kc, 1024:1536], start=(kc == 0), stop=(kc == 7))
            vo = v_o[t % 2]
            S.copy(vo, ps[:, :], eng='act')
            S.dma(C.sbv[:, :, t, :].rearrange("h p d -> p h d"), vo.rearrange("p (h d) -> p h d", h=4), q='sp')
            yield

    def rr1(gens):
        live = list(gens)
        while live:
            nxt = []
            for g_ in live:
                try:
                    next(g_)
                    nxt.append(g_)
                except StopIteration:
                    pass
            live = nxt

    NB1 = T // TB
    rr1([p1_front(0)])
    for tb in range(NB1):
        gens = [p1_qk(tb, idx) for idx in range(8)] + [p1_v(tb)]
        if tb + 1 < NB1:
            gens.append(p1_front(tb + 1))
        rr1(gens)
    A.release(m1)
    o_all = A.alloc([NT, 512], BF16)
    qT = [A.alloc([T], BF16) for _ in range(1)]
    kT = [A.alloc([T], BF16) for _ in range(1)]
    V = [A.alloc([NT, 128], BF16) for _ in range(1)]
    sp = [A.alloc([T], F32) for _ in range(2)]
    w = [A.alloc([T], F32) for _ in range(2)]
    Pc = A.alloc([T], F32)
    att = [A.alloc([T], BF16) for _ in range(2)]
    attT = [A.alloc([512], BF16) for _ in range(3)]
    ntot = [A.alloc([1], F32) for _ in range(2)]
    mask_s = A.alloc([128], F32)
    S.tt(mask_s, C.tril, C.ident, ALU.subtract)
    mask_b = A.alloc([128], BF16)
    S.copy(mask_b, mask_s)
    V2 = [V[0], A.alloc([NT, 128], BF16)]
    its = [(h, i) for h in range(4) for i in range(NT)]
    state = {'ai': 0}

    pss = {}

    def stage_a1(n):
        h, i = its[n]
        b = n % 2
        L = 128 * (i + 1)
        if i == 0:
            S.dma(qT[0], C.sbq[h])
            S.dma(kT[0], C.sbk[h])
            S.dma(V2[h % 2], C.sbv[h])
        sp_, w_ = sp[b], w[b]
        pss[n] = []
        for ci, k0 in enumerate(range(0, L, 512)):
            k1 = min(L, k0 + 512)
            if ci >= 5:
                sub_chunk(n, ci - 5)
            ps = next_ps6(C)
            pss[n].append((ps, k0, k1, False))
            S.mm(ps[:, 0:k1 - k0], qT[0][:, i * 128:(i + 1) * 128], kT[0][:, k0:k1])
            S.act(sp_[:, k0:k1], ps[:, 0:k1 - k0], AF.Exp)
            S.act(sp_[:, k0:k1], sp_[:, k0:k1], AF.Ln, bias=C.ones[:, 0:1])

    def sub_chunk(n, ci):
        b = n % 2
        ps, k0, k1, done = pss[n][ci]
        if done:
            return
        S.tt(w[b][:, k0:k1], ps[:, 0:k1 - k0], sp[b][:, k0:k1], ALU.subtract)
        pss[n][ci] = (ps, k0, k1, True)

    def stage_a2(n):
        h, i = its[n]
        b = n % 2
        L = 128 * (i + 1)
        sp_ = sp[b]
        for ci in range(len(pss[n])):
            sub_chunk(n, ci)
        del pss[n]
        S.tt(sp_[:, L - 128:L], sp_[:, L - 128:L], mask_s, ALU.mult, eng='pool')

    def stage_b1(n):
        h, i = its[n]
        b = n % 2
        L = 128 * (i + 1)
        sp_, w_, nt_ = sp[b], w[b], ntot[b]
        S.scan(Pc[:, 0:L], sp_[:, 0:L], sp_[:, 0:L], 0.0, ALU.add, ALU.max)
        S.ts(nt_, Pc[:, L - 1:L], -1.0, None, ALU.mult)
        S.tt(Pc[:, 0:L], Pc[:, 0:L], w_[:, 0:L], ALU.add, eng=('dve' if n % 3 == 0 else 'pool'))

    def stage_b2(n):
        h, i = its[n]
        b = n % 2
        L = 128 * (i + 1)
        att_, nt_ = att[b], ntot[b]
        S.act(att_[:, 0:L], Pc[:, 0:L], AF.Exp, bias=nt_)
        S.tt(att_[:, L - 128:L], att_[:, L - 128:L], mask_b, ALU.mult, eng='pool')

    def stage_c(n):
        h, i = its[n]
        b = n % 2
        att_ = att[b]
        vh = V2[h % 2]
        pso = C.ps[6 + b]
        nkt = i + 1
        groups = [(g0, min(nkt, g0 + 4)) for g0 in range(0, nkt, 4)]
        ats = {}

        def tr(gi):
            g0, g1 = groups[gi]
            pst = next_ps6(C)
            ptv = pst[:, :].bitcast(BF16)
            for kt in range(g0, g1):
                S.transpose(ptv[:, (kt - g0) * 128:(kt - g0 + 1) * 128], att_[:, kt * 128:(kt + 1) * 128], C.identb)
            at = attT[state['ai'] % 3]
            state['ai'] += 1
            S.copy(at[:, 0:(g1 - g0) * 128], ptv[:, 0:(g1 - g0) * 128], eng=('act' if state['ai'] % 3 == 0 else 'dve'))
            ats[gi] = at

        def av(gi):
            g0, g1 = groups[gi]
            at = ats[gi]
            for kt in range(g0, g1):
                S.mm(pso[:, 0:128], at[:, (kt - g0) * 128:(kt - g0 + 1) * 128], vh[:, kt, :],
                     start=(kt == 0), stop=(kt == nkt - 1))

        tr(0)
        for gi in range(len(groups)):
            if gi + 1 < len(groups):
                tr(gi + 1)
            av(gi)
        S.copy(o_all[:, i, h * 128:(h + 1) * 128], pso[:, 0:128], eng='act')

    NI = len(its)
    for n in range(NI + 2):
        if n < NI:
            stage_a1(n)
        if 0 <= n - 1 < NI:
            stage_b1(n - 1)
        if n < NI:
            stage_a2(n)
        if 0 <= n - 2 < NI:
            stage_c(n - 2)
        if 0 <= n - 1 < NI:
            stage_b2(n - 1)
    oT = [A.alloc([4, 128], BF16) for _ in range(2)]
    xts = [A.alloc([D], F32) for _ in range(2)]
    ysb = [A.alloc([D], F32) for _ in range(2)]
    outs = []
    for i in range(NT):
        xt = xts[i % 2]
        S.dma(xt, xsrc[i * 128:(i + 1) * 128, :])
        pst = next_ps(C)
        pv = pst[:, :].bitcast(BF16)
        for c in range(4):
            S.transpose(pv[:, c * 128:(c + 1) * 128], o_all[:, i, c * 128:(c + 1) * 128], C.identb)
        S.copy(oT[i % 2], pv[:, 0:512].rearrange("p (c t) -> p c t", c=4))
        y = ysb[i % 2]
        for nh in range(2):
            ps = next_ps(C)
            for c in range(4):
                S.mm(ps[:, :], oT[i % 2][:, c, :], wout[:, c, nh * 512:(nh + 1) * 512], start=(c == 0), stop=(c == 3))
            S.tt(y[:, nh * 512:(nh + 1) * 512], ps[:, :], xt[:, nh * 512:(nh + 1) * 512], ALU.add)
        outs.append(S.dma(xdst[i * 128:(i + 1) * 128, :], y, q='pool'))
    S.rank_mode = False
    A.release(m0)
    return outs


from concourse.bass_utils import run_bass_kernel_spmd

_NC_CACHE = {}

_WNAMES = ["norm_mix", "norm_xa", "norm_mem", "norm_ffn", "xa_w_q", "xa_w_kv", "xa_w_o", "xa_g_q", "xa_g_k",
           "ffn_w_up", "ffn_conv_w", "ffn_conv_b", "ffn_w_down", "ab_w_in", "ab_conv_w", "dn_a_log", "dn_dt_bias",
           "dn_norm_g", "sb_g_q", "sb_g_k", "ab_w_out", "gla_w_in", "gla_w_gk", "gla_b_gk", "gla_norm_g", "gla_w_out"]


def build_program(shapes):
    nc = bass.Bass("TRN2", target_bir_lowering=False)
    ap = {}
    for name, shp in shapes.items():
        ap[name] = nc.dram_tensor(name, list(shp), F32, kind="ExternalInput").ap()
    y = nc.dram_tensor("y", [T, D], F32, kind="ExternalOutput").ap()
    C = setup(nc)
    C.S.readonly |= set(shapes.keys())
    x = ap["x"]
    outs = []
    dn_phase(C, x, y, ap["norm_mix"][0:1, :], ap["ab_w_in"][0], ap["ab_conv_w"][0], ap["dn_a_log"][0:1, :],
             ap["dn_dt_bias"][0:1, :], ap["dn_norm_g"][0:1, :], ap["ab_w_out"][0])
    sb_phase(C, x, y, y, ap["norm_mix"][0:1, :], ap["ab_w_in"][0], ap["sb_g_q"][0], ap["sb_g_k"][0], ap["ab_w_out"][0])
    for layer in range(2):
        if layer == 1:
            gla_phase(C, y, y, ap["norm_mix"][1:2, :], ap["gla_w_in"][0], ap["gla_w_gk"][0], ap["gla_b_gk"][0:1, :],
                      ap["gla_norm_g"][0:1, :], ap["gla_w_out"][0])
        xa_phase(C, y, y, ap["mem"], ap["norm_xa"][layer:layer + 1, :], ap["norm_mem"][layer:layer + 1, :],
                 ap["xa_w_q"][layer], ap["xa_w_kv"][layer], ap["xa_w_o"][layer], ap["xa_g_q"][layer], ap["xa_g_k"][layer])
        outs = ffn_phase(C, y, y, ap["norm_ffn"][layer:layer + 1, :], ap["ffn_w_up"][layer], ap["ffn_conv_w"][layer],
                         ap["ffn_conv_b"][layer], ap["ffn_w_down"][layer])
    C.S.finalize(outs)
    return nc


def kernel(**inputs):
    x = np.ascontiguousarray(inputs["x"], dtype=np.float32)
    mem = np.ascontiguousarray(inputs["mem"], dtype=np.float32)
    B = x.shape[0]
    w = {k: np.ascontiguousarray(inputs[k], dtype=np.float32) for k in _WNAMES}
    shapes = {"x": x.shape[1:], "mem": mem.shape[1:]}
    for k in _WNAMES:
        shapes[k] = w[k].shape
    key = tuple(sorted((k, tuple(v)) for k, v in shapes.items()))
    nc = build_program(shapes)
    in_maps = []
    for b in range(B):
        m = {"x": x[b], "mem": mem[b]}
        m.update(w)
        in_maps.append(m)
    res = run_bass_kernel_spmd(nc, in_maps, core_ids=list(range(B)))
    return np.stack([np.asarray(r["y"], dtype=np.float32) for r in res.results], axis=0)
```

```python
import numpy as np
import concourse.bass as bass
import concourse.mybir as mybir

F32 = mybir.dt.float32
BF16 = mybir.dt.bfloat16
AF = mybir.ActivationFunctionType
ALU = mybir.AluOpType
AX = mybir.AxisListType

_DT_SIZE = {F32: 4, BF16: 2}


def _dtsize(dt):
    if dt in _DT_SIZE:
        return _DT_SIZE[dt]
    return mybir.dt.size(dt)


def region(ap):
    t = ap.tensor
    name = t.name
    es = _dtsize(ap.dtype)
    pairs = ap.ap
    off = int(ap.offset)
    sp = type(t).__name__
    if sp.startswith('DRam'):
        lo = off
        hi = off
        for st, cnt in pairs:
            if st >= 0:
                hi += st * (cnt - 1)
            else:
                lo += st * (cnt - 1)
        return (name, 0, 0, lo * es, (hi + 1) * es)
    if sp.startswith('PSum'):
        return (name, 0, 127, 0, 2048)
    pstep, pcnt = pairs[0]
    if pstep == 0:
        p0 = 0
        f0 = off
        pn = 1
        pstep = 1 << 40
    else:
        p0 = off // pstep
        f0 = off % pstep
        pn = pcnt
    lo = f0
    hi = f0
    for st, cnt in pairs[1:]:
        if st >= 0:
            hi += st * (cnt - 1)
        else:
            lo += st * (cnt - 1)
    return (name, p0, p0 + pn - 1, lo * es, (hi + 1) * es)


def _ovl(a, b):
    return not (a[2] < b[1] or b[2] < a[1] or a[4] <= b[3] or b[4] <= a[3])


def _covers(a, b):
    return a[1] <= b[1] and a[2] >= b[2] and a[3] <= b[3] and a[4] >= b[4]


class Sched:
    NSLOT = 8

    def __init__(self, nc):
        self.nc = nc
        self.engs = {'pe': nc.tensor, 'act': nc.scalar, 'dve': nc.vector,
                     'pool': nc.gpsimd, 'sp': nc.sync}
        self.ops = []
        self.hist = {}
        self.readonly = set()
        self.rank_mode = False
        self.dram_names = set()

    def sbuf(self, name, shape, dt):
        g = self.nc.sbuf_tensor(name, list(shape), dt)
        return g.__enter__()

    def psum(self, name, shape, dt):
        g = self.nc.psum_tensor(name, list(shape), dt)
        return g.__enter__()

    def _pages(self, reg):
        sh = 19 if reg[0] in self.dram_names else 12
        return range(reg[3] >> sh, ((reg[4] - 1) >> sh) + 1)

    def op(self, eng, fn, reads, writes, dma=False):
        idx = len(self.ops)
        deps = {}
        rr = [region(a) for a in reads if a.tensor.name not in self.readonly]
        ww = [region(a) for a in writes]
        for a in list(reads) + list(writes):
            if type(a.tensor).__name__.startswith('DRam'):
                self.dram_names.add(a.tensor.name)
        for r in rr:
            h = self.hist.get(r[0])
            if h is None:
                continue
            for pg in self._pages(r):
                e = h.get(pg)
                if e is None:
                    continue
                for (reg, oi) in e['w']:
                    if _ovl(reg, r):
                        deps[oi] = 'raw'
                if r[0].startswith('ps'):
                    for (reg, oi) in e['r']:
                        if oi not in deps and self.ops[oi]['eng'] != eng:
                            deps[oi] = 'rr'
        for w in ww:
            h = self.hist.setdefault(w[0], {})
            for pg in self._pages(w):
                e = h.get(pg)
                if e is None:
                    continue
                for (reg, oi) in e['w']:
                    if _ovl(reg, w) and oi not in deps:
                        deps[oi] = 'waw'
                for (reg, oi) in e['r']:
                    if _ovl(reg, w) and oi not in deps:
                        deps[oi] = 'war'
        for w in ww:
            h = self.hist[w[0]]
            for pg in self._pages(w):
                e = h.setdefault(pg, {'w': [], 'r': []})
                e['w'] = [(reg, oi) for (reg, oi) in e['w'] if not _covers(w, reg)]
                e['r'] = [(reg, oi) for (reg, oi) in e['r'] if not _covers(w, reg)]
                e['w'].append((w, idx))
        for r in rr:
            h = self.hist.setdefault(r[0], {})
            for pg in self._pages(r):
                e = h.setdefault(pg, {'w': [], 'r': []})
                if not dma:
                    keep = []
                    for (reg, oi) in e['r']:
                        if oi == idx or not (self.ops[oi]['eng'] == eng and not self.ops[oi]['dma']
                                             and _covers(r, reg)):
                            keep.append((reg, oi))
                        elif oi not in deps:
                            deps[oi] = 'ord'
                    e['r'] = keep
                e['r'].append((r, idx))
        fdeps = set()
        best = {}
        for oi, kind in deps.items():
            o = self.ops[oi]
            if (not o['dma']) and (not dma) and o['eng'] == eng and kind != 'raw':
                continue
            if o['dma']:
                fdeps.add(oi)
            else:
                if best.get(o['eng'], -1) < oi:
                    best[o['eng']] = oi
        for oi in best.values():
            fdeps.add(oi)
        nbytes = 0
        for w in ww:
            nbytes = max(nbytes, w[4] - w[3])
        for r in rr:
            nbytes = max(nbytes, r[4] - r[3])
        nel = 1
        for a in list(reads) + list(writes):
            k = 1
            for (_st, cnt) in a.ap[1:]:
                k *= cnt
            nel = max(nel, k)
        if eng == 'pe' and len(reads) >= 2:
            k = 1
            for (_st, cnt) in reads[1].ap[1:]:
                k *= cnt
            nel = k * (4 if _dtsize(reads[1].dtype) == 4 else 1)
        self.ops.append({'eng': eng, 'fn': fn, 'deps': fdeps, 'dma': dma, 'alldeps': dict(deps), 'nb': nbytes,
                         'nel': nel, 'cost': None, 'rk': self.rank_mode})
        return idx

    def dma(self, out, in_, q='sp', **kw):
        e = self.engs[q]
        return self.op(q, lambda: e.dma_start(out=out, in_=in_, **kw), [in_], [out], dma=True)

    def mm(self, out, lhsT, rhs, start=True, stop=True, **kw):
        t = self.nc.tensor
        return self.op('pe', lambda: t.matmul(out, lhsT, rhs, start=start, stop=stop, **kw),
                       [lhsT, rhs], [out])

    def transpose(self, out, in_, ident):
        t = self.nc.tensor
        return self.op('pe', lambda: t.transpose(out, in_, ident), [in_, ident], [out])

    def act(self, out, in_, func, bias=None, scale=None, accum_out=None, eng='act'):
        e = self.engs[eng]
        kw = {}
        reads = [in_]
        writes = [out]
        if bias is not None:
            kw['bias'] = bias
            if not isinstance(bias, (int, float)):
                reads.append(bias)
        if scale is not None:
            kw['scale'] = scale
            if not isinstance(scale, (int, float)):
                reads.append(scale)
        if accum_out is not None:
            kw['accum_out'] = accum_out
            writes.append(accum_out)
        return self.op(eng, lambda: e.activation(out, in_, func, **kw), reads, writes)

    def tt(self, out, in0, in1, op, eng='dve'):
        e = self.engs[eng]
        return self.op(eng, lambda: e.tensor_tensor(out, in0, in1, op), [in0, in1], [out])

    def ts(self, out, in0, s1, s2, op0, op1=None, accum_out=None, eng='dve'):
        e = self.engs[eng]
        reads = [in0]
        if not isinstance(s1, (int, float)):
            reads.append(s1)
        if s2 is not None and not isinstance(s2, (int, float)):
            reads.append(s2)
        writes = [out]
        kw = {}
        if op1 is not None:
            kw['op1'] = op1
        if accum_out is not None:
            kw['accum_out'] = accum_out
            writes.append(accum_out)
        return self.op(eng, lambda: e.tensor_scalar(out, in0, s1, s2, op0, **kw), reads, writes)

    def stt(self, out, in0, scalar, in1, op0, op1, eng='dve'):
        e = self.engs[eng]
        reads = [in0, in1]
        if not isinstance(scalar, (int, float)):
            reads.append(scalar)
        return self.op(eng, lambda: e.scalar_tensor_tensor(out, in0, scalar, in1, op0, op1),
                       reads, [out])

    def copy(self, out, in_, eng='dve'):
        e = self.engs[eng]
        if eng == 'act':
            return self.op(eng, lambda: e.copy(out, in_), [in_], [out])
        return self.op(eng, lambda: e.tensor_copy(out, in_), [in_], [out])

    def memset(self, ap, val, eng='dve'):
        e = self.engs[eng]
        return self.op(eng, lambda: e.memset(ap, val), [], [ap])

    def reduce(self, out, in_, op, axis=None, eng='dve'):
        e = self.engs[eng]
        ax = axis if axis is not None else AX.X
        return self.op(eng, lambda: e.tensor_reduce(out, in_, ax, op), [in_], [out])

    def scan(self, out, d0, d1, initial, op0, op1, eng='dve'):
        e = self.engs[eng]
        reads = [d0, d1]
        if not isinstance(initial, (int, float)):
            reads.append(initial)
        return self.op(eng, lambda: e.tensor_tensor_scan(out, d0, d1, initial, op0, op1),
                       reads, [out])

    def recip(self, out, in_):
        e = self.nc.vector
        return self.op('dve', lambda: e.reciprocal(out, in_), [in_], [out])

    def _cost(self, o):
        if o['cost'] is not None:
            return o['cost']
        nb = o['nb']
        nel = o['nel']
        e = o['eng']
        if o['dma']:
            return 0.06
        if e == 'pe':
            return 0.035 + nel / 2400.0
        if e == 'act':
            return 0.2 + nel / 1200.0
        if e == 'dve':
            return 0.07 + nel / 960.0
        if e == 'pool':
            return 0.15 + nel / 600.0
        return 0.05

    def schedule(self, window=None):
        import os
        if window is None:
            window = int(os.environ.get("SCHED_WINDOW", "64"))
        ops = self.ops
        n = len(ops)
        succ = [[] for _ in range(n)]
        nrem = [0] * n
        for i, o in enumerate(ops):
            ds = o['alldeps']
            nrem[i] = len(ds)
            for d in ds:
                succ[d].append(i)
        ready_t = [0.0] * n
        finish = [0.0] * n
        rank = [0.0] * n
        for i in range(n - 1, -1, -1):
            r = 0.0
            for sidx in succ[i]:
                if rank[sidx] > r:
                    r = rank[sidx]
            rank[i] = r + self._cost(ops[i]) + 0.1
        use_rank = os.environ.get("SCHED_RANK", "1") == "1"
        pend = {}
        for i, o in enumerate(ops):
            pend.setdefault(o['eng'], []).append(i)
        pos = {e: 0 for e in pend}
        done = [False] * n
        eng_t = {e: 0.0 for e in pend}
        order = []
        LAT = 0.12
        left = n
        while left:
            best = None
            for e, lst in pend.items():
                p = pos[e]
                while p < len(lst) and done[lst[p]]:
                    p += 1
                pos[e] = p
                cnt = 0
                q = p
                while q < len(lst) and cnt < window:
                    i = lst[q]
                    q += 1
                    if done[i]:
                        continue
                    cnt += 1
                    if nrem[i]:
                        continue
                    st = ready_t[i] if ready_t[i] > eng_t[e] else eng_t[e]
                    key = (round(st, 1), -rank[i], i) if (use_rank and ops[i]['rk']) else (st, 0.0, i)
                    if best is None or key < best[0]:
                        best = (key, e, i)
            assert best is not None, "scheduler deadlock"
            e, i = best[1], best[2]
            st = ready_t[i] if ready_t[i] > eng_t[e] else eng_t[e]
            o = ops[i]
            c = self._cost(o)
            if o['dma']:
                eng_t[e] = st + c
                fin = st + 2.0 + o['nb'] / 200000.0
            else:
                eng_t[e] = st + c
                fin = st + c
            finish[i] = fin
            done[i] = True
            left -= 1
            order.append(i)
            for sidx in succ[i]:
                nrem[sidx] -= 1
                t = fin + LAT
                if t > ready_t[sidx]:
                    ready_t[sidx] = t
        return order

    def finalize(self, final_wait_ops=(), reorder=True):
        nc = self.nc
        ops = self.ops
        order = self.schedule() if reorder else list(range(len(ops)))
        wait_deps = []
        needed = set()
        for idx, o in enumerate(ops):
            wd = []
            for d, kind in o['alldeps'].items():
                po = ops[d]
                if (not po['dma']) and (not o['dma']) and po['eng'] == o['eng'] and kind != 'raw':
                    continue
                wd.append(d)
                needed.add(d)
            wait_deps.append(wd)
        for d in final_wait_ops:
            needed.add(d)
        eng_sem = {}
        eng_cnt = {}
        dma_sems = {}
        dma_cnt = {}
        ticket = {}
        waited = {k: {} for k in self.engs}

        def esem(eng):
            if eng not in eng_sem:
                eng_sem[eng] = nc.alloc_semaphore(name='c_' + eng)
                eng_cnt[eng] = 0
            return eng_sem[eng]

        import os
        embed_ok = os.environ.get('EMBED_WAIT', '1') == '1'
        snap = {}
        nwait = [0, 0]

        pending_embed = []

        def do_wait(eng, sem, val, embed=False):
            w = waited[eng]
            if w.get(sem.num, 0) >= val:
                nwait[1] += 1
                return
            if embed:
                pending_embed.append((sem, val))
            else:
                self.engs[eng].wait_ge(sem, val)
            nwait[0] += 1
            w[sem.num] = val
            k = snap.get((sem.num, val))
            if k:
                for sn, v in k.items():
                    if w.get(sn, 0) < v:
                        w[sn] = v

        for idx in order:
            o = ops[idx]
            eng = o['eng']
            wmax = {}
            for d in wait_deps[idx]:
                sem, val = ticket[d]
                if wmax.get(sem.num, (None, 0))[1] < val:
                    wmax[sem.num] = (sem, val)
            keys = sorted(wmax)
            for kk, k in enumerate(keys):
                do_wait(eng, wmax[k][0], wmax[k][1], embed=(embed_ok and not o['dma'] and kk == len(keys) - 1))
            if o['dma']:
                if eng not in dma_sems:
                    dma_sems[eng] = [nc.alloc_semaphore(name='d_%s_%d' % (eng, i))
                                     for i in range(self.NSLOT)]
                    dma_cnt[eng] = 0
                c = dma_cnt[eng]
                dma_cnt[eng] = c + 1
                sem = dma_sems[eng][c % self.NSLOT]
                rnd = c // self.NSLOT
                if rnd > 0:
                    do_wait(eng, sem, 16 * rnd)
                inst = o['fn']()
                inst.then_inc(sem, 16)
                ticket[idx] = (sem, 16 * (rnd + 1))
                snap[(sem.num, 16 * (rnd + 1))] = dict(waited[eng])
            else:
                inst = o['fn']()
                if pending_embed:
                    sem_e, val_e = pending_embed.pop()
                    inst._wait_ge(sem_e, val_e)
                if idx in needed:
                    sem = esem(eng)
                    eng_cnt[eng] += 1
                    inst.then_inc(sem, 1)
                    ticket[idx] = (sem, eng_cnt[eng])
                    snap[(sem.num, eng_cnt[eng])] = dict(waited[eng])
        for d in final_wait_ops:
            sem, val = ticket[d]
            do_wait('sp', sem, val)
        self.stats = {'n_ops': len(ops), 'eng_cnt': dict(eng_cnt), 'dma_cnt': dict(dma_cnt), 'waits': nwait[0],
                      'waits_skipped': nwait[1]}
        return self.stats


class Arena:
    def __init__(self, S, nbytes, name='arena'):
        self.t = S.sbuf(name, [128, nbytes // 4], F32)
        self.n = nbytes
        self.top = 0

    def mark(self):
        return self.top

    def release(self, m):
        self.top = m

    def alloc(self, free_shape, dt):
        es = _dtsize(dt)
        n = 1
        for d in free_shape:
            n *= d
        nb = (n * es + 31) // 32 * 32
        off = self.top
        assert off + nb <= self.n, "arena overflow %d + %d > %d" % (off, nb, self.n)
        self.top = off + nb
        v = self.t[:, off // 4:(off + nb) // 4]
        if dt != F32:
            v = v.bitcast(dt)
        v = v[:, 0:n]
        if len(free_shape) == 2:
            v = v.rearrange("p (a b) -> p a b", a=free_shape[0])
        elif len(free_shape) == 3:
            v = v.rearrange("p (a b c) -> p a b c", a=free_shape[0], b=free_shape[1])
        return v


T = 4096
D = 1024
NT = T // 128
DFF = 2816
EPS = 1e-6


class Ctx:
    pass


def setup(nc, phases_debug=None):
    C = Ctx()
    S = Sched(nc)
    C.S = S
    C.nc = nc
    A = Arena(S, 200 * 1024)
    C.A = A
    C.ps = [S.psum("ps%d" % i, [128, 512], F32) for i in range(8)]
    C.psi = 0
    C.norm_i = 0
    C.psi6 = 0
    C.ident = A.alloc([128], F32)
    C.identb = A.alloc([128], BF16)
    C.ones = A.alloc([128], F32)
    C.epsb = A.alloc([1], F32)
    C.tril = A.alloc([128], F32)
    C.triu = A.alloc([128], F32)
    C.triu_s = A.alloc([128], F32)
    g = nc.gpsimd
    S.memset(C.ident, 0.0)
    S.op('pool', lambda: g.affine_select(out=C.ident, in_=C.ident, compare_op=ALU.not_equal, fill=1.0,
                                         base=0, pattern=[[-1, 128]], channel_multiplier=1),
         [C.ident], [C.ident])
    S.copy(C.identb, C.ident)
    S.memset(C.ones, 1.0)
    C.onesb = A.alloc([128], BF16)
    S.copy(C.onesb, C.ones)
    S.memset(C.epsb, EPS)
    S.memset(C.tril, 1.0)
    S.op('pool', lambda: g.affine_select(out=C.tril, in_=C.tril, compare_op=ALU.is_ge, fill=0.0,
                                         base=0, pattern=[[-1, 128]], channel_multiplier=1),
         [C.tril], [C.tril])
    S.memset(C.triu, 1.0)
    S.op('pool', lambda: g.affine_select(out=C.triu, in_=C.triu, compare_op=ALU.is_ge, fill=0.0,
                                         base=0, pattern=[[1, 128]], channel_multiplier=-1),
         [C.triu], [C.triu])
    S.memset(C.triu_s, 1.0)
    S.op('pool', lambda: g.affine_select(out=C.triu_s, in_=C.triu_s, compare_op=ALU.is_gt, fill=0.0,
                                         base=0, pattern=[[1, 128]], channel_multiplier=-1),
         [C.triu_s], [C.triu_s])
    C.base_mark = A.mark()
    C.cvt_i = 0
    return C


def next_ps6(C):
    p = C.ps[C.psi6 % 6]
    C.psi6 += 1
    return p


def next_ps(C):
    p = C.ps[C.psi % 7]
    C.psi += 1
    return p


def load_w_bf16(C, w2d, rows, cols, dst=None, stage=None):
    S, A = C.S, C.A
    kc_n = rows // 128
    if dst is None:
        dst = A.alloc([kc_n, cols], BF16)
    CH = 2048
    for kc in range(kc_n):
        for c0 in range(0, cols, CH):
            c1 = min(cols, c0 + CH)
            st = stage[C.cvt_i % len(stage)]
            S.dma(st[:, 0:c1 - c0], w2d[kc * 128:(kc + 1) * 128, c0:c1])
            eng = ('act', 'dve', 'act', 'dve', 'act', 'dve', 'pool')[C.cvt_i % 7]
            S.copy(dst[:, kc, c0:c1], st[:, 0:c1 - c0], eng=eng)
            C.cvt_i += 1
    return dst


def load_cols(C, row_ap, n, dst):
    S = C.S
    nj = n // 128
    src = row_ap.rearrange("(j p) -> p j", p=128)
    step = 16
    for j0 in range(0, nj, step):
        j1 = min(nj, j0 + step)
        S.dma(dst[:, j0:j1], src[:, j0:j1], allow_slow_non_contiguous=True)


def norm_T(C, xt, gb, hb, hT, col0, sq, ss, rstd):
    S = C.S
    if isinstance(hb, list):
        k = C.norm_i % len(hb)
        C.norm_i += 1
        hb, sq, ss, rstd = hb[k], sq[k], ss[k], rstd[k]
    S.act(sq, xt, AF.Square, accum_out=ss)
    S.act(rstd, ss, AF.Ln, scale=1.0 / D, bias=C.epsb)
    S.act(rstd, rstd, AF.Exp, scale=-0.5)
    S.stt(hb, xt, rstd, gb, ALU.mult, ALU.mult)
    pst = next_ps(C)
    pv = pst[:, :].bitcast(BF16)
    for c in range(8):
        S.transpose(pv[:, c * 128:(c + 1) * 128], hb[:, c * 128:(c + 1) * 128], C.identb)
    S.copy(hT[:, :, col0:col0 + 128], pv.rearrange("p (c t) -> p c t", c=8))


def ffn_phase(C, xsrc, xdst, g_norm, w_up, conv_w, conv_b, w_down):
    S, A = C.S, C.A
    m0 = A.mark()
    wup = A.alloc([22, 8, 256], BF16)
    wdn = A.alloc([22, D], BF16)
    gb = A.alloc([D], F32)
    S.dma(gb, g_norm.partition_broadcast(128))
    cw = A.alloc([3, 44], F32)
    cb = A.alloc([44], F32)
    for i in range(3):
        load_cols(C, conv_w[i, :], 2 * DFF, cw[:, i, :])
    load_cols(C, conv_b, 2 * DFF, cb)
    TB = 256
    NTT = TB // 128
    hTs = [A.alloc([8, TB + 2], BF16) for _ in range(2)]
    xts = [A.alloc([D], F32) for _ in range(2)]
    sq = A.alloc([D], BF16)
    hb = A.alloc([D], BF16)
    ss = A.alloc([1], F32)
    rstd = A.alloc([1], F32)
    Cc = [[A.alloc([TB], F32) for _ in range(2)] for _ in range(2)]
    sg = [A.alloc([TB], F32) for _ in range(2)]
    aj = [A.alloc([TB], BF16) for _ in range(4)]
    ysb = [A.alloc([D], F32) for _ in range(2)]
    acc = [[C.ps[4 + tt * 2 + nh] for nh in range(2)] for tt in range(NTT)]
    rot = {'i': 0}

    def rps():
        p = C.ps[rot['i'] % 4]
        rot['i'] += 1
        return p

    def norm_block(tb):
        hT = hTs[tb % 2]
        if tb == 0:
            S.memset(hT[:, :, 0:2], 0.0)
        else:
            S.copy(hT[:, :, 0:2], hTs[(tb - 1) % 2][:, :, TB:TB + 2], eng='pool')
        for tt in range(NTT):
            t = tb * NTT + tt
            xt = xts[t % 2]
            S.dma(xt, xsrc[t * 128:(t + 1) * 128, :])
            S.act(sq, xt, AF.Square, accum_out=ss)
            S.act(rstd, ss, AF.Ln, scale=1.0 / D, bias=C.epsb)
            S.act(rstd, rstd, AF.Exp, scale=-0.5)
            S.stt(hb, xt, rstd, gb, ALU.mult, ALU.mult)
            pst = rps()
            pv = pst[:, :].bitcast(BF16)
            for c in range(8):
                S.transpose(pv[:, c * 128:(c + 1) * 128], hb[:, c * 128:(c + 1) * 128], C.identb)
            S.copy(hT[:, :, 2 + tt * 128:2 + (tt + 1) * 128], pv.rearrange("p (c t) -> p c t", c=8))

    outs = []
    NB = T // TB
    norm_block(0)
    m1 = A.mark()
    stage = [A.alloc([2048], F32) for _ in range(3)]
    wup_src = w_up.rearrange("(kc p) c -> p kc c", p=128)
    for jp in range(11):
        for half in range(2):
            st = stage[C.cvt_i % 3]
            c0 = half * DFF + jp * 256
            S.dma(st.rearrange("p (a b) -> p a b", a=8), wup_src[:, :, c0:c0 + 256])
            eng = ('act', 'dve', 'act', 'dve', 'act', 'dve', 'pool')[C.cvt_i % 7]
            S.copy(wup[:, half * 11 + jp, :, :], st.rearrange("p (a b) -> p a b", a=8), eng=eng)
            C.cvt_i += 1
        st = stage[C.cvt_i % 3]
        S.dma(st.rearrange("p (a b) -> p a b", a=2),
              w_down[jp * 256:(jp + 1) * 256, :].rearrange("(a p) c -> p a c", p=128))
        eng = ('act', 'dve', 'act', 'dve', 'act', 'dve', 'pool')[C.cvt_i % 7]
        S.copy(wdn[:, 2 * jp:2 * jp + 2, :], st.rearrange("p (a b) -> p a b", a=2), eng=eng)
        C.cvt_i += 1
    A.release(m1)
    for tb in range(NB):
        hT = hTs[tb % 2]
        for j in range(22):
            cc = []
            for half in range(2):
                jj = half * 22 + j
                col0 = half * DFF + j * 128
                ps = rps()
                for kc in range(8):
                    S.mm(ps[:, 0:TB + 2], wup[:, half * 11 + j // 2, kc, (j % 2) * 128:(j % 2 + 1) * 128], hT[:, kc, :],
                         start=(kc == 0), stop=(kc == 7))
                c = Cc[j % 2][half]
                S.act(c, ps[:, 2:TB + 2], AF.Identity, scale=cw[:, 2, jj:jj + 1], bias=cb[:, jj:jj + 1])
                S.stt(c, ps[:, 1:TB + 1], cw[:, 1, jj:jj + 1], c, ALU.mult, ALU.add)
                S.stt(c, ps[:, 0:TB], cw[:, 0, jj:jj + 1], c, ALU.mult, ALU.add)
                cc.append(c)
            sg_ = sg[j % 2]
            a_ = aj[j % 4]
            S.act(sg_, cc[0], AF.Silu)
            S.tt(a_, sg_, cc[1], ALU.mult, eng='pool')

            def down(jd):
                ad = aj[jd % 4]
                for tt in range(NTT):
                    for nh in range(2):
                        S.mm(acc[tt][nh][:, :], ad[:, tt * 128:(tt + 1) * 128], wdn[:, jd, nh * 512:(nh + 1) * 512],
                             start=(jd == 0), stop=(jd == 21))
            if j >= 2:
                down(j - 2)
            if j == 21:
                down(20)
                down(21)
            if j == 10 and tb + 1 < NB:
                norm_block(tb + 1)
        for tt in range(NTT):
            t = tb * NTT + tt
            xt = xts[t % 2]
            S.dma(xt, xsrc[t * 128:(t + 1) * 128, :])
            y = ysb[t % 2]
            for nh in range(2):
                S.tt(y[:, nh * 512:(nh + 1) * 512], acc[tt][nh][:, :], xt[:, nh * 512:(nh + 1) * 512], ALU.add)
            outs.append(S.dma(xdst[t * 128:(t + 1) * 128, :], y, q='pool'))
    A.release(m0)
    return outs


def xa_phase(C, xsrc, xdst, mem, g_xa, g_mem, w_q, w_kv, w_o, g_q, g_k):
    S, A = C.S, C.A
    m0 = A.mark()
    wq = A.alloc([8, D], BF16)
    wkv = A.alloc([8, 2 * D], BF16)
    wo = A.alloc([8, D], BF16)
    m1 = A.mark()
    stage = [A.alloc([2048], F32) for _ in range(4)]
    load_w_bf16(C, w_q, D, D, dst=wq, stage=stage)
    load_w_bf16(C, w_kv, D, 2 * D, dst=wkv, stage=stage)
    load_w_bf16(C, w_o, D, D, dst=wo, stage=stage)
    A.release(m1)
    gb = A.alloc([D], F32)
    gmb = A.alloc([D], F32)
    S.dma(gb, g_xa.partition_broadcast(128))
    S.dma(gmb, g_mem.partition_broadcast(128))
    gq = A.alloc([2], F32)
    gk = A.alloc([2], F32)
    load_cols(C, g_q, 256, gq)
    load_cols(C, g_k, 256, gk)
    onesb = A.alloc([128], BF16)
    S.copy(onesb, C.ones)
    TB = 256
    xts = [A.alloc([D], F32) for _ in range(2)]
    sqj = [A.alloc([D], BF16) for _ in range(2)]
    hb = [A.alloc([D], BF16) for _ in range(2)]
    ss = [A.alloc([1], F32) for _ in range(2)]
    rstd1 = [A.alloc([1], F32) for _ in range(2)]
    memT = A.alloc([8, 256], BF16)
    kT = A.alloc([8, 256], BF16)
    V = A.alloc([2, D], BF16)
    hT = A.alloc([8, TB], BF16)
    qraw = [A.alloc([TB], F32) for _ in range(2)]
    sq = [A.alloc([TB], F32) for _ in range(2)]
    rstd = A.alloc([TB], F32)
    qn = [A.alloc([TB], BF16) for _ in range(2)]
    ex = [A.alloc([TB], BF16) for _ in range(2)]
    rinv = A.alloc([TB], F32)
    oT = A.alloc([8, TB], BF16)
    ysb = A.alloc([D], F32)
    for mt in range(2):
        xt = xts[mt]
        S.dma(xt, mem[mt * 128:(mt + 1) * 128, :])
        norm_T(C, xt, gmb, hb, memT, mt * 128, sqj, ss, rstd1)
    for h in range(4):
        for c in range(2):
            ch = 2 * h + c
            ps = next_ps(C)
            for kc in range(8):
                S.mm(ps[:, 0:256], wkv[:, kc, ch * 128:(ch + 1) * 128], memT[:, kc, :], start=(kc == 0), stop=(kc == 7))
            S.copy(qraw[c][:, 0:256], ps[:, 0:256], eng='act')
            S.act(sq[c][:, 0:256], ps[:, 0:256], AF.Square)
        ps = next_ps(C)
        for c in range(2):
            S.mm(ps[:, 0:256], C.ones, sq[c][:, 0:256], start=(c == 0), stop=(c == 1))
        S.act(rstd[:, 0:256], ps[:, 0:256], AF.Ln, scale=1.0 / 256, bias=C.epsb)
        S.act(rstd[:, 0:256], rstd[:, 0:256], AF.Exp, scale=-0.5)
        for c in range(2):
            S.stt(kT[:, 2 * h + c, :], qraw[c][:, 0:256], gk[:, c:c + 1], rstd[:, 0:256], ALU.mult, ALU.mult)
    for mt in range(2):
        for nh in range(2):
            ps = next_ps(C)
            for kc in range(8):
                S.mm(ps[:, :], memT[:, kc, mt * 128:(mt + 1) * 128], wkv[:, kc, D + nh * 512:D + (nh + 1) * 512],
                     start=(kc == 0), stop=(kc == 7))
            S.copy(V[:, mt, nh * 512:(nh + 1) * 512], ps[:, :], eng='act')
    outs = []
    NTT = TB // 128
    NB = T // TB
    hTs = [hT, A.alloc([8, TB], BF16)]
    oTs = [oT, A.alloc([8, TB], BF16)]
    xtb = [[A.alloc([D], F32) for _ in range(NTT)] for _ in range(3)]
    ysbs = [ysb, A.alloc([D], F32)]

    class HB:
        pass
    HBs = []
    for h in range(4):
        b_ = HB()
        b_.qraw = [A.alloc([TB], F32) for _ in range(2)]
        b_.sq = [A.alloc([TB], BF16) for _ in range(2)]
        b_.rstd = A.alloc([TB], F32)
        b_.qn = [A.alloc([TB], BF16) for _ in range(2)]
        b_.ex = [A.alloc([TB], BF16) for _ in range(2)]
        b_.rinv = A.alloc([TB], F32)
        HBs.append(b_)

    def front(tb):
        for tt in range(NTT):
            t = tb * NTT + tt
            xt = xtb[tb % 3][tt]
            S.dma(xt, xsrc[t * 128:(t + 1) * 128, :])
            norm_T(C, xt, gb, hb, hTs[tb % 2], tt * 128, sqj, ss, rstd1)
            yield

    def head(tb, h):
        B = HBs[h]
        hT_ = hTs[tb % 2]
        oT_ = oTs[tb % 2]
        for c in range(2):
            ch = 2 * h + c
            ps = next_ps(C)
            for kc in range(8):
                S.mm(ps[:, 0:TB], wq[:, kc, ch * 128:(ch + 1) * 128], hT_[:, kc, :], start=(kc == 0), stop=(kc == 7))
            S.copy(B.qraw[c], ps[:, 0:TB], eng='act')
            S.act(B.sq[c], ps[:, 0:TB], AF.Square)
            yield
        ps = next_ps(C)
        for c in range(2):
            S.mm(ps[:, 0:TB], C.onesb, B.sq[c], start=(c == 0), stop=(c == 1))
        S.act(B.rstd, ps[:, 0:TB], AF.Ln, scale=1.0 / 256, bias=C.epsb)
        yield
        S.act(B.rstd, B.rstd, AF.Exp, scale=-0.5)
        for c in range(2):
            S.stt(B.qn[c], B.qraw[c], gq[:, c:c + 1], B.rstd, ALU.mult, ALU.mult)
        yield
        for mc in range(2):
            ps = next_ps(C)
            for c in range(2):
                S.mm(ps[:, 0:TB], kT[:, 2 * h + c, mc * 128:(mc + 1) * 128], B.qn[c], start=(c == 0), stop=(c == 1))
            S.act(B.ex[mc], ps[:, 0:TB], AF.Exp, scale=1.0 / 16)
        yield
        ps = next_ps(C)
        for mc in range(2):
            S.mm(ps[:, 0:TB], onesb, B.ex[mc], start=(mc == 0), stop=(mc == 1))
        S.act(B.rinv, ps[:, 0:TB], AF.Ln)
        S.act(B.rinv, B.rinv, AF.Exp, scale=-1.0)
        yield
        for c in range(2):
            ps = next_ps(C)
            for mc in range(2):
                S.mm(ps[:, 0:TB], V[:, mc, h * 256 + c * 128:h * 256 + (c + 1) * 128], B.ex[mc],
                     start=(mc == 0), stop=(mc == 1))
            S.tt(oT_[:, 2 * h + c, :], ps[:, 0:TB], B.rinv, ALU.mult)
        yield

    def back(tb):
        oT_ = oTs[tb % 2]
        for tt in range(NTT):
            t = tb * NTT + tt
            xt = xtb[tb % 3][tt]
            y = ysbs[t % 2]
            for nh in range(2):
                ps = next_ps(C)
                for kc in range(8):
                    S.mm(ps[:, :], oT_[:, kc, tt * 128:(tt + 1) * 128], wo[:, kc, nh * 512:(nh + 1) * 512],
                         start=(kc == 0), stop=(kc == 7))
                S.tt(y[:, nh * 512:(nh + 1) * 512], ps[:, :], xt[:, nh * 512:(nh + 1) * 512], ALU.add)
            outs.append(S.dma(xdst[t * 128:(t + 1) * 128, :], y, q='pool'))
            yield

    def rr(gens):
        live = list(gens)
        while live:
            nxt = []
            for g_ in live:
                try:
                    next(g_)
                    nxt.append(g_)
                except StopIteration:
                    pass
            live = nxt

    rr([front(0)])
    for tb in range(NB):
        gens = [head(tb, h) for h in range(4)]
        if tb + 1 < NB:
            gens.append(front(tb + 1))
        if tb >= 1:
            gens.append(back(tb - 1))
        rr(gens)
    rr([back(NB - 1)])
    A.release(m0)
    return outs


def gla_phase(C, xsrc, xdst, g_norm, w_in, w_gk, b_gk, norm_g, w_out):
    S, A = C.S, C.A
    S.rank_mode = True
    m0 = A.mark()
    NIN = 3088
    win = A.alloc([8, NIN], BF16)
    wout = A.alloc([8, D], BF16)
    m1 = A.mark()
    stage = [A.alloc([2048], F32) for _ in range(4)]
    load_w_bf16(C, w_in, D, NIN, dst=win, stage=stage)
    load_w_bf16(C, w_out, D, D, dst=wout, stage=stage)
    A.release(m1)
    gb = A.alloc([D], F32)
    S.dma(gb, g_norm.partition_broadcast(128))
    bgk = A.alloc([512], F32)
    S.dma(bgk, b_gk.partition_broadcast(128))
    ngb = A.alloc([256], F32)
    S.dma(ngb, norm_g.partition_broadcast(128))
    wgk32 = A.alloc([512], F32)
    S.dma(wgk32[0:16, :], w_gk)
    wgk = A.alloc([512], BF16)
    S.copy(wgk[0:16, :], wgk32[0:16, :])
    xts = [A.alloc([D], F32) for _ in range(4)]
    sqj = [A.alloc([D], BF16) for _ in range(2)]
    hb = [A.alloc([D], BF16) for _ in range(2)]
    ss = [A.alloc([1], F32) for _ in range(2)]
    rstd1 = [A.alloc([1], F32) for _ in range(2)]
    hTs = [A.alloc([8, 128], BF16) for _ in range(2)]
    lrT = A.alloc([128], BF16)
    zb = A.alloc([512], F32)
    Lgs = [A.alloc([512], F32) for _ in range(2)]
    vtms = [A.alloc([D], BF16) for _ in range(3)]
    rss = [A.alloc([D], F32) for _ in range(4)]

    class HB:
        pass
    HBs = []
    for h in range(4):
        b_ = HB()
        b_.BC = A.alloc([128], F32)
        b_.nbm = A.alloc([1], F32)
        b_.eb = A.alloc([1], F32)
        b_.E = [A.alloc([128], F32) for _ in range(4)]
        for nm in ("qe", "ke", "kdT"):
            setattr(b_, nm, A.alloc([128], BF16))
        b_.hand = []
        for par in range(2):
            hd = HB()
            for nm in ("qg", "kd", "att"):
                setattr(hd, nm, A.alloc([128], BF16))
            hd.eb = A.alloc([1], F32)
            b_.hand.append(hd)
        HBs.append(b_)
    St = [A.alloc([256], F32) for _ in range(4)]
    Sb = [A.alloc([256], BF16) for _ in range(4)]
    o_alls = [A.alloc([D], F32) for _ in range(2)]
    oss = A.alloc([4], F32)
    orstd = A.alloc([4], F32)
    ob = A.alloc([D], BF16)
    oT = A.alloc([8, 128], BF16)
    ysbs = [A.alloc([D], F32) for _ in range(2)]
    for h in range(4):
        S.memset(St[h], 0.0)
        S.memset(Sb[h], 0.0)
    outs = []
    sc = 128.0 ** -0.5

    def front(t):
        xt = xts[t % 4]
        hT, Lg, vtm, rs = hTs[t % 2], Lgs[t % 2], vtms[t % 3], rss[t % 4]
        S.dma(xt, xsrc[t * 128:(t + 1) * 128, :])
        norm_T(C, xt, gb, hb, hT, 0, sqj, ss, rstd1)
        yield
        ps = next_ps(C)
        for kc in range(8):
            S.mm(ps[0:16, 0:128], win[:, kc, 3072:3088], hT[:, kc, :], start=(kc == 0), stop=(kc == 7))
        S.copy(lrT[0:16, :], ps[0:16, 0:128], eng='act')
        yield
        ps = next_ps(C)
        S.mm(ps[:, :], lrT[0:16, :], wgk[0:16, :])
        S.tt(zb, ps[:, :], bgk, ALU.add)
        S.act(zb, zb, AF.Exp, scale=-1.0)
        S.act(Lg, zb, AF.Ln, bias=C.ones[:, 0:1])
        yield
        for nh in range(2):
            ps = next_ps(C)
            for kc in range(8):
                S.mm(ps[:, :], hT[:, kc, :], win[:, kc, 1024 + nh * 512:1024 + (nh + 1) * 512], start=(kc == 0), stop=(kc == 7))
            S.copy(vtm[:, nh * 512:(nh + 1) * 512], ps[:, :], eng='act')
            yield
        for nh in range(2):
            ps = next_ps(C)
            for kc in range(8):
                S.mm(ps[:, :], hT[:, kc, :], win[:, kc, 2048 + nh * 512:2048 + (nh + 1) * 512], start=(kc == 0), stop=(kc == 7))
            S.act(rs[:, nh * 512:(nh + 1) * 512], ps[:, :], AF.Silu)
            yield

    def headA(t, h):
        B = HBs[h]
        Hd = B.hand[t % 2]
        hT, Lg = hTs[t % 2], Lgs[t % 2]
        E = B.E
        ps = next_ps(C)
        S.mm(ps[:, 0:128], Lg[:, h * 128:(h + 1) * 128], C.triu)
        S.act(B.BC, ps[:, 0:128], AF.Copy, scale=-1.0 / 16)
        yield
        S.act(B.nbm, B.BC[:, 64:65], AF.Copy, scale=-1.0)
        S.act(E[0], B.BC, AF.Exp, bias=B.nbm)
        S.act(E[1], B.BC, AF.Exp, scale=-1.0, bias=B.BC[:, 64:65])
        yield
        S.act(E[2], B.BC, AF.Exp)
        S.act(E[3], B.BC, AF.Exp, scale=-1.0, bias=B.BC[:, 127:128])
        S.act(Hd.eb, B.BC[:, 127:128], AF.Exp)
        yield
        psq = next_ps(C)
        for kc in range(8):
            S.mm(psq[:, 0:128], win[:, kc, h * 128:(h + 1) * 128], hT[:, kc, :], start=(kc == 0), stop=(kc == 7))
        S.stt(B.qe, psq[:, 0:128], sc, E[0], ALU.mult, ALU.mult)
        S.stt(Hd.qg, psq[:, 0:128], sc, E[2], ALU.mult, ALU.mult)
        yield
        psk = next_ps(C)
        for kc in range(8):
            S.mm(psk[:, 0:128], win[:, kc, 512 + h * 128:512 + (h + 1) * 128], hT[:, kc, :], start=(kc == 0), stop=(kc == 7))
        S.tt(B.ke, psk[:, 0:128], E[1], ALU.mult)
        S.tt(B.kdT, psk[:, 0:128], E[3], ALU.mult)
        yield
        pst = next_ps(C)
        ptv = pst[:, :].bitcast(BF16)
        S.transpose(ptv[:, 0:128], B.kdT, C.identb)
        S.copy(Hd.kd, ptv[:, 0:128], eng='act')
        psa = next_ps(C)
        S.mm(psa[:, 0:128], B.ke, B.qe)
        S.tt(Hd.att, psa[:, 0:128], C.triu, ALU.mult)
        yield

    def headB(t, h):
        B = HBs[h]
        Hd = B.hand[t % 2]
        vtm, o_all = vtms[t % 3], o_alls[t % 2]
        pso = next_ps(C)
        S.mm(pso[:, 0:256], Hd.att, vtm[:, h * 256:(h + 1) * 256], start=True, stop=False)
        S.mm(pso[:, 0:256], Hd.qg, Sb[h], start=False, stop=True)
        S.copy(o_all[:, h * 256:(h + 1) * 256], pso[:, 0:256], eng='act')
        pss = next_ps(C)
        S.mm(pss[:, 0:256], Hd.kd, vtm[:, h * 256:(h + 1) * 256])
        S.stt(St[h], St[h], Hd.eb, pss[:, 0:256], ALU.mult, ALU.add)
        S.copy(Sb[h], St[h], eng='pool')
        yield

    def back(t):
        xt = xts[t % 4]
        o_all, rs, ysb = o_alls[t % 2], rss[t % 4], ysbs[t % 2]
        for h in range(4):
            S.act(ob[:, h * 256:(h + 1) * 256], o_all[:, h * 256:(h + 1) * 256], AF.Square, accum_out=oss[:, h:h + 1])
        S.act(orstd, oss, AF.Ln, scale=1.0 / 256, bias=C.epsb)
        S.act(orstd, orstd, AF.Exp, scale=-0.5)
        yield
        for h in range(4):
            S.stt(o_all[:, h * 256:(h + 1) * 256], o_all[:, h * 256:(h + 1) * 256], orstd[:, h:h + 1], ngb, ALU.mult, ALU.mult)
        S.tt(ob, o_all, rs, ALU.mult)
        yield
        pst = next_ps(C)
        pv = pst[:, :].bitcast(BF16)
        for c in range(8):
            S.transpose(pv[:, c * 128:(c + 1) * 128], ob[:, c * 128:(c + 1) * 128], C.identb)
        S.copy(oT, pv.rearrange("p (c t) -> p c t", c=8))
        yield
        for nh in range(2):
            ps = next_ps(C)
            for kc in range(8):
                S.mm(ps[:, :], oT[:, kc, :], wout[:, kc, nh * 512:(nh + 1) * 512], start=(kc == 0), stop=(kc == 7))
            S.tt(ysb[:, nh * 512:(nh + 1) * 512], ps[:, :], xt[:, nh * 512:(nh + 1) * 512], ALU.add)
        outs.append(S.dma(xdst[t * 128:(t + 1) * 128, :], ysb, q='pool'))
        yield

    def rr(gens):
        live = list(gens)
        while live:
            nxt = []
            for g_ in live:
                try:
                    next(g_)
                    nxt.append(g_)
                except StopIteration:
                    pass
            live = nxt

    rr([front(0)])
    rr([headA(0, h) for h in range(4)] + [front(1)])
    for t in range(NT):
        gens = []
        if t >= 1:
            gens.append(back(t - 1))
        gens += [headB(t, h) for h in range(4)]
        if t + 1 < NT:
            gens += [headA(t + 1, h) for h in range(4)]
        if t + 2 < NT:
            gens.append(front(t + 2))
        rr(gens)
    rr([back(NT - 1)])
    S.rank_mode = False
    A.release(m0)
    return outs


def dn_phase(C, xsrc, xdst, g_norm, w_in, conv_w, a_log, dt_bias, norm_g, w_out):
    S, A = C.S, C.A
    S.rank_mode = True
    m0 = A.mark()
    NA = 2056
    win = A.alloc([8, NA], BF16)
    wout = A.alloc([4, D], BF16)
    m1 = A.mark()
    stage = [A.alloc([2048], F32) for _ in range(4)]
    load_w_bf16(C, w_in[:, 0:NA], D, NA, dst=win, stage=stage)
    load_w_bf16(C, w_out[0:512, :], 512, D, dst=wout, stage=stage)
    A.release(m1)
    gb = A.alloc([D], F32)
    S.dma(gb, g_norm.partition_broadcast(128))
    cw = A.alloc([4, 12], F32)
    for i in range(4):
        load_cols(C, conv_w[i, :], 1536, cw[:, i, :])
    ngb = A.alloc([128], F32)
    S.dma(ngb, norm_g.partition_broadcast(128))
    dtb = A.alloc([4], F32)
    S.dma(dtb, dt_bias.partition_broadcast(128))
    nea = A.alloc([4], F32)
    S.dma(nea, a_log.partition_broadcast(128))
    S.act(nea, nea, AF.Exp)
    S.ts(nea, nea, -1.0, None, ALU.mult)
    sqj = [A.alloc([D], BF16) for _ in range(2)]
    hb = [A.alloc([D], BF16) for _ in range(2)]
    ss = [A.alloc([1], F32) for _ in range(2)]
    rstd1 = [A.alloc([1], F32) for _ in range(2)]
    hT = A.alloc([8, 128], BF16)
    U = A.alloc([12, 131], F32)
    S.memset(U, 0.0)
    cvs = [A.alloc([12, 128], F32) for _ in range(2)]
    sqf = A.alloc([128], F32)
    rinv = A.alloc([128], F32)
    qns = [[A.alloc([128], F32) for _ in range(4)] for _ in range(3)]
    kns = [[A.alloc([128], F32) for _ in range(4)] for _ in range(2)]
    ab = A.alloc([8], F32)
    betas = [A.alloc([4], F32) for _ in range(2)]
    ggs = [A.alloc([4], F32) for _ in range(2)]
    zss = [A.alloc([512], F32) for _ in range(4)]
    sq8 = A.alloc([8, 128], F32)
    rinv8 = A.alloc([8, 128], F32)

    class HB:
        pass
    HBs = []
    for h in range(4):
        b_ = HB()
        for nm in ("ktm", "vtm", "tmp", "arg", "DT", "AT0", "Am", "ATm", "P0", "P1", "PT0", "PT1", "TT", "vb", "kbg",
                   "delta", "o1"):
            setattr(b_, nm, A.alloc([128], F32))
        for nm in ("gcc", "beg"):
            setattr(b_, nm, A.alloc([1], F32))
        b_.hand = []
        for par in range(2):
            hd = HB()
            for nm in ("usb", "wT", "qkm", "kdec"):
                setattr(hd, nm, A.alloc([128], F32))
            for nm in ("egc", "egl"):
                setattr(hd, nm, A.alloc([1], F32))
            b_.hand.append(hd)
        HBs.append(b_)
    St = [A.alloc([128], F32) for _ in range(4)]
    o_as = [A.alloc([512], F32) for _ in range(2)]
    oss = A.alloc([4], F32)
    orstd = A.alloc([4], F32)
    ob = A.alloc([512], BF16)
    oT = A.alloc([4, 128], BF16)
    ysbs = [A.alloc([D], F32) for _ in range(2)]
    xts = [A.alloc([D], F32) for _ in range(4)]
    for h in range(4):
        S.memset(St[h], 0.0)
    outs = []
    sc = 128.0 ** -0.5
    def front(t):
        xt = xts[t % 4]
        cv, qn, kn, beta, gg, zs = cvs[t % 2], qns[t % 3], kns[t % 2], betas[t % 2], ggs[t % 2], zss[t % 4]
        S.dma(xt, xsrc[t * 128:(t + 1) * 128, :])
        norm_T(C, xt, gb, hb, hT, 0, sqj, ss, rstd1)
        for c in range(12):
            ps = next_ps(C)
            for kc in range(8):
                S.mm(ps[:, 0:128], win[:, kc, c * 128:(c + 1) * 128], hT[:, kc, :], start=(kc == 0), stop=(kc == 7))
            S.copy(U[:, c, 3:131], ps[:, 0:128], eng='act')
            S.act(cv[:, c, :], ps[:, 0:128], AF.Copy, scale=cw[:, 3, c:c + 1])
            for i in range(3):
                S.stt(cv[:, c, :], U[:, c, i:i + 128], cw[:, i, c:c + 1], cv[:, c, :], ALU.mult, ALU.add)
            S.copy(U[:, c, 0:3], U[:, c, 128:131], eng='pool')
            yield
        S.act(cv, cv, AF.Silu)
        ps = next_ps(C)
        for kc in range(8):
            S.mm(ps[:, 0:8], hT[:, kc, :], win[:, kc, 2048:2056], start=(kc == 0), stop=(kc == 7))
        S.copy(ab, ps[:, 0:8], eng='act')
        ps = next_ps(C)
        for kc in range(8):
            S.mm(ps[:, :], hT[:, kc, :], win[:, kc, 1536:2048], start=(kc == 0), stop=(kc == 7))
        S.act(zs, ps[:, :], AF.Silu)
        yield
        S.act(beta, ab[:, 4:8], AF.Exp, scale=-1.0)
        S.ts(beta, beta, 1.0, None, ALU.add)
        S.recip(beta, beta)
        S.tt(gg, ab[:, 0:4], dtb, ALU.add)
        S.act(gg, gg, AF.Exp)
        S.act(gg, gg, AF.Ln, bias=C.ones[:, 0:1])
        S.tt(gg, gg, nea, ALU.mult)
        yield
        S.act(sq8, cv[:, 0:8, :], AF.Square)
        for half in range(2):
            ps = next_ps(C)
            S.mm(ps[:, :], C.ones, sq8[:, half * 4:(half + 1) * 4, :])
            S.act(rinv8[:, half * 4:(half + 1) * 4, :], ps[:, :].rearrange("p (a b) -> p a b", a=4), AF.Ln, bias=C.epsb)
        S.act(rinv8, rinv8, AF.Exp, scale=-0.5)
        yield
        for h in range(4):
            S.stt(qn[h], cv[:, h, :], sc, rinv8[:, h, :], ALU.mult, ALU.mult)
        for h in range(4):
            S.tt(kn[h], cv[:, 4 + h, :], rinv8[:, 4 + h, :], ALU.mult)

        yield

    def chainA(t, h):
        B = HBs[h]
        Hd = B.hand[t % 2]
        cv, qn, kn, beta, gg = cvs[t % 2], qns[t % 3], kns[t % 2], betas[t % 2], ggs[t % 2]
        ps = next_ps(C)
        S.transpose(ps[:, 0:128], kn[h], C.ident)
        S.copy(B.ktm, ps[:, 0:128], eng='act')
        ps = next_ps(C)
        S.transpose(ps[:, 0:128], cv[:, 8 + h, :], C.ident)
        S.copy(B.vtm, ps[:, 0:128], eng='act')
        yield
        S.ts(B.tmp, C.triu, gg[:, h:h + 1], None, ALU.mult)
        psg = next_ps(C)
        S.mm(psg[:, 0:128], C.ones, B.tmp)
        psc = next_ps(C)
        S.mm(psc[:, 0:1], C.triu, gg[:, h:h + 1])
        S.copy(B.gcc, psc[:, 0:1], eng='act')
        S.act(Hd.egl, psg[:, 127:128], AF.Exp)
        S.ts(B.arg, psg[:, 0:128], B.gcc, 0.0, ALU.subtract, ALU.min)
        yield
        S.act(B.DT, B.arg, AF.Exp)
        S.act(Hd.egc, B.gcc, AF.Exp)
        S.tt(B.beg, beta[:, h:h + 1], Hd.egc, ALU.mult)
        psk = next_ps(C)
        S.mm(psk[:, 0:128], kn[h], kn[h])
        S.tt(B.AT0, psk[:, 0:128], B.DT, ALU.mult)
        S.tt(B.AT0, B.AT0, C.triu_s, ALU.mult, eng='pool')
        yield
        ps = next_ps(C)
        S.transpose(ps[:, 0:128], B.AT0, C.ident)
        S.act(B.Am, ps[:, 0:128], AF.Copy, scale=beta[:, h:h + 1])
        yield
        ps = next_ps(C)
        S.transpose(ps[:, 0:128], B.Am, C.ident)
        S.copy(B.ATm, ps[:, 0:128], eng='act')
        S.tt(B.TT, C.ident, B.ATm, ALU.subtract, eng='pool')
        yield
        pk, pkt = B.ATm, B.Am
        Ps = (B.P0, B.P1)
        PTs = (B.PT0, B.PT1)
        for lvl in range(1, 7):
            npk, npkt = Ps[lvl % 2], PTs[lvl % 2]
            if lvl < 6:
                ps1 = next_ps(C)
                S.mm(ps1[:, 0:128], pkt, pk)
                S.copy(npk, ps1[:, 0:128], eng='act')
            ps2 = next_ps(C)
            S.mm(ps2[:, 0:128], pk, pkt)
            S.copy(npkt, ps2[:, 0:128], eng='act')
            pk, pkt = npk, npkt
            yield
            ps3 = next_ps(C)
            S.mm(ps3[:, 0:128], pkt, B.TT)
            S.tt(B.TT, ps3[:, 0:128], B.TT, ALU.add)
            yield
        S.ts(B.vb, B.vtm, beta[:, h:h + 1], None, ALU.mult)
        S.ts(B.kbg, B.ktm, B.beg, None, ALU.mult)
        psu = next_ps(C)
        S.mm(psu[:, 0:128], B.TT, B.vb)
        S.copy(Hd.usb, psu[:, 0:128], eng='act')
        psw = next_ps(C)
        S.mm(psw[:, 0:128], B.kbg, B.TT)
        S.copy(Hd.wT, psw[:, 0:128], eng='act')
        ps = next_ps(C)
        S.mm(ps[:, 0:128], kn[h], qn[h])
        S.tt(Hd.qkm, ps[:, 0:128], B.DT, ALU.mult)
        S.tt(Hd.qkm, Hd.qkm, C.triu, ALU.mult, eng='pool')
        S.ts(Hd.kdec, B.ktm, B.DT[:, 127:128], None, ALU.mult)
        yield

    def chainB(t, h):
        B = HBs[h]
        Hd = B.hand[t % 2]
        qn, o_a = qns[t % 3], o_as[t % 2]
        ps = next_ps(C)
        S.mm(ps[:, 0:128], Hd.wT, St[h])
        S.tt(B.delta, Hd.usb, ps[:, 0:128], ALU.subtract)
        psq = next_ps(C)
        S.mm(psq[:, 0:128], qn[h], St[h])
        S.act(B.o1, psq[:, 0:128], AF.Copy, scale=Hd.egc)
        yield
        ps = next_ps(C)
        S.mm(ps[:, 0:128], Hd.qkm, B.delta)
        S.tt(o_a[:, h * 128:(h + 1) * 128], ps[:, 0:128], B.o1, ALU.add)
        ps = next_ps(C)
        S.mm(ps[:, 0:128], Hd.kdec, B.delta)
        S.stt(St[h], St[h], Hd.egl, ps[:, 0:128], ALU.mult, ALU.add)
        yield


    def back(t):
        xt = xts[t % 4]
        o_a, zs, ysb = o_as[t % 2], zss[t % 4], ysbs[t % 2]
        for h in range(4):
            S.act(ob[:, h * 128:(h + 1) * 128], o_a[:, h * 128:(h + 1) * 128], AF.Square, accum_out=oss[:, h:h + 1])
        S.act(orstd, oss, AF.Ln, scale=1.0 / 128, bias=C.epsb)
        S.act(orstd, orstd, AF.Exp, scale=-0.5)
        for h in range(4):
            S.stt(o_a[:, h * 128:(h + 1) * 128], o_a[:, h * 128:(h + 1) * 128], orstd[:, h:h + 1], ngb, ALU.mult, ALU.mult)
        S.tt(ob, o_a, zs, ALU.mult)
        yield
        pst = next_ps(C)
        pv = pst[:, :].bitcast(BF16)
        for c in range(4):
            S.transpose(pv[:, c * 128:(c + 1) * 128], ob[:, c * 128:(c + 1) * 128], C.identb)
        S.copy(oT, pv[:, 0:512].rearrange("p (c t) -> p c t", c=4))
        yield
        for nh in range(2):
            ps = next_ps(C)
            for c in range(4):
                S.mm(ps[:, :], oT[:, c, :], wout[:, c, nh * 512:(nh + 1) * 512], start=(c == 0), stop=(c == 3))
            S.tt(ysb[:, nh * 512:(nh + 1) * 512], ps[:, :], xt[:, nh * 512:(nh + 1) * 512], ALU.add)
        outs.append(S.dma(xdst[t * 128:(t + 1) * 128, :], ysb, q='pool'))
        yield

    def run_round_robin(gens):
        live = list(gens)
        while live:
            nxt = []
            for g_ in live:
                try:
                    next(g_)
                    nxt.append(g_)
                except StopIteration:
                    pass
            live = nxt

    run_round_robin([front(0)])
    run_round_robin([chainA(0, h) for h in range(4)] + [front(1)])
    for t in range(NT):
        gens = []
        if t >= 1:
            gens.append(back(t - 1))
        gens += [chainB(t, h) for h in range(4)]
        if t + 1 < NT:
            gens += [chainA(t + 1, h) for h in range(4)]
        if t + 2 < NT:
            gens.append(front(t + 2))
        run_round_robin(gens)
    run_round_robin([back(NT - 1)])
    S.rank_mode = False
    A.release(m0)
    return outs


def sb_phase(C, xorig, xsrc, xdst, g_norm, w_in, g_q, g_k, w_out):
    S, A, nc = C.S, C.A, C.nc
    S.rank_mode = True
    m0 = A.mark()
    if not hasattr(C, 'sbq'):
        C.sbq = nc.dram_tensor("sb_q", [4, 128, T], BF16, kind="Internal").ap()
        C.sbk = nc.dram_tensor("sb_k", [4, 128, T], BF16, kind="Internal").ap()
        C.sbv = nc.dram_tensor("sb_v", [4, 128, NT, 128], BF16, kind="Internal").ap()
    wout = A.alloc([4, D], BF16)
    m1 = A.mark()
    gb = A.alloc([D], F32)
    S.dma(gb, g_norm.partition_broadcast(128))
    gq = A.alloc([1], F32)
    gk = A.alloc([1], F32)
    load_cols(C, g_q, 128, gq)
    load_cols(C, g_k, 128, gk)
    win = A.alloc([8, 1536], BF16)
    stage = [A.alloc([2048], F32) for _ in range(4)]
    load_w_bf16(C, w_in[:, 2056:3592], D, 1536, dst=win, stage=stage)
    load_w_bf16(C, w_out[512:1024, :], 512, D, dst=wout, stage=stage)
    TB = 512
    xts = [A.alloc([D], F32) for _ in range(2)]
    sqj = [A.alloc([D], BF16) for _ in range(2)]
    hb = [A.alloc([D], BF16) for _ in range(2)]
    ss = [A.alloc([1], F32) for _ in range(2)]
    rstd1 = [A.alloc([1], F32) for _ in range(2)]
    hT = A.alloc([8, TB], BF16)
    hTs = [hT, A.alloc([8, TB], BF16)]
    sq = [A.alloc([TB], BF16) for _ in range(8)]
    rstd = [A.alloc([TB], F32) for _ in range(8)]
    qk_o = [A.alloc([TB], BF16) for _ in range(8)]
    qg_b = [A.alloc([TB], F32) for _ in range(8)]
    v_o = [A.alloc([512], BF16) for _ in range(2)]
    sc = 128.0 ** -0.5
    NTT = TB // 128

    def p1_front(tb):
        for tt in range(NTT):
            t = tb * NTT + tt
            xt = xts[t % 2]
            S.dma(xt, xorig[t * 128:(t + 1) * 128, :])
            norm_T(C, xt, gb, hb, hTs[tb % 2], tt * 128, sqj, ss, rstd1)
            yield

    def p1_qk(tb, idx):
        h, isk = idx // 2, idx % 2
        c0, dst, gcol, scl = ((h * 128, C.sbq, gq, sc), (512 + h * 128, C.sbk, gk, 1.0))[isk]
        hT_ = hTs[tb % 2]
        sq_, rs_, o_ = sq[idx], rstd[idx], qk_o[idx]
        ps = next_ps(C)
        for kc in range(8):
            S.mm(ps[:, :], win[:, kc, c0:c0 + 128], hT_[:, kc, :], start=(kc == 0), stop=(kc == 7))
        S.act(sq_, ps[:, :], AF.Square)
        S.ts(qg_b[idx], ps[:, :], gcol, scl, ALU.mult, ALU.mult)
        yield
        ps2 = next_ps(C)
        S.mm(ps2[:, :], C.onesb, sq_)
        S.act(rs_, ps2[:, :], AF.Ln, scale=1.0 / 128, bias=C.epsb)
        yield
        S.act(rs_, rs_, AF.Exp, scale=-0.5)
        S.tt(o_, qg_b[idx], rs_, ALU.mult, eng='pool')
        S.dma(dst[h, :, tb * TB:(tb + 1) * TB], o_, q='sp')
        yield

    def p1_v(tb):
        hT_ = hTs[tb % 2]
        for tt in range(NTT):
            t = tb * NTT + tt
            ps = next_ps(C)
            for kc in range(8):
                S.mm(ps[:, :], hT_[:, kc, tt * 128:(tt + 1) * 128], win[:, kc, 1024:1536], start=(kc == 0), stop=(kc == 7))
            vo = v_o[t % 2]
            S.copy(vo, ps[:, :], eng='act')
            S.dma(C.sbv[:, :, t, :].rearrange("h p d -> p h d"), vo.rearrange("p (h d) -> p h d", h=4), q='sp')
            yield

    def rr1(gens):
        live = list(gens)
        while live:
            nxt = []
            for g_ in live:
                try:
                    next(g_)
                    nxt.append(g_)
                except StopIteration:
                    pass
            live = nxt

    NB1 = T // TB
    rr1([p1_front(0)])
    for tb in range(NB1):
        gens = [p1_qk(tb, idx) for idx in range(8)] + [p1_v(tb)]
        if tb + 1 < NB1:
            gens.append(p1_front(tb + 1))
        rr1(gens)
    A.release(m1)
    o_all = A.alloc([NT, 512], BF16)
    qT = [A.alloc([T], BF16) for _ in range(1)]
    kT = [A.alloc([T], BF16) for _ in range(1)]
    V = [A.alloc([NT, 128], BF16) for _ in range(1)]
    sp = [A.alloc([T], F32) for _ in range(2)]
    w = [A.alloc([T], F32) for _ in range(2)]
    Pc = A.alloc([T], F32)
    att = [A.alloc([T], BF16) for _ in range(2)]
    attT = [A.alloc([512], BF16) for _ in range(3)]
    ntot = [A.alloc([1], F32) for _ in range(2)]
    mask_s = A.alloc([128], F32)
    S.tt(mask_s, C.tril, C.ident, ALU.subtract)
    mask_b = A.alloc([128], BF16)
    S.copy(mask_b, mask_s)
    V2 = [V[0], A.alloc([NT, 128], BF16)]
    its = [(h, i) for h in range(4) for i in range(NT)]
    state = {'ai': 0}

    pss = {}

    def stage_a1(n):
        h, i = its[n]
        b = n % 2
        L = 128 * (i + 1)
        if i == 0:
            S.dma(qT[0], C.sbq[h])
            S.dma(kT[0], C.sbk[h])
            S.dma(V2[h % 2], C.sbv[h])
        sp_, w_ = sp[b], w[b]
        pss[n] = []
        for ci, k0 in enumerate(range(0, L, 512)):
            k1 = min(L, k0 + 512)
            if ci >= 5:
                sub_chunk(n, ci - 5)
            ps = next_ps6(C)
            pss[n].append((ps, k0, k1, False))
            S.mm(ps[:, 0:k1 - k0], qT[0][:, i * 128:(i + 1) * 128], kT[0][:, k0:k1])
            S.act(sp_[:, k0:k1], ps[:, 0:k1 - k0], AF.Exp)
            S.act(sp_[:, k0:k1], sp_[:, k0:k1], AF.Ln, bias=C.ones[:, 0:1])

    def sub_chunk(n, ci):
        b = n % 2
        ps, k0, k1, done = pss[n][ci]
        if done:
            return
        S.tt(w[b][:, k0:k1], ps[:, 0:k1 - k0], sp[b][:, k0:k1], ALU.subtract)
        pss[n][ci] = (ps, k0, k1, True)

    def stage_a2(n):
        h, i = its[n]
        b = n % 2
        L = 128 * (i + 1)
        sp_ = sp[b]
        for ci in range(len(pss[n])):
            sub_chunk(n, ci)
        del pss[n]
        S.tt(sp_[:, L - 128:L], sp_[:, L - 128:L], mask_s, ALU.mult, eng='pool')

    def stage_b1(n):
        h, i = its[n]
        b = n % 2
        L = 128 * (i + 1)
        sp_, w_, nt_ = sp[b], w[b], ntot[b]
        S.scan(Pc[:, 0:L], sp_[:, 0:L], sp_[:, 0:L], 0.0, ALU.add, ALU.max)
        S.ts(nt_, Pc[:, L - 1:L], -1.0, None, ALU.mult)
        S.tt(Pc[:, 0:L], Pc[:, 0:L], w_[:, 0:L], ALU.add, eng=('dve' if n % 3 == 0 else 'pool'))

    def stage_b2(n):
        h, i = its[n]
        b = n % 2
        L = 128 * (i + 1)
        att_, nt_ = att[b], ntot[b]
        S.act(att_[:, 0:L], Pc[:, 0:L], AF.Exp, bias=nt_)
        S.tt(att_[:, L - 128:L], att_[:, L - 128:L], mask_b, ALU.mult, eng='pool')

    def stage_c(n):
        h, i = its[n]
        b = n % 2
        att_ = att[b]
        vh = V2[h % 2]
        pso = C.ps[6 + b]
        nkt = i + 1
        groups = [(g0, min(nkt, g0 + 4)) for g0 in range(0, nkt, 4)]
        ats = {}

        def tr(gi):
            g0, g1 = groups[gi]
            pst = next_ps6(C)
            ptv = pst[:, :].bitcast(BF16)
            for kt in range(g0, g1):
                S.transpose(ptv[:, (kt - g0) * 128:(kt - g0 + 1) * 128], att_[:, kt * 128:(kt + 1) * 128], C.identb)
            at = attT[state['ai'] % 3]
            state['ai'] += 1
            S.copy(at[:, 0:(g1 - g0) * 128], ptv[:, 0:(g1 - g0) * 128], eng=('act' if state['ai'] % 3 == 0 else 'dve'))
            ats[gi] = at

        def av(gi):
            g0, g1 = groups[gi]
            at = ats[gi]
            for kt in range(g0, g1):
                S.mm(pso[:, 0:128], at[:, (kt - g0) * 128:(kt - g0 + 1) * 128], vh[:, kt, :],
                     start=(kt == 0), stop=(kt == nkt - 1))

        tr(0)
        for gi in range(len(groups)):
            if gi + 1 < len(groups):
                tr(gi + 1)
            av(gi)
        S.copy(o_all[:, i, h * 128:(h + 1) * 128], pso[:, 0:128], eng='act')

    NI = len(its)
    for n in range(NI + 2):
        if n < NI:
            stage_a1(n)
        if 0 <= n - 1 < NI:
            stage_b1(n - 1)
        if n < NI:
            stage_a2(n)
        if 0 <= n - 2 < NI:
            stage_c(n - 2)
        if 0 <= n - 1 < NI:
            stage_b2(n - 1)
    oT = [A.alloc([4, 128], BF16) for _ in range(2)]
    xts = [A.alloc([D], F32) for _ in range(2)]
    ysb = [A.alloc([D], F32) for _ in range(2)]
    outs = []
    for i in range(NT):
        xt = xts[i % 2]
        S.dma(xt, xsrc[i * 128:(i + 1) * 128, :])
        pst = next_ps(C)
        pv = pst[:, :].bitcast(BF16)
        for c in range(4):
            S.transpose(pv[:, c * 128:(c + 1) * 128], o_all[:, i, c * 128:(c + 1) * 128], C.identb)
        S.copy(oT[i % 2], pv[:, 0:512].rearrange("p (c t) -> p c t", c=4))
        y = ysb[i % 2]
        for nh in range(2):
            ps = next_ps(C)
            for c in range(4):
                S.mm(ps[:, :], oT[i % 2][:, c, :], wout[:, c, nh * 512:(nh + 1) * 512], start=(c == 0), stop=(c == 3))
            S.tt(y[:, nh * 512:(nh + 1) * 512], ps[:, :], xt[:, nh * 512:(nh + 1) * 512], ALU.add)
        outs.append(S.dma(xdst[i * 128:(i + 1) * 128, :], y, q='pool'))
    S.rank_mode = False
    A.release(m0)
    return outs


from concourse.bass_utils import run_bass_kernel_spmd

_NC_CACHE = {}

_WNAMES = ["norm_mix", "norm_xa", "norm_mem", "norm_ffn", "xa_w_q", "xa_w_kv", "xa_w_o", "xa_g_q", "xa_g_k",
           "ffn_w_up", "ffn_conv_w", "ffn_conv_b", "ffn_w_down", "ab_w_in", "ab_conv_w", "dn_a_log", "dn_dt_bias",
           "dn_norm_g", "sb_g_q", "sb_g_k", "ab_w_out", "gla_w_in", "gla_w_gk", "gla_b_gk", "gla_norm_g", "gla_w_out"]


def build_program(shapes):
    nc = bass.Bass("TRN2", target_bir_lowering=False)
    ap = {}
    for name, shp in shapes.items():
        ap[name] = nc.dram_tensor(name, list(shp), F32, kind="ExternalInput").ap()
    y = nc.dram_tensor("y", [T, D], F32, kind="ExternalOutput").ap()
    C = setup(nc)
    C.S.readonly |= set(shapes.keys())
    x = ap["x"]
    outs = []
    dn_phase(C, x, y, ap["norm_mix"][0:1, :], ap["ab_w_in"][0], ap["ab_conv_w"][0], ap["dn_a_log"][0:1, :],
             ap["dn_dt_bias"][0:1, :], ap["dn_norm_g"][0:1, :], ap["ab_w_out"][0])
    sb_phase(C, x, y, y, ap["norm_mix"][0:1, :], ap["ab_w_in"][0], ap["sb_g_q"][0], ap["sb_g_k"][0], ap["ab_w_out"][0])
    for layer in range(2):
        if layer == 1:
            gla_phase(C, y, y, ap["norm_mix"][1:2, :], ap["gla_w_in"][0], ap["gla_w_gk"][0], ap["gla_b_gk"][0:1, :],
                      ap["gla_norm_g"][0:1, :], ap["gla_w_out"][0])
        xa_phase(C, y, y, ap["mem"], ap["norm_xa"][layer:layer + 1, :], ap["norm_mem"][layer:layer + 1, :],
                 ap["xa_w_q"][layer], ap["xa_w_kv"][layer], ap["xa_w_o"][layer], ap["xa_g_q"][layer], ap["xa_g_k"][layer])
        outs = ffn_phase(C, y, y, ap["norm_ffn"][layer:layer + 1, :], ap["ffn_w_up"][layer], ap["ffn_conv_w"][layer],
                         ap["ffn_conv_b"][layer], ap["ffn_w_down"][layer])
    C.S.finalize(outs)
    return nc


def kernel(**inputs):
    x = np.ascontiguousarray(inputs["x"], dtype=np.float32)
    mem = np.ascontiguousarray(inputs["mem"], dtype=np.float32)
    B = x.shape[0]
    w = {k: np.ascontiguousarray(inputs[k], dtype=np.float32) for k in _WNAMES}
    shapes = {"x": x.shape[1:], "mem": mem.shape[1:]}
    for k in _WNAMES:
        shapes[k] = w[k].shape
    key = tuple(sorted((k, tuple(v)) for k, v in shapes.items()))
    nc = build_program(shapes)
    in_maps = []
    for b in range(B):
        m = {"x": x[b], "mem": mem[b]}
        m.update(w)
        in_maps.append(m)
    res = run_bass_kernel_spmd(nc, in_maps, core_ids=list(range(B)))
    return np.stack([np.asarray(r["y"], dtype=np.float32) for r in res.results], axis=0)
```

```python
import numpy as np
import concourse.bass as bass
import concourse.mybir as mybir

F32 = mybir.dt.float32
BF16 = mybir.dt.bfloat16
AF = mybir.ActivationFunctionType
ALU = mybir.AluOpType
AX = mybir.AxisListType

_DT_SIZE = {F32: 4, BF16: 2}


def _dtsize(dt):
    if dt in _DT_SIZE:
        return _DT_SIZE[dt]
    return mybir.dt.size(dt)


def region(ap):
    t = ap.tensor
    name = t.name
    es = _dtsize(ap.dtype)
    pairs = ap.ap
    off = int(ap.offset)
    sp = type(t).__name__
    if sp.startswith('DRam'):
        lo = off
        hi = off
        for st, cnt in pairs:
            if st >= 0:
                hi += st * (cnt - 1)
            else:
                lo += st * (cnt - 1)
        return (name, 0, 0, lo * es, (hi + 1) * es)
    if sp.startswith('PSum'):
        return (name, 0, 127, 0, 2048)
    pstep, pcnt = pairs[0]
    if pstep == 0:
        p0 = 0
        f0 = off
        pn = 1
        pstep = 1 << 40
    else:
        p0 = off // pstep
        f0 = off % pstep
        pn = pcnt
    lo = f0
    hi = f0
    for st, cnt in pairs[1:]:
        if st >= 0:
            hi += st * (cnt - 1)
        else:
            lo += st * (cnt - 1)
    return (name, p0, p0 + pn - 1, lo * es, (hi + 1) * es)


def _ovl(a, b):
    return not (a[2] < b[1] or b[2] < a[1] or a[4] <= b[3] or b[4] <= a[3])


def _covers(a, b):
    return a[1] <= b[1] and a[2] >= b[2] and a[3] <= b[3] and a[4] >= b[4]


class Sched:
    NSLOT = 8

    def __init__(self, nc):
        self.nc = nc
        self.engs = {'pe': nc.tensor, 'act': nc.scalar, 'dve': nc.vector,
                     'pool': nc.gpsimd, 'sp': nc.sync}
        self.ops = []
        self.hist = {}
        self.readonly = set()
        self.rank_mode = False
        self.dram_names = set()

    def sbuf(self, name, shape, dt):
        g = self.nc.sbuf_tensor(name, list(shape), dt)
        return g.__enter__()

    def psum(self, name, shape, dt):
        g = self.nc.psum_tensor(name, list(shape), dt)
        return g.__enter__()

    def _pages(self, reg):
        sh = 19 if reg[0] in self.dram_names else 12
        return range(reg[3] >> sh, ((reg[4] - 1) >> sh) + 1)

    def op(self, eng, fn, reads, writes, dma=False):
        idx = len(self.ops)
        deps = {}
        rr = [region(a) for a in reads if a.tensor.name not in self.readonly]
        ww = [region(a) for a in writes]
        for a in list(reads) + list(writes):
            if type(a.tensor).__name__.startswith('DRam'):
                self.dram_names.add(a.tensor.name)
        for r in rr:
            h = self.hist.get(r[0])
            if h is None:
                continue
            for pg in self._pages(r):
                e = h.get(pg)
                if e is None:
                    continue
                for (reg, oi) in e['w']:
                    if _ovl(reg, r):
                        deps[oi] = 'raw'
                if r[0].startswith('ps'):
                    for (reg, oi) in e['r']:
                        if oi not in deps and self.ops[oi]['eng'] != eng:
                            deps[oi] = 'rr'
        for w in ww:
            h = self.hist.setdefault(w[0], {})
            for pg in self._pages(w):
                e = h.get(pg)
                if e is None:
                    continue
                for (reg, oi) in e['w']:
                    if _ovl(reg, w) and oi not in deps:
                        deps[oi] = 'waw'
                for (reg, oi) in e['r']:
                    if _ovl(reg, w) and oi not in deps:
                        deps[oi] = 'war'
        for w in ww:
            h = self.hist[w[0]]
            for pg in self._pages(w):
                e = h.setdefault(pg, {'w': [], 'r': []})
                e['w'] = [(reg, oi) for (reg, oi) in e['w'] if not _covers(w, reg)]
                e['r'] = [(reg, oi) for (reg, oi) in e['r'] if not _covers(w, reg)]
                e['w'].append((w, idx))
        for r in rr:
            h = self.hist.setdefault(r[0], {})
            for pg in self._pages(r):
                e = h.setdefault(pg, {'w': [], 'r': []})
                if not dma:
                    keep = []
                    for (reg, oi) in e['r']:
                        if oi == idx or not (self.ops[oi]['eng'] == eng and not self.ops[oi]['dma']
                                             and _covers(r, reg)):
                            keep.append((reg, oi))
                        elif oi not in deps:
                            deps[oi] = 'ord'
                    e['r'] = keep
                e['r'].append((r, idx))
        fdeps = set()
        best = {}
        for oi, kind in deps.items():
            o = self.ops[oi]
            if (not o['dma']) and (not dma) and o['eng'] == eng and kind != 'raw':
                continue
            if o['dma']:
                fdeps.add(oi)
            else:
                if best.get(o['eng'], -1) < oi:
                    best[o['eng']] = oi
        for oi in best.values():
            fdeps.add(oi)
        nbytes = 0
        for w in ww:
            nbytes = max(nbytes, w[4] - w[3])
        for r in rr:
            nbytes = max(nbytes, r[4] - r[3])
        nel = 1
        for a in list(reads) + list(writes):
            k = 1
            for (_st, cnt) in a.ap[1:]:
                k *= cnt
            nel = max(nel, k)
        if eng == 'pe' and len(reads) >= 2:
            k = 1
            for (_st, cnt) in reads[1].ap[1:]:
                k *= cnt
            nel = k * (4 if _dtsize(reads[1].dtype) == 4 else 1)
        self.ops.append({'eng': eng, 'fn': fn, 'deps': fdeps, 'dma': dma, 'alldeps': dict(deps), 'nb': nbytes,
                         'nel': nel, 'cost': None, 'rk': self.rank_mode})
        return idx

    def dma(self, out, in_, q='sp', **kw):
        e = self.engs[q]
        return self.op(q, lambda: e.dma_start(out=out, in_=in_, **kw), [in_], [out], dma=True)

    def mm(self, out, lhsT, rhs, start=True, stop=True, **kw):
        t = self.nc.tensor
        return self.op('pe', lambda: t.matmul(out, lhsT, rhs, start=start, stop=stop, **kw),
                       [lhsT, rhs], [out])

    def transpose(self, out, in_, ident):
        t = self.nc.tensor
        return self.op('pe', lambda: t.transpose(out, in_, ident), [in_, ident], [out])

    def act(self, out, in_, func, bias=None, scale=None, accum_out=None, eng='act'):
        e = self.engs[eng]
        kw = {}
        reads = [in_]
        writes = [out]
        if bias is not None:
            kw['bias'] = bias
            if not isinstance(bias, (int, float)):
                reads.append(bias)
        if scale is not None:
            kw['scale'] = scale
            if not isinstance(scale, (int, float)):
                reads.append(scale)
        if accum_out is not None:
            kw['accum_out'] = accum_out
            writes.append(accum_out)
        return self.op(eng, lambda: e.activation(out, in_, func, **kw), reads, writes)

    def tt(self, out, in0, in1, op, eng='dve'):
        e = self.engs[eng]
        return self.op(eng, lambda: e.tensor_tensor(out, in0, in1, op), [in0, in1], [out])

    def ts(self, out, in0, s1, s2, op0, op1=None, accum_out=None, eng='dve'):
        e = self.engs[eng]
        reads = [in0]
        if not isinstance(s1, (int, float)):
            reads.append(s1)
        if s2 is not None and not isinstance(s2, (int, float)):
            reads.append(s2)
        writes = [out]
        kw = {}
        if op1 is not None:
            kw['op1'] = op1
        if accum_out is not None:
            kw['accum_out'] = accum_out
            writes.append(accum_out)
        return self.op(eng, lambda: e.tensor_scalar(out, in0, s1, s2, op0, **kw), reads, writes)

    def stt(self, out, in0, scalar, in1, op0, op1, eng='dve'):
        e = self.engs[eng]
        reads = [in0, in1]
        if not isinstance(scalar, (int, float)):
            reads.append(scalar)
        return self.op(eng, lambda: e.scalar_tensor_tensor(out, in0, scalar, in1, op0, op1),
                       reads, [out])

    def copy(self, out, in_, eng='dve'):
        e = self.engs[eng]
        if eng == 'act':
            return self.op(eng, lambda: e.copy(out, in_), [in_], [out])
        return self.op(eng, lambda: e.tensor_copy(out, in_), [in_], [out])

    def memset(self, ap, val, eng='dve'):
        e = self.engs[eng]
        return self.op(eng, lambda: e.memset(ap, val), [], [ap])

    def reduce(self, out, in_, op, axis=None, eng='dve'):
        e = self.engs[eng]
        ax = axis if axis is not None else AX.X
        return self.op(eng, lambda: e.tensor_reduce(out, in_, ax, op), [in_], [out])

    def scan(self, out, d0, d1, initial, op0, op1, eng='dve'):
        e = self.engs[eng]
        reads = [d0, d1]
        if not isinstance(initial, (int, float)):
            reads.append(initial)
        return self.op(eng, lambda: e.tensor_tensor_scan(out, d0, d1, initial, op0, op1),
                       reads, [out])

    def recip(self, out, in_):
        e = self.nc.vector
        return self.op('dve', lambda: e.reciprocal(out, in_), [in_], [out])

    def _cost(self, o):
        if o['cost'] is not None:
            return o['cost']
        nb = o['nb']
        nel = o['nel']
        e = o['eng']
        if o['dma']:
            return 0.06
        if e == 'pe':
            return 0.035 + nel / 2400.0
        if e == 'act':
            return 0.2 + nel / 1200.0
        if e == 'dve':
            return 0.07 + nel / 960.0
        if e == 'pool':
            return 0.15 + nel / 600.0
        return 0.05

    def schedule(self, window=None):
        import os
        if window is None:
            window = int(os.environ.get("SCHED_WINDOW", "64"))
        ops = self.ops
        n = len(ops)
        succ = [[] for _ in range(n)]
        nrem = [0] * n
        for i, o in enumerate(ops):
            ds = o['alldeps']
            nrem[i] = len(ds)
            for d in ds:
                succ[d].append(i)
        ready_t = [0.0] * n
        finish = [0.0] * n
        rank = [0.0] * n
        for i in range(n - 1, -1, -1):
            r = 0.0
            for sidx in succ[i]:
                if rank[sidx] > r:
                    r = rank[sidx]
            rank[i] = r + self._cost(ops[i]) + 0.1
        use_rank = os.environ.get("SCHED_RANK", "1") == "1"
        pend = {}
        for i, o in enumerate(ops):
            pend.setdefault(o['eng'], []).append(i)
        pos = {e: 0 for e in pend}
        done = [False] * n
        eng_t = {e: 0.0 for e in pend}
        order = []
        LAT = 0.12
        left = n
        while left:
            best = None
            for e, lst in pend.items():
                p = pos[e]
                while p < len(lst) and done[lst[p]]:
                    p += 1
                pos[e] = p
                cnt = 0
                q = p
                while q < len(lst) and cnt < window:
                    i = lst[q]
                    q += 1
                    if done[i]:
                        continue
                    cnt += 1
                    if nrem[i]:
                        continue
                    st = ready_t[i] if ready_t[i] > eng_t[e] else eng_t[e]
                    key = (round(st, 1), -rank[i], i) if (use_rank and ops[i]['rk']) else (st, 0.0, i)
                    if best is None or key < best[0]:
                        best = (key, e, i)
            assert best is not None, "scheduler deadlock"
            e, i = best[1], best[2]
            st = ready_t[i] if ready_t[i] > eng_t[e] else eng_t[e]
            o = ops[i]
            c = self._cost(o)
            if o['dma']:
                eng_t[e] = st + c
                fin = st + 2.0 + o['nb'] / 200000.0
            else:
                eng_t[e] = st + c
                fin = st + c
            finish[i] = fin
            done[i] = True
            left -= 1
            order.append(i)
            for sidx in succ[i]:
                nrem[sidx] -= 1
                t = fin + LAT
                if t > ready_t[sidx]:
                    ready_t[sidx] = t
        return order

    def finalize(self, final_wait_ops=(), reorder=True):
        nc = self.nc
        ops = self.ops
        order = self.schedule() if reorder else list(range(len(ops)))
        wait_deps = []
        needed = set()
        for idx, o in enumerate(ops):
            wd = []
            for d, kind in o['alldeps'].items():
                po = ops[d]
                if (not po['dma']) and (not o['dma']) and po['eng'] == o['eng'] and kind != 'raw':
                    continue
                wd.append(d)
                needed.add(d)
            wait_deps.append(wd)
        for d in final_wait_ops:
            needed.add(d)
        eng_sem = {}
        eng_cnt = {}
        dma_sems = {}
        dma_cnt = {}
        ticket = {}
        waited = {k: {} for k in self.engs}

        def esem(eng):
            if eng not in eng_sem:
                eng_sem[eng] = nc.alloc_semaphore(name='c_' + eng)
                eng_cnt[eng] = 0
            return eng_sem[eng]

        import os
        embed_ok = os.environ.get('EMBED_WAIT', '1') == '1'
        snap = {}
        nwait = [0, 0]

        pending_embed = []

        def do_wait(eng, sem, val, embed=False):
            w = waited[eng]
            if w.get(sem.num, 0) >= val:
                nwait[1] += 1
                return
            if embed:
                pending_embed.append((sem, val))
            else:
                self.engs[eng].wait_ge(sem, val)
            nwait[0] += 1
            w[sem.num] = val
            k = snap.get((sem.num, val))
            if k:
                for sn, v in k.items():
                    if w.get(sn, 0) < v:
                        w[sn] = v

        for idx in order:
            o = ops[idx]
            eng = o['eng']
            wmax = {}
            wsrc = {}
            for d in wait_deps[idx]:
                sem, val = ticket[d]
                if wmax.get(sem.num, (None, 0))[1] < val:
                    wmax[sem.num] = (sem, val)
                    wsrc[sem.num] = d
            keys = sorted(wmax, key=lambda kq: -wsrc[kq])
            wloc = waited[eng]
            need = []
            for k in keys:
                sem_k, val_k = wmax[k]
                if wloc.get(sem_k.num, 0) >= val_k:
                    nwait[1] += 1
                    continue
                need.append((sem_k, val_k))
                wloc[sem_k.num] = val_k
                kn = snap.get((sem_k.num, val_k))
                if kn:
                    for sn, v in kn.items():
                        if wloc.get(sn, 0) < v:
                            wloc[sn] = v
            nwait[0] += len(need)
            for (sem_k, val_k) in reversed(need[1:]):
                self.engs[eng].wait_ge(sem_k, val_k)
            if need:
                if embed_ok and not o['dma']:
                    pending_embed.append(need[0])
                else:
                    self.engs[eng].wait_ge(need[0][0], need[0][1])
            if o['dma']:
                if eng not in dma_sems:
                    dma_sems[eng] = [nc.alloc_semaphore(name='d_%s_%d' % (eng, i))
                                     for i in range(self.NSLOT)]
                    dma_cnt[eng] = 0
                c = dma_cnt[eng]
                dma_cnt[eng] = c + 1
                sem = dma_sems[eng][c % self.NSLOT]
                rnd = c // self.NSLOT
                if rnd > 0:
                    do_wait(eng, sem, 16 * rnd)
                inst = o['fn']()
                inst.then_inc(sem, 16)
                ticket[idx] = (sem, 16 * (rnd + 1))
                snap[(sem.num, 16 * (rnd + 1))] = dict(waited[eng])
            else:
                inst = o['fn']()
                if pending_embed:
                    sem_e, val_e = pending_embed.pop()
                    inst._wait_ge(sem_e, val_e)
                if idx in needed:
                    sem = esem(eng)
                    eng_cnt[eng] += 1
                    inst.then_inc(sem, 1)
                    ticket[idx] = (sem, eng_cnt[eng])
                    snap[(sem.num, eng_cnt[eng])] = dict(waited[eng])
        for d in final_wait_ops:
            sem, val = ticket[d]
            do_wait('sp', sem, val)
        self.stats = {'n_ops': len(ops), 'eng_cnt': dict(eng_cnt), 'dma_cnt': dict(dma_cnt), 'waits': nwait[0],
                      'waits_skipped': nwait[1]}
        return self.stats


class Arena:
    def __init__(self, S, nbytes, name='arena'):
        self.t = S.sbuf(name, [128, nbytes // 4], F32)
        self.n = nbytes
        self.top = 0

    def mark(self):
        return self.top

    def release(self, m):
        self.top = m

    def alloc(self, free_shape, dt):
        es = _dtsize(dt)
        n = 1
        for d in free_shape:
            n *= d
        nb = (n * es + 31) // 32 * 32
        off = self.top
        assert off + nb <= self.n, "arena overflow %d + %d > %d" % (off, nb, self.n)
        self.top = off + nb
        v = self.t[:, off // 4:(off + nb) // 4]
        if dt != F32:
            v = v.bitcast(dt)
        v = v[:, 0:n]
        if len(free_shape) == 2:
            v = v.rearrange("p (a b) -> p a b", a=free_shape[0])
        elif len(free_shape) == 3:
            v = v.rearrange("p (a b c) -> p a b c", a=free_shape[0], b=free_shape[1])
        return v


T = 4096
D = 1024
NT = T // 128
DFF = 2816
EPS = 1e-6


class Ctx:
    pass


def setup(nc, phases_debug=None):
    C = Ctx()
    S = Sched(nc)
    C.S = S
    C.nc = nc
    A = Arena(S, 200 * 1024)
    C.A = A
    C.ps = [S.psum("ps%d" % i, [128, 512], F32) for i in range(8)]
    C.psi = 0
    C.norm_i = 0
    C.psi6 = 0
    C.ident = A.alloc([128], F32)
    C.identb = A.alloc([128], BF16)
    C.ones = A.alloc([128], F32)
    C.epsb = A.alloc([1], F32)
    C.tril = A.alloc([128], F32)
    C.triu = A.alloc([128], F32)
    C.triu_s = A.alloc([128], F32)
    g = nc.gpsimd
    S.memset(C.ident, 0.0)
    S.op('pool', lambda: g.affine_select(out=C.ident, in_=C.ident, compare_op=ALU.not_equal, fill=1.0,
                                         base=0, pattern=[[-1, 128]], channel_multiplier=1),
         [C.ident], [C.ident])
    S.copy(C.identb, C.ident)
    S.memset(C.ones, 1.0)
    C.onesb = A.alloc([128], BF16)
    S.copy(C.onesb, C.ones)
    S.memset(C.epsb, EPS)
    S.memset(C.tril, 1.0)
    S.op('pool', lambda: g.affine_select(out=C.tril, in_=C.tril, compare_op=ALU.is_ge, fill=0.0,
                                         base=0, pattern=[[-1, 128]], channel_multiplier=1),
         [C.tril], [C.tril])
    S.memset(C.triu, 1.0)
    S.op('pool', lambda: g.affine_select(out=C.triu, in_=C.triu, compare_op=ALU.is_ge, fill=0.0,
                                         base=0, pattern=[[1, 128]], channel_multiplier=-1),
         [C.triu], [C.triu])
    S.memset(C.triu_s, 1.0)
    S.op('pool', lambda: g.affine_select(out=C.triu_s, in_=C.triu_s, compare_op=ALU.is_gt, fill=0.0,
                                         base=0, pattern=[[1, 128]], channel_multiplier=-1),
         [C.triu_s], [C.triu_s])
    C.base_mark = A.mark()
    C.cvt_i = 0
    return C


def next_ps6(C):
    p = C.ps[C.psi6 % 6]
    C.psi6 += 1
    return p


def next_ps(C):
    p = C.ps[C.psi % 7]
    C.psi += 1
    return p


def load_w_bf16(C, w2d, rows, cols, dst=None, stage=None):
    S, A = C.S, C.A
    kc_n = rows // 128
    if dst is None:
        dst = A.alloc([kc_n, cols], BF16)
    CH = 2048
    for kc in range(kc_n):
        for c0 in range(0, cols, CH):
            c1 = min(cols, c0 + CH)
            st = stage[C.cvt_i % len(stage)]
            S.dma(st[:, 0:c1 - c0], w2d[kc * 128:(kc + 1) * 128, c0:c1])
            eng = ('act', 'dve', 'act', 'dve', 'act', 'dve', 'pool')[C.cvt_i % 7]
            S.copy(dst[:, kc, c0:c1], st[:, 0:c1 - c0], eng=eng)
            C.cvt_i += 1
    return dst


def load_cols(C, row_ap, n, dst):
    S = C.S
    nj = n // 128
    src = row_ap.rearrange("(j p) -> p j", p=128)
    step = 16
    for j0 in range(0, nj, step):
        j1 = min(nj, j0 + step)
        S.dma(dst[:, j0:j1], src[:, j0:j1], allow_slow_non_contiguous=True)


def norm_T(C, xt, gb, hb, hT, col0, sq, ss, rstd):
    S = C.S
    if isinstance(hb, list):
        k = C.norm_i % len(hb)
        C.norm_i += 1
        hb, sq, ss, rstd = hb[k], sq[k], ss[k], rstd[k]
    S.act(sq, xt, AF.Square, accum_out=ss)
    S.act(rstd, ss, AF.Ln, scale=1.0 / D, bias=C.epsb)
    S.act(rstd, rstd, AF.Exp, scale=-0.5)
    S.stt(hb, xt, rstd, gb, ALU.mult, ALU.mult)
    pst = next_ps(C)
    pv = pst[:, :].bitcast(BF16)
    for c in range(8):
        S.transpose(pv[:, c * 128:(c + 1) * 128], hb[:, c * 128:(c + 1) * 128], C.identb)
    S.copy(hT[:, :, col0:col0 + 128], pv.rearrange("p (c t) -> p c t", c=8))


def ffn_phase(C, xsrc, xdst, g_norm, w_up, conv_w, conv_b, w_down):
    S, A = C.S, C.A
    m0 = A.mark()
    wup = A.alloc([22, 8, 256], BF16)
    wdn = A.alloc([22, D], BF16)
    gb = A.alloc([D], F32)
    S.dma(gb, g_norm.partition_broadcast(128))
    cw = A.alloc([3, 44], F32)
    cb = A.alloc([44], F32)
    for i in range(3):
        load_cols(C, conv_w[i, :], 2 * DFF, cw[:, i, :])
    load_cols(C, conv_b, 2 * DFF, cb)
    TB = 256
    NTT = TB // 128
    hTs = [A.alloc([8, TB + 2], BF16) for _ in range(2)]
    xts = [A.alloc([D], F32) for _ in range(2)]
    sq = A.alloc([D], BF16)
    hb = A.alloc([D], BF16)
    ss = A.alloc([1], F32)
    rstd = A.alloc([1], F32)
    Cc = [[A.alloc([TB], F32) for _ in range(2)] for _ in range(2)]
    sg = [A.alloc([TB], F32) for _ in range(2)]
    aj = [A.alloc([TB], BF16) for _ in range(4)]
    ysb = [A.alloc([D], F32) for _ in range(2)]
    acc = [[C.ps[4 + tt * 2 + nh] for nh in range(2)] for tt in range(NTT)]
    rot = {'i': 0}

    def rps():
        p = C.ps[rot['i'] % 4]
        rot['i'] += 1
        return p

    def norm_block(tb):
        hT = hTs[tb % 2]
        if tb == 0:
            S.memset(hT[:, :, 0:2], 0.0)
        else:
            S.copy(hT[:, :, 0:2], hTs[(tb - 1) % 2][:, :, TB:TB + 2], eng='pool')
        for tt in range(NTT):
            t = tb * NTT + tt
            xt = xts[t % 2]
            S.dma(xt, xsrc[t * 128:(t + 1) * 128, :])
            S.act(sq, xt, AF.Square, accum_out=ss)
            S.act(rstd, ss, AF.Ln, scale=1.0 / D, bias=C.epsb)
            S.act(rstd, rstd, AF.Exp, scale=-0.5)
            S.stt(hb, xt, rstd, gb, ALU.mult, ALU.mult)
            pst = rps()
            pv = pst[:, :].bitcast(BF16)
            for c in range(8):
                S.transpose(pv[:, c * 128:(c + 1) * 128], hb[:, c * 128:(c + 1) * 128], C.identb)
            S.copy(hT[:, :, 2 + tt * 128:2 + (tt + 1) * 128], pv.rearrange("p (c t) -> p c t", c=8))

    outs = []
    NB = T // TB
    norm_block(0)
    m1 = A.mark()
    stage = [A.alloc([2048], F32) for _ in range(3)]
    wup_src = w_up.rearrange("(kc p) c -> p kc c", p=128)
    for jp in range(11):
        for half in range(2):
            st = stage[C.cvt_i % 3]
            c0 = half * DFF + jp * 256
            S.dma(st.rearrange("p (a b) -> p a b", a=8), wup_src[:, :, c0:c0 + 256])
            eng = ('act', 'dve', 'act', 'dve', 'act', 'dve', 'pool')[C.cvt_i % 7]
            S.copy(wup[:, half * 11 + jp, :, :], st.rearrange("p (a b) -> p a b", a=8), eng=eng)
            C.cvt_i += 1
        st = stage[C.cvt_i % 3]
        S.dma(st.rearrange("p (a b) -> p a b", a=2),
              w_down[jp * 256:(jp + 1) * 256, :].rearrange("(a p) c -> p a c", p=128))
        eng = ('act', 'dve', 'act', 'dve', 'act', 'dve', 'pool')[C.cvt_i % 7]
        S.copy(wdn[:, 2 * jp:2 * jp + 2, :], st.rearrange("p (a b) -> p a b", a=2), eng=eng)
        C.cvt_i += 1
    A.release(m1)
    for tb in range(NB):
        hT = hTs[tb % 2]
        for j in range(22):
            cc = []
            for half in range(2):
                jj = half * 22 + j
                col0 = half * DFF + j * 128
                ps = rps()
                for kc in range(8):
                    S.mm(ps[:, 0:TB + 2], wup[:, half * 11 + j // 2, kc, (j % 2) * 128:(j % 2 + 1) * 128], hT[:, kc, :],
                         start=(kc == 0), stop=(kc == 7))
                c = Cc[j % 2][half]
                S.act(c, ps[:, 2:TB + 2], AF.Identity, scale=cw[:, 2, jj:jj + 1], bias=cb[:, jj:jj + 1])
                S.stt(c, ps[:, 1:TB + 1], cw[:, 1, jj:jj + 1], c, ALU.mult, ALU.add)
                S.stt(c, ps[:, 0:TB], cw[:, 0, jj:jj + 1], c, ALU.mult, ALU.add)
                cc.append(c)
            sg_ = sg[j % 2]
            a_ = aj[j % 4]
            S.act(sg_, cc[0], AF.Silu)
            S.tt(a_, sg_, cc[1], ALU.mult, eng='pool')

            def down(jd):
                ad = aj[jd % 4]
                for tt in range(NTT):
                    for nh in range(2):
                        S.mm(acc[tt][nh][:, :], ad[:, tt * 128:(tt + 1) * 128], wdn[:, jd, nh * 512:(nh + 1) * 512],
                             start=(jd == 0), stop=(jd == 21))
            if j >= 2:
                down(j - 2)
            if j == 21:
                down(20)
                down(21)
            if j == 10 and tb + 1 < NB:
                norm_block(tb + 1)
        for tt in range(NTT):
            t = tb * NTT + tt
            xt = xts[t % 2]
            S.dma(xt, xsrc[t * 128:(t + 1) * 128, :])
            y = ysb[t % 2]
            for nh in range(2):
                S.tt(y[:, nh * 512:(nh + 1) * 512], acc[tt][nh][:, :], xt[:, nh * 512:(nh + 1) * 512], ALU.add)
            outs.append(S.dma(xdst[t * 128:(t + 1) * 128, :], y, q='pool'))
    A.release(m0)
    return outs


def xa_phase(C, xsrc, xdst, mem, g_xa, g_mem, w_q, w_kv, w_o, g_q, g_k):
    S, A = C.S, C.A
    m0 = A.mark()
    wq = A.alloc([8, D], BF16)
    wkv = A.alloc([8, 2 * D], BF16)
    wo = A.alloc([8, D], BF16)
    m1 = A.mark()
    stage = [A.alloc([2048], F32) for _ in range(4)]
    load_w_bf16(C, w_q, D, D, dst=wq, stage=stage)
    load_w_bf16(C, w_kv, D, 2 * D, dst=wkv, stage=stage)
    load_w_bf16(C, w_o, D, D, dst=wo, stage=stage)
    A.release(m1)
    gb = A.alloc([D], F32)
    gmb = A.alloc([D], F32)
    S.dma(gb, g_xa.partition_broadcast(128))
    S.dma(gmb, g_mem.partition_broadcast(128))
    gq = A.alloc([2], F32)
    gk = A.alloc([2], F32)
    load_cols(C, g_q, 256, gq)
    load_cols(C, g_k, 256, gk)
    onesb = A.alloc([128], BF16)
    S.copy(onesb, C.ones)
    TB = 256
    xts = [A.alloc([D], F32) for _ in range(2)]
    sqj = [A.alloc([D], BF16) for _ in range(2)]
    hb = [A.alloc([D], BF16) for _ in range(2)]
    ss = [A.alloc([1], F32) for _ in range(2)]
    rstd1 = [A.alloc([1], F32) for _ in range(2)]
    memT = A.alloc([8, 256], BF16)
    kT = A.alloc([8, 256], BF16)
    V = A.alloc([2, D], BF16)
    hT = A.alloc([8, TB], BF16)
    qraw = [A.alloc([TB], F32) for _ in range(2)]
    sq = [A.alloc([TB], F32) for _ in range(2)]
    rstd = A.alloc([TB], F32)
    qn = [A.alloc([TB], BF16) for _ in range(2)]
    ex = [A.alloc([TB], BF16) for _ in range(2)]
    rinv = A.alloc([TB], F32)
    oT = A.alloc([8, TB], BF16)
    ysb = A.alloc([D], F32)
    for mt in range(2):
        xt = xts[mt]
        S.dma(xt, mem[mt * 128:(mt + 1) * 128, :])
        norm_T(C, xt, gmb, hb, memT, mt * 128, sqj, ss, rstd1)
    for h in range(4):
        for c in range(2):
            ch = 2 * h + c
            ps = next_ps(C)
            for kc in range(8):
                S.mm(ps[:, 0:256], wkv[:, kc, ch * 128:(ch + 1) * 128], memT[:, kc, :], start=(kc == 0), stop=(kc == 7))
            S.copy(qraw[c][:, 0:256], ps[:, 0:256], eng='act')
            S.act(sq[c][:, 0:256], ps[:, 0:256], AF.Square)
        ps = next_ps(C)
        for c in range(2):
            S.mm(ps[:, 0:256], C.ones, sq[c][:, 0:256], start=(c == 0), stop=(c == 1))
        S.act(rstd[:, 0:256], ps[:, 0:256], AF.Ln, scale=1.0 / 256, bias=C.epsb)
        S.act(rstd[:, 0:256], rstd[:, 0:256], AF.Exp, scale=-0.5)
        for c in range(2):
            S.stt(kT[:, 2 * h + c, :], qraw[c][:, 0:256], gk[:, c:c + 1], rstd[:, 0:256], ALU.mult, ALU.mult)
    for mt in range(2):
        for nh in range(2):
            ps = next_ps(C)
            for kc in range(8):
                S.mm(ps[:, :], memT[:, kc, mt * 128:(mt + 1) * 128], wkv[:, kc, D + nh * 512:D + (nh + 1) * 512],
                     start=(kc == 0), stop=(kc == 7))
            S.copy(V[:, mt, nh * 512:(nh + 1) * 512], ps[:, :], eng='act')
    outs = []
    NTT = TB // 128
    NB = T // TB
    hTs = [hT, A.alloc([8, TB], BF16)]
    oTs = [oT, A.alloc([8, TB], BF16)]
    xtb = [[A.alloc([D], F32) for _ in range(NTT)] for _ in range(3)]
    ysbs = [ysb, A.alloc([D], F32)]

    class HB:
        pass
    HBs = []
    for h in range(4):
        b_ = HB()
        b_.qraw = [A.alloc([TB], F32) for _ in range(2)]
        b_.sq = [A.alloc([TB], BF16) for _ in range(2)]
        b_.rstd = A.alloc([TB], F32)
        b_.qn = [A.alloc([TB], BF16) for _ in range(2)]
        b_.ex = [A.alloc([TB], BF16) for _ in range(2)]
        b_.rinv = A.alloc([TB], F32)
        HBs.append(b_)

    def front(tb):
        for tt in range(NTT):
            t = tb * NTT + tt
            xt = xtb[tb % 3][tt]
            S.dma(xt, xsrc[t * 128:(t + 1) * 128, :])
            norm_T(C, xt, gb, hb, hTs[tb % 2], tt * 128, sqj, ss, rstd1)
            yield

    def head(tb, h):
        B = HBs[h]
        hT_ = hTs[tb % 2]
        oT_ = oTs[tb % 2]
        for c in range(2):
            ch = 2 * h + c
            ps = next_ps(C)
            for kc in range(8):
                S.mm(ps[:, 0:TB], wq[:, kc, ch * 128:(ch + 1) * 128], hT_[:, kc, :], start=(kc == 0), stop=(kc == 7))
            S.copy(B.qraw[c], ps[:, 0:TB], eng='act')
            S.act(B.sq[c], ps[:, 0:TB], AF.Square)
            yield
        ps = next_ps(C)
        for c in range(2):
            S.mm(ps[:, 0:TB], C.onesb, B.sq[c], start=(c == 0), stop=(c == 1))
        S.act(B.rstd, ps[:, 0:TB], AF.Ln, scale=1.0 / 256, bias=C.epsb)
        yield
        S.act(B.rstd, B.rstd, AF.Exp, scale=-0.5)
        for c in range(2):
            S.stt(B.qn[c], B.qraw[c], gq[:, c:c + 1], B.rstd, ALU.mult, ALU.mult)
        yield
        for mc in range(2):
            ps = next_ps(C)
            for c in range(2):
                S.mm(ps[:, 0:TB], kT[:, 2 * h + c, mc * 128:(mc + 1) * 128], B.qn[c], start=(c == 0), stop=(c == 1))
            S.act(B.ex[mc], ps[:, 0:TB], AF.Exp, scale=1.0 / 16)
        yield
        ps = next_ps(C)
        for mc in range(2):
            S.mm(ps[:, 0:TB], onesb, B.ex[mc], start=(mc == 0), stop=(mc == 1))
        S.act(B.rinv, ps[:, 0:TB], AF.Ln)
        S.act(B.rinv, B.rinv, AF.Exp, scale=-1.0)
        yield
        for c in range(2):
            ps = next_ps(C)
            for mc in range(2):
                S.mm(ps[:, 0:TB], V[:, mc, h * 256 + c * 128:h * 256 + (c + 1) * 128], B.ex[mc],
                     start=(mc == 0), stop=(mc == 1))
            S.tt(oT_[:, 2 * h + c, :], ps[:, 0:TB], B.rinv, ALU.mult)
        yield

    def back(tb):
        oT_ = oTs[tb % 2]
        for tt in range(NTT):
            t = tb * NTT + tt
            xt = xtb[tb % 3][tt]
            y = ysbs[t % 2]
            for nh in range(2):
                ps = next_ps(C)
                for kc in range(8):
                    S.mm(ps[:, :], oT_[:, kc, tt * 128:(tt + 1) * 128], wo[:, kc, nh * 512:(nh + 1) * 512],
                         start=(kc == 0), stop=(kc == 7))
                S.tt(y[:, nh * 512:(nh + 1) * 512], ps[:, :], xt[:, nh * 512:(nh + 1) * 512], ALU.add)
            outs.append(S.dma(xdst[t * 128:(t + 1) * 128, :], y, q='pool'))
            yield

    def rr(gens):
        live = list(gens)
        while live:
            nxt = []
            for g_ in live:
                try:
                    next(g_)
                    nxt.append(g_)
                except StopIteration:
                    pass
            live = nxt

    rr([front(0)])
    for tb in range(NB):
        gens = [head(tb, h) for h in range(4)]
        if tb + 1 < NB:
            gens.append(front(tb + 1))
        if tb >= 1:
            gens.append(back(tb - 1))
        rr(gens)
    rr([back(NB - 1)])
    A.release(m0)
    return outs


def gla_phase(C, xsrc, xdst, g_norm, w_in, w_gk, b_gk, norm_g, w_out):
    S, A = C.S, C.A
    S.rank_mode = True
    m0 = A.mark()
    NIN = 3088
    win = A.alloc([8, NIN], BF16)
    wout = A.alloc([8, D], BF16)
    m1 = A.mark()
    stage = [A.alloc([2048], F32) for _ in range(4)]
    load_w_bf16(C, w_in, D, NIN, dst=win, stage=stage)
    load_w_bf16(C, w_out, D, D, dst=wout, stage=stage)
    A.release(m1)
    gb = A.alloc([D], F32)
    S.dma(gb, g_norm.partition_broadcast(128))
    bgk = A.alloc([512], F32)
    S.dma(bgk, b_gk.partition_broadcast(128))
    ngb = A.alloc([256], F32)
    S.dma(ngb, norm_g.partition_broadcast(128))
    wgk32 = A.alloc([512], F32)
    S.dma(wgk32[0:16, :], w_gk)
    wgk = A.alloc([512], BF16)
    S.copy(wgk[0:16, :], wgk32[0:16, :])
    xts = [A.alloc([D], F32) for _ in range(4)]
    sqj = [A.alloc([D], BF16) for _ in range(2)]
    hb = [A.alloc([D], BF16) for _ in range(2)]
    ss = [A.alloc([1], F32) for _ in range(2)]
    rstd1 = [A.alloc([1], F32) for _ in range(2)]
    hTs = [A.alloc([8, 128], BF16) for _ in range(2)]
    lrT = A.alloc([128], BF16)
    zb = A.alloc([512], F32)
    Lgs = [A.alloc([512], F32) for _ in range(2)]
    vtms = [A.alloc([D], BF16) for _ in range(3)]
    rss = [A.alloc([D], F32) for _ in range(4)]

    class HB:
        pass
    HBs = []
    for h in range(4):
        b_ = HB()
        b_.BC = A.alloc([128], F32)
        b_.nbm = A.alloc([1], F32)
        b_.eb = A.alloc([1], F32)
        b_.E = [A.alloc([128], F32) for _ in range(4)]
        for nm in ("qe", "ke", "kdT"):
            setattr(b_, nm, A.alloc([128], BF16))
        b_.hand = []
        for par in range(2):
            hd = HB()
            for nm in ("qg", "kd", "att"):
                setattr(hd, nm, A.alloc([128], BF16))
            hd.eb = A.alloc([1], F32)
            b_.hand.append(hd)
        HBs.append(b_)
    St = [A.alloc([256], F32) for _ in range(4)]
    Sb = [A.alloc([256], BF16) for _ in range(4)]
    o_alls = [A.alloc([D], F32) for _ in range(2)]
    oss = A.alloc([4], F32)
    orstd = A.alloc([4], F32)
    ob = A.alloc([D], BF16)
    oT = A.alloc([8, 128], BF16)
    ysbs = [A.alloc([D], F32) for _ in range(2)]
    for h in range(4):
        S.memset(St[h], 0.0)
        S.memset(Sb[h], 0.0)
    outs = []
    sc = 128.0 ** -0.5

    def front(t):
        xt = xts[t % 4]
        hT, Lg, vtm, rs = hTs[t % 2], Lgs[t % 2], vtms[t % 3], rss[t % 4]
        S.dma(xt, xsrc[t * 128:(t + 1) * 128, :])
        norm_T(C, xt, gb, hb, hT, 0, sqj, ss, rstd1)
        yield
        ps = next_ps(C)
        for kc in range(8):
            S.mm(ps[0:16, 0:128], win[:, kc, 3072:3088], hT[:, kc, :], start=(kc == 0), stop=(kc == 7))
        S.copy(lrT[0:16, :], ps[0:16, 0:128], eng='act')
        yield
        ps = next_ps(C)
        S.mm(ps[:, :], lrT[0:16, :], wgk[0:16, :])
        S.tt(zb, ps[:, :], bgk, ALU.add)
        S.act(zb, zb, AF.Exp, scale=-1.0)
        S.act(Lg, zb, AF.Ln, bias=C.ones[:, 0:1])
        yield
        for nh in range(2):
            ps = next_ps(C)
            for kc in range(8):
                S.mm(ps[:, :], hT[:, kc, :], win[:, kc, 1024 + nh * 512:1024 + (nh + 1) * 512], start=(kc == 0), stop=(kc == 7))
            S.copy(vtm[:, nh * 512:(nh + 1) * 512], ps[:, :], eng='act')
            yield
        for nh in range(2):
            ps = next_ps(C)
            for kc in range(8):
                S.mm(ps[:, :], hT[:, kc, :], win[:, kc, 2048 + nh * 512:2048 + (nh + 1) * 512], start=(kc == 0), stop=(kc == 7))
            S.act(rs[:, nh * 512:(nh + 1) * 512], ps[:, :], AF.Silu)
            yield

    def headA(t, h):
        B = HBs[h]
        Hd = B.hand[t % 2]
        hT, Lg = hTs[t % 2], Lgs[t % 2]
        E = B.E
        ps = next_ps(C)
        S.mm(ps[:, 0:128], Lg[:, h * 128:(h + 1) * 128], C.triu)
        S.act(B.BC, ps[:, 0:128], AF.Copy, scale=-1.0 / 16)
        yield
        S.act(B.nbm, B.BC[:, 64:65], AF.Copy, scale=-1.0)
        S.act(E[0], B.BC, AF.Exp, bias=B.nbm)
        S.act(E[1], B.BC, AF.Exp, scale=-1.0, bias=B.BC[:, 64:65])
        yield
        S.act(E[2], B.BC, AF.Exp)
        S.act(E[3], B.BC, AF.Exp, scale=-1.0, bias=B.BC[:, 127:128])
        S.act(Hd.eb, B.BC[:, 127:128], AF.Exp)
        yield
        psq = next_ps(C)
        for kc in range(8):
            S.mm(psq[:, 0:128], win[:, kc, h * 128:(h + 1) * 128], hT[:, kc, :], start=(kc == 0), stop=(kc == 7))
        S.stt(B.qe, psq[:, 0:128], sc, E[0], ALU.mult, ALU.mult)
        S.stt(Hd.qg, psq[:, 0:128], sc, E[2], ALU.mult, ALU.mult)
        yield
        psk = next_ps(C)
        for kc in range(8):
            S.mm(psk[:, 0:128], win[:, kc, 512 + h * 128:512 + (h + 1) * 128], hT[:, kc, :], start=(kc == 0), stop=(kc == 7))
        S.tt(B.ke, psk[:, 0:128], E[1], ALU.mult)
        S.tt(B.kdT, psk[:, 0:128], E[3], ALU.mult)
        yield
        pst = next_ps(C)
        ptv = pst[:, :].bitcast(BF16)
        S.transpose(ptv[:, 0:128], B.kdT, C.identb)
        S.copy(Hd.kd, ptv[:, 0:128], eng='act')
        psa = next_ps(C)
        S.mm(psa[:, 0:128], B.ke, B.qe)
        S.tt(Hd.att, psa[:, 0:128], C.triu, ALU.mult)
        yield

    def headB(t, h):
        B = HBs[h]
        Hd = B.hand[t % 2]
        vtm, o_all = vtms[t % 3], o_alls[t % 2]
        pso = next_ps(C)
        S.mm(pso[:, 0:256], Hd.att, vtm[:, h * 256:(h + 1) * 256], start=True, stop=False)
        S.mm(pso[:, 0:256], Hd.qg, Sb[h], start=False, stop=True)
        S.copy(o_all[:, h * 256:(h + 1) * 256], pso[:, 0:256], eng='act')
        pss = next_ps(C)
        S.mm(pss[:, 0:256], Hd.kd, vtm[:, h * 256:(h + 1) * 256])
        S.stt(St[h], St[h], Hd.eb, pss[:, 0:256], ALU.mult, ALU.add)
        S.copy(Sb[h], St[h], eng='pool')
        yield

    def back(t):
        xt = xts[t % 4]
        o_all, rs, ysb = o_alls[t % 2], rss[t % 4], ysbs[t % 2]
        for h in range(4):
            S.act(ob[:, h * 256:(h + 1) * 256], o_all[:, h * 256:(h + 1) * 256], AF.Square, accum_out=oss[:, h:h + 1])
        S.act(orstd, oss, AF.Ln, scale=1.0 / 256, bias=C.epsb)
        S.act(orstd, orstd, AF.Exp, scale=-0.5)
        yield
        for h in range(4):
            S.stt(o_all[:, h * 256:(h + 1) * 256], o_all[:, h * 256:(h + 1) * 256], orstd[:, h:h + 1], ngb, ALU.mult, ALU.mult)
        S.tt(ob, o_all, rs, ALU.mult)
        yield
        pst = next_ps(C)
        pv = pst[:, :].bitcast(BF16)
        for c in range(8):
            S.transpose(pv[:, c * 128:(c + 1) * 128], ob[:, c * 128:(c + 1) * 128], C.identb)
        S.copy(oT, pv.rearrange("p (c t) -> p c t", c=8))
        yield
        for nh in range(2):
            ps = next_ps(C)
            for kc in range(8):
                S.mm(ps[:, :], oT[:, kc, :], wout[:, kc, nh * 512:(nh + 1) * 512], start=(kc == 0), stop=(kc == 7))
            S.tt(ysb[:, nh * 512:(nh + 1) * 512], ps[:, :], xt[:, nh * 512:(nh + 1) * 512], ALU.add)
        outs.append(S.dma(xdst[t * 128:(t + 1) * 128, :], ysb, q='pool'))
        yield

    def rr(gens):
        live = list(gens)
        while live:
            nxt = []
            for g_ in live:
                try:
                    next(g_)
                    nxt.append(g_)
                except StopIteration:
                    pass
            live = nxt

    rr([front(0)])
    rr([headA(0, h) for h in range(4)] + [front(1)])
    for t in range(NT):
        gens = []
        if t >= 1:
            gens.append(back(t - 1))
        gens += [headB(t, h) for h in range(4)]
        if t + 1 < NT:
            gens += [headA(t + 1, h) for h in range(4)]
        if t + 2 < NT:
            gens.append(front(t + 2))
        rr(gens)
    rr([back(NT - 1)])
    S.rank_mode = False
    A.release(m0)
    return outs


def dn_phase(C, xsrc, xdst, g_norm, w_in, conv_w, a_log, dt_bias, norm_g, w_out):
    S, A = C.S, C.A
    S.rank_mode = True
    m0 = A.mark()
    NA = 2056
    win = A.alloc([8, NA], BF16)
    wout = A.alloc([4, D], BF16)
    m1 = A.mark()
    stage = [A.alloc([2048], F32) for _ in range(4)]
    load_w_bf16(C, w_in[:, 0:NA], D, NA, dst=win, stage=stage)
    load_w_bf16(C, w_out[0:512, :], 512, D, dst=wout, stage=stage)
    A.release(m1)
    gb = A.alloc([D], F32)
    S.dma(gb, g_norm.partition_broadcast(128))
    cw = A.alloc([4, 12], F32)
    for i in range(4):
        load_cols(C, conv_w[i, :], 1536, cw[:, i, :])
    ngb = A.alloc([128], F32)
    S.dma(ngb, norm_g.partition_broadcast(128))
    dtb = A.alloc([4], F32)
    S.dma(dtb, dt_bias.partition_broadcast(128))
    nea = A.alloc([4], F32)
    S.dma(nea, a_log.partition_broadcast(128))
    S.act(nea, nea, AF.Exp)
    S.ts(nea, nea, -1.0, None, ALU.mult)
    sqj = [A.alloc([D], BF16) for _ in range(2)]
    hb = [A.alloc([D], BF16) for _ in range(2)]
    ss = [A.alloc([1], F32) for _ in range(2)]
    rstd1 = [A.alloc([1], F32) for _ in range(2)]
    hT = A.alloc([8, 128], BF16)
    U = A.alloc([12, 131], F32)
    S.memset(U, 0.0)
    cvs = [A.alloc([12, 128], F32) for _ in range(2)]
    sqf = A.alloc([128], F32)
    rinv = A.alloc([128], F32)
    qns = [[A.alloc([128], F32) for _ in range(4)] for _ in range(3)]
    kns = [[A.alloc([128], F32) for _ in range(4)] for _ in range(2)]
    ab = A.alloc([8], F32)
    betas = [A.alloc([4], F32) for _ in range(2)]
    ggs = [A.alloc([4], F32) for _ in range(2)]
    zss = [A.alloc([512], F32) for _ in range(4)]
    sq8 = A.alloc([8, 128], F32)
    rinv8 = A.alloc([8, 128], F32)

    class HB:
        pass
    HBs = []
    for h in range(4):
        b_ = HB()
        for nm in ("ktm", "vtm", "tmp", "arg", "DT", "AT0", "Am", "ATm", "P0", "P1", "PT0", "PT1", "TT", "vb", "kbg",
                   "delta", "o1"):
            setattr(b_, nm, A.alloc([128], F32))
        for nm in ("gcc", "beg"):
            setattr(b_, nm, A.alloc([1], F32))
        b_.hand = []
        for par in range(2):
            hd = HB()
            for nm in ("usb", "wT", "qkm", "kdec"):
                setattr(hd, nm, A.alloc([128], F32))
            for nm in ("egc", "egl"):
                setattr(hd, nm, A.alloc([1], F32))
            b_.hand.append(hd)
        HBs.append(b_)
    St = [A.alloc([128], F32) for _ in range(4)]
    o_as = [A.alloc([512], F32) for _ in range(2)]
    oss = A.alloc([4], F32)
    orstd = A.alloc([4], F32)
    ob = A.alloc([512], BF16)
    oT = A.alloc([4, 128], BF16)
    ysbs = [A.alloc([D], F32) for _ in range(2)]
    xts = [A.alloc([D], F32) for _ in range(4)]
    for h in range(4):
        S.memset(St[h], 0.0)
    outs = []
    sc = 128.0 ** -0.5
    def front(t):
        xt = xts[t % 4]
        cv, qn, kn, beta, gg, zs = cvs[t % 2], qns[t % 3], kns[t % 2], betas[t % 2], ggs[t % 2], zss[t % 4]
        S.dma(xt, xsrc[t * 128:(t + 1) * 128, :])
        norm_T(C, xt, gb, hb, hT, 0, sqj, ss, rstd1)
        for c in range(12):
            ps = next_ps(C)
            for kc in range(8):
                S.mm(ps[:, 0:128], win[:, kc, c * 128:(c + 1) * 128], hT[:, kc, :], start=(kc == 0), stop=(kc == 7))
            S.copy(U[:, c, 3:131], ps[:, 0:128], eng='act')
            S.act(cv[:, c, :], ps[:, 0:128], AF.Copy, scale=cw[:, 3, c:c + 1])
            for i in range(3):
                S.stt(cv[:, c, :], U[:, c, i:i + 128], cw[:, i, c:c + 1], cv[:, c, :], ALU.mult, ALU.add)
            S.copy(U[:, c, 0:3], U[:, c, 128:131], eng='pool')
            yield
        S.act(cv, cv, AF.Silu)
        ps = next_ps(C)
        for kc in range(8):
            S.mm(ps[:, 0:8], hT[:, kc, :], win[:, kc, 2048:2056], start=(kc == 0), stop=(kc == 7))
        S.copy(ab, ps[:, 0:8], eng='act')
        ps = next_ps(C)
        for kc in range(8):
            S.mm(ps[:, :], hT[:, kc, :], win[:, kc, 1536:2048], start=(kc == 0), stop=(kc == 7))
        S.act(zs, ps[:, :], AF.Silu)
        yield
        S.act(beta, ab[:, 4:8], AF.Exp, scale=-1.0)
        S.ts(beta, beta, 1.0, None, ALU.add)
        S.recip(beta, beta)
        S.tt(gg, ab[:, 0:4], dtb, ALU.add)
        S.act(gg, gg, AF.Exp)
        S.act(gg, gg, AF.Ln, bias=C.ones[:, 0:1])
        S.tt(gg, gg, nea, ALU.mult)
        yield
        S.act(sq8, cv[:, 0:8, :], AF.Square)
        for half in range(2):
            ps = next_ps(C)
            S.mm(ps[:, :], C.ones, sq8[:, half * 4:(half + 1) * 4, :])
            S.act(rinv8[:, half * 4:(half + 1) * 4, :], ps[:, :].rearrange("p (a b) -> p a b", a=4), AF.Ln, bias=C.epsb)
        S.act(rinv8, rinv8, AF.Exp, scale=-0.5)
        yield
        for h in range(4):
            S.stt(qn[h], cv[:, h, :], sc, rinv8[:, h, :], ALU.mult, ALU.mult)
        for h in range(4):
            S.tt(kn[h], cv[:, 4 + h, :], rinv8[:, 4 + h, :], ALU.mult)

        yield

    def chainA(t, h):
        B = HBs[h]
        Hd = B.hand[t % 2]
        cv, qn, kn, beta, gg = cvs[t % 2], qns[t % 3], kns[t % 2], betas[t % 2], ggs[t % 2]
        ps = next_ps(C)
        S.transpose(ps[:, 0:128], kn[h], C.ident)
        S.copy(B.ktm, ps[:, 0:128], eng='act')
        ps = next_ps(C)
        S.transpose(ps[:, 0:128], cv[:, 8 + h, :], C.ident)
        S.copy(B.vtm, ps[:, 0:128], eng='act')
        yield
        S.ts(B.tmp, C.triu, gg[:, h:h + 1], None, ALU.mult)
        psg = next_ps(C)
        S.mm(psg[:, 0:128], C.ones, B.tmp)
        psc = next_ps(C)
        S.mm(psc[:, 0:1], C.triu, gg[:, h:h + 1])
        S.copy(B.gcc, psc[:, 0:1], eng='act')
        S.act(Hd.egl, psg[:, 127:128], AF.Exp)
        S.ts(B.arg, psg[:, 0:128], B.gcc, 0.0, ALU.subtract, ALU.min)
        yield
        S.act(B.DT, B.arg, AF.Exp)
        S.act(Hd.egc, B.gcc, AF.Exp)
        S.tt(B.beg, beta[:, h:h + 1], Hd.egc, ALU.mult)
        psk = next_ps(C)
        S.mm(psk[:, 0:128], kn[h], kn[h])
        S.tt(B.AT0, psk[:, 0:128], B.DT, ALU.mult)
        S.tt(B.AT0, B.AT0, C.triu_s, ALU.mult, eng='pool')
        yield
        ps = next_ps(C)
        S.transpose(ps[:, 0:128], B.AT0, C.ident)
        S.act(B.Am, ps[:, 0:128], AF.Copy, scale=beta[:, h:h + 1])
        yield
        ps = next_ps(C)
        S.transpose(ps[:, 0:128], B.Am, C.ident)
        S.copy(B.ATm, ps[:, 0:128], eng='act')
        S.tt(B.TT, C.ident, B.ATm, ALU.subtract, eng='pool')
        yield
        pk, pkt = B.ATm, B.Am
        Ps = (B.P0, B.P1)
        PTs = (B.PT0, B.PT1)
        for lvl in range(1, 7):
            npk, npkt = Ps[lvl % 2], PTs[lvl % 2]
            if lvl < 6:
                ps1 = next_ps(C)
                S.mm(ps1[:, 0:128], pkt, pk)
                S.copy(npk, ps1[:, 0:128], eng='act')
            ps2 = next_ps(C)
            S.mm(ps2[:, 0:128], pk, pkt)
            S.copy(npkt, ps2[:, 0:128], eng='act')
            pk, pkt = npk, npkt
            yield
            ps3 = next_ps(C)
            S.mm(ps3[:, 0:128], pkt, B.TT)
            S.tt(B.TT, ps3[:, 0:128], B.TT, ALU.add)
            yield
        S.ts(B.vb, B.vtm, beta[:, h:h + 1], None, ALU.mult)
        S.ts(B.kbg, B.ktm, B.beg, None, ALU.mult)
        psu = next_ps(C)
        S.mm(psu[:, 0:128], B.TT, B.vb)
        S.copy(Hd.usb, psu[:, 0:128], eng='act')
        psw = next_ps(C)
        S.mm(psw[:, 0:128], B.kbg, B.TT)
        S.copy(Hd.wT, psw[:, 0:128], eng='act')
        ps = next_ps(C)
        S.mm(ps[:, 0:128], kn[h], qn[h])
        S.tt(Hd.qkm, ps[:, 0:128], B.DT, ALU.mult)
        S.tt(Hd.qkm, Hd.qkm, C.triu, ALU.mult, eng='pool')
        S.ts(Hd.kdec, B.ktm, B.DT[:, 127:128], None, ALU.mult)
        yield

    def chainB(t, h):
        B = HBs[h]
        Hd = B.hand[t % 2]
        qn, o_a = qns[t % 3], o_as[t % 2]
        ps = next_ps(C)
        S.mm(ps[:, 0:128], Hd.wT, St[h])
        S.tt(B.delta, Hd.usb, ps[:, 0:128], ALU.subtract)
        psq = next_ps(C)
        S.mm(psq[:, 0:128], qn[h], St[h])
        S.act(B.o1, psq[:, 0:128], AF.Copy, scale=Hd.egc)
        yield
        ps = next_ps(C)
        S.mm(ps[:, 0:128], Hd.qkm, B.delta)
        S.tt(o_a[:, h * 128:(h + 1) * 128], ps[:, 0:128], B.o1, ALU.add)
        ps = next_ps(C)
        S.mm(ps[:, 0:128], Hd.kdec, B.delta)
        S.stt(St[h], St[h], Hd.egl, ps[:, 0:128], ALU.mult, ALU.add)
        yield


    def back(t):
        xt = xts[t % 4]
        o_a, zs, ysb = o_as[t % 2], zss[t % 4], ysbs[t % 2]
        for h in range(4):
            S.act(ob[:, h * 128:(h + 1) * 128], o_a[:, h * 128:(h + 1) * 128], AF.Square, accum_out=oss[:, h:h + 1])
        S.act(orstd, oss, AF.Ln, scale=1.0 / 128, bias=C.epsb)
        S.act(orstd, orstd, AF.Exp, scale=-0.5)
        for h in range(4):
            S.stt(o_a[:, h * 128:(h + 1) * 128], o_a[:, h * 128:(h + 1) * 128], orstd[:, h:h + 1], ngb, ALU.mult, ALU.mult)
        S.tt(ob, o_a, zs, ALU.mult)
        yield
        pst = next_ps(C)
        pv = pst[:, :].bitcast(BF16)
        for c in range(4):
            S.transpose(pv[:, c * 128:(c + 1) * 128], ob[:, c * 128:(c + 1) * 128], C.identb)
        S.copy(oT, pv[:, 0:512].rearrange("p (c t) -> p c t", c=4))
        yield
        for nh in range(2):
            ps = next_ps(C)
            for c in range(4):
                S.mm(ps[:, :], oT[:, c, :], wout[:, c, nh * 512:(nh + 1) * 512], start=(c == 0), stop=(c == 3))
            S.tt(ysb[:, nh * 512:(nh + 1) * 512], ps[:, :], xt[:, nh * 512:(nh + 1) * 512], ALU.add)
        outs.append(S.dma(xdst[t * 128:(t + 1) * 128, :], ysb, q='pool'))
        yield

    def run_round_robin(gens):
        live = list(gens)
        while live:
            nxt = []
            for g_ in live:
                try:
                    next(g_)
                    nxt.append(g_)
                except StopIteration:
                    pass
            live = nxt

    run_round_robin([front(0)])
    run_round_robin([chainA(0, h) for h in range(4)] + [front(1)])
    for t in range(NT):
        gens = []
        if t >= 1:
            gens.append(back(t - 1))
        gens += [chainB(t, h) for h in range(4)]
        if t + 1 < NT:
            gens += [chainA(t + 1, h) for h in range(4)]
        if t + 2 < NT:
            gens.append(front(t + 2))
        run_round_robin(gens)
    run_round_robin([back(NT - 1)])
    S.rank_mode = False
    A.release(m0)
    return outs


def sb_phase(C, xorig, xsrc, xdst, g_norm, w_in, g_q, g_k, w_out):
    S, A, nc = C.S, C.A, C.nc
    S.rank_mode = True
    m0 = A.mark()
    if not hasattr(C, 'sbq'):
        C.sbq = nc.dram_tensor("sb_q", [4, 128, T], BF16, kind="Internal").ap()
        C.sbk = nc.dram_tensor("sb_k", [4, 128, T], BF16, kind="Internal").ap()
        C.sbv = nc.dram_tensor("sb_v", [4, 128, NT, 128], BF16, kind="Internal").ap()
    wout = A.alloc([4, D], BF16)
    m1 = A.mark()
    gb = A.alloc([D], F32)
    S.dma(gb, g_norm.partition_broadcast(128))
    gq = A.alloc([1], F32)
    gk = A.alloc([1], F32)
    load_cols(C, g_q, 128, gq)
    load_cols(C, g_k, 128, gk)
    win = A.alloc([8, 1536], BF16)
    stage = [A.alloc([2048], F32) for _ in range(4)]
    load_w_bf16(C, w_in[:, 2056:3592], D, 1536, dst=win, stage=stage)
    load_w_bf16(C, w_out[512:1024, :], 512, D, dst=wout, stage=stage)
    TB = 512
    xts = [A.alloc([D], F32) for _ in range(2)]
    sqj = [A.alloc([D], BF16) for _ in range(2)]
    hb = [A.alloc([D], BF16) for _ in range(2)]
    ss = [A.alloc([1], F32) for _ in range(2)]
    rstd1 = [A.alloc([1], F32) for _ in range(2)]
    hT = A.alloc([8, TB], BF16)
    hTs = [hT, A.alloc([8, TB], BF16)]
    sq = [A.alloc([TB], BF16) for _ in range(8)]
    rstd = [A.alloc([TB], F32) for _ in range(8)]
    qk_o = [A.alloc([TB], BF16) for _ in range(8)]
    qg_b = [A.alloc([TB], F32) for _ in range(8)]
    v_o = [A.alloc([512], BF16) for _ in range(2)]
    sc = 128.0 ** -0.5
    NTT = TB // 128

    def p1_front(tb):
        for tt in range(NTT):
            t = tb * NTT + tt
            xt = xts[t % 2]
            S.dma(xt, xorig[t * 128:(t + 1) * 128, :])
            norm_T(C, xt, gb, hb, hTs[tb % 2], tt * 128, sqj, ss, rstd1)
            yield

    def p1_qk(tb, idx):
        h, isk = idx // 2, idx % 2
        c0, dst, gcol, scl = ((h * 128, C.sbq, gq, sc), (512 + h * 128, C.sbk, gk, 1.0))[isk]
        hT_ = hTs[tb % 2]
        sq_, rs_, o_ = sq[idx], rstd[idx], qk_o[idx]
        ps = next_ps(C)
        for kc in range(8):
            S.mm(ps[:, :], win[:, kc, c0:c0 + 128], hT_[:, kc, :], start=(kc == 0), stop=(kc == 7))
        S.act(sq_, ps[:, :], AF.Square)
        S.ts(qg_b[idx], ps[:, :], gcol, scl, ALU.mult, ALU.mult)
        yield
        ps2 = next_ps(C)
        S.mm(ps2[:, :], C.onesb, sq_)
        S.act(rs_, ps2[:, :], AF.Ln, scale=1.0 / 128, bias=C.epsb)
        yield
        S.act(rs_, rs_, AF.Exp, scale=-0.5)
        S.tt(o_, qg_b[idx], rs_, ALU.mult, eng='pool')
        S.dma(dst[h, :, tb * TB:(tb + 1) * TB], o_, q='sp')
        yield

    def p1_v(tb):
        hT_ = hTs[tb % 2]
        for tt in range(NTT):
            t = tb * NTT + tt
            ps = next_ps(C)
            for kc in range(8):
                S.mm(ps[:, :], hT_[:, kc, tt * 128:(tt + 1) * 128], win[:, kc, 1024:1536], start=(kc == 0), stop=(kc == 7))
            vo = v_o[t % 2]
            S.copy(vo, ps[:, :], eng='act')
            S.dma(C.sbv[:, :, t, :].rearrange("h p d -> p h d"), vo.rearrange("p (h d) -> p h d", h=4), q='sp')
            yield

    def rr1(gens):
        live = list(gens)
        while live:
            nxt = []
            for g_ in live:
                try:
                    next(g_)
                    nxt.append(g_)
                except StopIteration:
                    pass
            live = nxt

    NB1 = T // TB
    rr1([p1_front(0)])
    for tb in range(NB1):
        gens = [p1_qk(tb, idx) for idx in range(8)] + [p1_v(tb)]
        if tb + 1 < NB1:
            gens.append(p1_front(tb + 1))
        rr1(gens)
    A.release(m1)
    o_all = A.alloc([NT, 512], BF16)
    qT = [A.alloc([T], BF16) for _ in range(1)]
    kT = [A.alloc([T], BF16) for _ in range(1)]
    V = [A.alloc([NT, 128], BF16) for _ in range(1)]
    sp = [A.alloc([T], F32) for _ in range(2)]
    w = [A.alloc([T], F32) for _ in range(2)]
    Pc = A.alloc([T], F32)
    att = [A.alloc([T], BF16) for _ in range(2)]
    attT = [A.alloc([512], BF16) for _ in range(3)]
    ntot = [A.alloc([1], F32) for _ in range(2)]
    mask_s = A.alloc([128], F32)
    S.tt(mask_s, C.tril, C.ident, ALU.subtract)
    mask_b = A.alloc([128], BF16)
    S.copy(mask_b, mask_s)
    V2 = [V[0], A.alloc([NT, 128], BF16)]
    its = [(h, i) for h in range(4) for i in range(NT)]
    state = {'ai': 0}

    pss = {}

    def stage_a1(n):
        h, i = its[n]
        b = n % 2
        L = 128 * (i + 1)
        if i == 0:
            S.dma(qT[0], C.sbq[h])
            S.dma(kT[0], C.sbk[h])
            S.dma(V2[h % 2], C.sbv[h])
        sp_, w_ = sp[b], w[b]
        pss[n] = []
        for ci, k0 in enumerate(range(0, L, 512)):
            k1 = min(L, k0 + 512)
            if ci >= 5:
                sub_chunk(n, ci - 5)
            ps = next_ps6(C)
            pss[n].append((ps, k0, k1, False))
            S.mm(ps[:, 0:k1 - k0], qT[0][:, i * 128:(i + 1) * 128], kT[0][:, k0:k1])
            S.act(sp_[:, k0:k1], ps[:, 0:k1 - k0], AF.Exp)
            S.act(sp_[:, k0:k1], sp_[:, k0:k1], AF.Ln, bias=C.ones[:, 0:1])

    def sub_chunk(n, ci):
        b = n % 2
        ps, k0, k1, done = pss[n][ci]
        if done:
            return
        S.tt(w[b][:, k0:k1], ps[:, 0:k1 - k0], sp[b][:, k0:k1], ALU.subtract)
        pss[n][ci] = (ps, k0, k1, True)

    def stage_a2(n):
        h, i = its[n]
        b = n % 2
        L = 128 * (i + 1)
        sp_ = sp[b]
        for ci in range(len(pss[n])):
            sub_chunk(n, ci)
        del pss[n]
        S.tt(sp_[:, L - 128:L], sp_[:, L - 128:L], mask_s, ALU.mult, eng='pool')

    def stage_b1(n):
        h, i = its[n]
        b = n % 2
        L = 128 * (i + 1)
        sp_, w_, nt_ = sp[b], w[b], ntot[b]
        S.scan(Pc[:, 0:L], sp_[:, 0:L], sp_[:, 0:L], 0.0, ALU.add, ALU.max)
        S.ts(nt_, Pc[:, L - 1:L], -1.0, None, ALU.mult)
        S.tt(Pc[:, 0:L], Pc[:, 0:L], w_[:, 0:L], ALU.add, eng=('dve' if n % 3 == 0 else 'pool'))

    def stage_b2(n):
        h, i = its[n]
        b = n % 2
        L = 128 * (i + 1)
        att_, nt_ = att[b], ntot[b]
        S.act(att_[:, 0:L], Pc[:, 0:L], AF.Exp, bias=nt_)
        S.tt(att_[:, L - 128:L], att_[:, L - 128:L], mask_b, ALU.mult, eng='pool')

    def stage_c(n):
        h, i = its[n]
        b = n % 2
        att_ = att[b]
        vh = V2[h % 2]
        pso = C.ps[6 + b]
        nkt = i + 1
        groups = [(g0, min(nkt, g0 + 4)) for g0 in range(0, nkt, 4)]
        ats = {}

        def tr(gi):
            g0, g1 = groups[gi]
            pst = next_ps6(C)
            ptv = pst[:, :].bitcast(BF16)
            for kt in range(g0, g1):
                S.transpose(ptv[:, (kt - g0) * 128:(kt - g0 + 1) * 128], att_[:, kt * 128:(kt + 1) * 128], C.identb)
            at = attT[state['ai'] % 3]
            state['ai'] += 1
            S.copy(at[:, 0:(g1 - g0) * 128], ptv[:, 0:(g1 - g0) * 128], eng=('act' if state['ai'] % 3 == 0 else 'dve'))
            ats[gi] = at

        def av(gi):
            g0, g1 = groups[gi]
            at = ats[gi]
            for kt in range(g0, g1):
                S.mm(pso[:, 0:128], at[:, (kt - g0) * 128:(kt - g0 + 1) * 128], vh[:, kt, :],
                     start=(kt == 0), stop=(kt == nkt - 1))

        tr(0)
        for gi in range(len(groups)):
            if gi + 1 < len(groups):
                tr(gi + 1)
            av(gi)
        S.copy(o_all[:, i, h * 128:(h + 1) * 128], pso[:, 0:128], eng='act')

    NI = len(its)
    for n in range(NI + 2):
        if n < NI:
            stage_a1(n)
        if 0 <= n - 1 < NI:
            stage_b1(n - 1)
        if n < NI:
            stage_a2(n)
        if 0 <= n - 2 < NI:
            stage_c(n - 2)
        if 0 <= n - 1 < NI:
            stage_b2(n - 1)
    oT = [A.alloc([4, 128], BF16) for _ in range(2)]
    xts = [A.alloc([D], F32) for _ in range(2)]
    ysb = [A.alloc([D], F32) for _ in range(2)]
    outs = []
    for i in range(NT):
        xt = xts[i % 2]
        S.dma(xt, xsrc[i * 128:(i + 1) * 128, :])
        pst = next_ps(C)
        pv = pst[:, :].bitcast(BF16)
        for c in range(4):
            S.transpose(pv[:, c * 128:(c + 1) * 128], o_all[:, i, c * 128:(c + 1) * 128], C.identb)
        S.copy(oT[i % 2], pv[:, 0:512].rearrange("p (c t) -> p c t", c=4))
        y = ysb[i % 2]
        for nh in range(2):
            ps = next_ps(C)
            for c in range(4):
                S.mm(ps[:, :], oT[i % 2][:, c, :], wout[:, c, nh * 512:(nh + 1) * 512], start=(c == 0), stop=(c == 3))
            S.tt(y[:, nh * 512:(nh + 1) * 512], ps[:, :], xt[:, nh * 512:(nh + 1) * 512], ALU.add)
        outs.append(S.dma(xdst[i * 128:(i + 1) * 128, :], y, q='pool'))
    S.rank_mode = False
    A.release(m0)
    return outs


from concourse.bass_utils import run_bass_kernel_spmd

_NC_CACHE = {}

_WNAMES = ["norm_mix", "norm_xa", "norm_mem", "norm_ffn", "xa_w_q", "xa_w_kv", "xa_w_o", "xa_g_q", "xa_g_k",
           "ffn_w_up", "ffn_conv_w", "ffn_conv_b", "ffn_w_down", "ab_w_in", "ab_conv_w", "dn_a_log", "dn_dt_bias",
           "dn_norm_g", "sb_g_q", "sb_g_k", "ab_w_out", "gla_w_in", "gla_w_gk", "gla_b_gk", "gla_norm_g", "gla_w_out"]


def build_program(shapes):
    nc = bass.Bass("TRN2", target_bir_lowering=False)
    ap = {}
    for name, shp in shapes.items():
        ap[name] = nc.dram_tensor(name, list(shp), F32, kind="ExternalInput").ap()
    y = nc.dram_tensor("y", [T, D], F32, kind="ExternalOutput").ap()
    C = setup(nc)
    C.S.readonly |= set(shapes.keys())
    x = ap["x"]
    outs = []
    dn_phase(C, x, y, ap["norm_mix"][0:1, :], ap["ab_w_in"][0], ap["ab_conv_w"][0], ap["dn_a_log"][0:1, :],
             ap["dn_dt_bias"][0:1, :], ap["dn_norm_g"][0:1, :], ap["ab_w_out"][0])
    sb_phase(C, x, y, y, ap["norm_mix"][0:1, :], ap["ab_w_in"][0], ap["sb_g_q"][0], ap["sb_g_k"][0], ap["ab_w_out"][0])
    for layer in range(2):
        if layer == 1:
            gla_phase(C, y, y, ap["norm_mix"][1:2, :], ap["gla_w_in"][0], ap["gla_w_gk"][0], ap["gla_b_gk"][0:1, :],
                      ap["gla_norm_g"][0:1, :], ap["gla_w_out"][0])
        xa_phase(C, y, y, ap["mem"], ap["norm_xa"][layer:layer + 1, :], ap["norm_mem"][layer:layer + 1, :],
                 ap["xa_w_q"][layer], ap["xa_w_kv"][layer], ap["xa_w_o"][layer], ap["xa_g_q"][layer], ap["xa_g_k"][layer])
        outs = ffn_phase(C, y, y, ap["norm_ffn"][layer:layer + 1, :], ap["ffn_w_up"][layer], ap["ffn_conv_w"][layer],
                         ap["ffn_conv_b"][layer], ap["ffn_w_down"][layer])
    C.S.finalize(outs)
    return nc


def kernel(**inputs):
    x = np.ascontiguousarray(inputs["x"], dtype=np.float32)
    mem = np.ascontiguousarray(inputs["mem"], dtype=np.float32)
    B = x.shape[0]
    w = {k: np.ascontiguousarray(inputs[k], dtype=np.float32) for k in _WNAMES}
    shapes = {"x": x.shape[1:], "mem": mem.shape[1:]}
    for k in _WNAMES:
        shapes[k] = w[k].shape
    key = tuple(sorted((k, tuple(v)) for k, v in shapes.items()))
    nc = build_program(shapes)
    in_maps = []
    for b in range(B):
        m = {"x": x[b], "mem": mem[b]}
        m.update(w)
        in_maps.append(m)
    res = run_bass_kernel_spmd(nc, in_maps, core_ids=list(range(B)))
    return np.stack([np.asarray(r["y"], dtype=np.float32) for r in res.results], axis=0)
```
